# Optimizing a Trainium2 kernel written in Bass

```python
import jax, jax.numpy as jnp
from jax import lax
import numpy as np

D_MODEL = 2048
BATCH = 4
SEQ = 4096
DEPTH = 1
DEC_BATCH = 16
DEC_SEQ = 32
PAST_LEN = 1024

CHUNK = 64
Q_BLOCK = 128
HGRN_HEADS = 8
HGRN_DK = 128
HGRN_DV = 128
D_HGRN = HGRN_HEADS * HGRN_DK
MLA_HEADS = 8
Q_LORA = 512
KV_LORA = 256
QK_NOPE = 128
QK_ROPE = 64
V_HEAD = 128
D_MLA = MLA_HEADS * V_HEAD
D_MIX = D_HGRN + D_MLA
D_IN = 4 * D_HGRN + Q_LORA + KV_LORA + QK_ROPE
SPLITS = [D_HGRN, 2 * D_HGRN, 3 * D_HGRN, 4 * D_HGRN, 4 * D_HGRN + Q_LORA, 4 * D_HGRN + Q_LORA + KV_LORA]
D_FF = 5632
CONV_W = 3
ROPE_THETA = 10000.0
LN_EPS = 1e-5
RMS_EPS = 1e-6
ALPHA = (2.0 * DEPTH) ** 0.25
BETA = (8.0 * DEPTH) ** -0.25

kernel_name = "hymba_hgrn2_mla_convffn_stream_step"


def layer_norm(x, g, b):
    xf = x.astype(jnp.float32)
    mu = jnp.mean(xf, axis=-1, keepdims=True)
    var = jnp.mean(jnp.square(xf - mu), axis=-1, keepdims=True)
    return ((xf - mu) * lax.rsqrt(var + LN_EPS) * g + b).astype(x.dtype)


def rms_norm(x, g):
    xf = x.astype(jnp.float32)
    return (xf * lax.rsqrt(jnp.mean(xf * xf, axis=-1, keepdims=True) + RMS_EPS) * g).astype(x.dtype)


def rope_tables(pos):
    inv = ROPE_THETA ** (-jnp.arange(0, QK_ROPE, 2, dtype=jnp.float32) / QK_ROPE)
    ang = pos.astype(jnp.float32)[:, None] * inv[None, :]
    return jnp.cos(ang), jnp.sin(ang)


def apply_rope(x, cos, sin):
    xf = x.astype(jnp.float32)
    x1, x2 = jnp.split(xf, 2, axis=-1)
    return jnp.concatenate([x1 * cos - x2 * sin, x1 * sin + x2 * cos], axis=-1).astype(x.dtype)


def hgrn2_recurrence(q, k, log_f, v, s0, blk):
    B, T, H, _ = q.shape
    nb = T // blk

    def to_blocks(a):
        return jnp.moveaxis(a.astype(jnp.float32).reshape(B, nb, blk, H, a.shape[-1]), 1, 0)

    tri = jnp.tril(jnp.ones((blk, blk), dtype=bool))[None, :, :, None, None]

    def step(S, inp):
        qb, kb, gb, vb = inp
        b = jnp.cumsum(gb, axis=1)
        diff = b[:, :, None] - b[:, None, :]
        decay = jnp.exp(jnp.where(tri, diff, -jnp.inf))
        attn = jnp.einsum('bthk,btshk,bshk->bhts', qb, decay, kb)
        o_intra = jnp.einsum('bhts,bshv->bthv', attn, vb)
        o_inter = jnp.einsum('bthk,bhkv->bthv', qb * jnp.exp(b), S)
        b_last = b[:, -1]
        k_dec = kb * jnp.exp(b_last[:, None] - b)
        S_new = jnp.exp(b_last)[..., None] * S + jnp.einsum('bshk,bshv->bhkv', k_dec, vb)
        return S_new, o_intra + o_inter

    S_fin, o = lax.scan(step, s0.astype(jnp.float32),
                        (to_blocks(q), to_blocks(k), to_blocks(log_f), to_blocks(v)))
    o = jnp.moveaxis(o, 0, 1).reshape(B, T, H, v.shape[-1]).astype(v.dtype)
    return o, S_fin


def mla_attention(q_nope, q_rope, k_nope, k_rope, v, q_pos, k_pos):
    B, T, H, _ = q_nope.shape
    nqb = T // Q_BLOCK if T % Q_BLOCK == 0 else 1
    qb = T // nqb
    scale = (QK_NOPE + QK_ROPE) ** -0.5
    k_chunk = k_pos // CHUNK

    def blocks(a):
        return jnp.moveaxis(a.reshape(B, nqb, qb, *a.shape[2:]), 1, 0)

    def attend(args):
        qn, qr, qp = args
        s = (jnp.einsum('bqhd,bkhd->bhqk', qn, k_nope)
             + jnp.einsum('bqhd,bkd->bhqk', qr, k_rope)).astype(jnp.float32) * scale
        mask = k_chunk[None, :] <= (qp // CHUNK)[:, None]
        p = jax.nn.softmax(jnp.where(mask, s, -jnp.inf), axis=-1).astype(v.dtype)
        return jnp.einsum('bhqk,bkhd->bqhd', p, v)

    o = lax.map(attend, (blocks(q_nope), blocks(q_rope), q_pos.reshape(nqb, qb)))
    return jnp.moveaxis(o, 0, 1).reshape(B, T, H, V_HEAD)


def token_mixer(x, s0, lat_past, kr_past, blk, lb, w_in, hgrn_norm_g, q_a_g, w_q_b, kv_a_g, w_kv_b, w_out):
    B, T, _ = x.shape
    past_len = lat_past.shape[1]
    proj = x @ w_in
    hq, hf, hi, hg, cq, ckv, kr = jnp.split(proj, SPLITS, axis=-1)

    f = lb + (1.0 - lb) * jax.nn.sigmoid(hf.astype(jnp.float32))
    heads = lambda a: a.reshape(B, T, HGRN_HEADS, -1)
    o_h, s_new = hgrn2_recurrence(heads(hq), heads(1.0 - f), heads(jnp.log(f)), heads(hi), s0, blk)
    o_h = rms_norm(o_h, hgrn_norm_g.reshape(HGRN_HEADS, HGRN_DV)).reshape(B, T, D_HGRN) * jax.nn.silu(hg)

    q_pos = past_len + jnp.arange(T)
    cos, sin = rope_tables(q_pos)
    q = (rms_norm(cq, q_a_g) @ w_q_b).reshape(B, T, MLA_HEADS, QK_NOPE + QK_ROPE)
    q_nope = q[..., :QK_NOPE]
    q_rope = apply_rope(q[..., QK_NOPE:], cos[:, None, :], sin[:, None, :])
    lat_new = rms_norm(ckv, kv_a_g)
    kr_new = apply_rope(kr, cos, sin)
    lat = jnp.concatenate([lat_past.astype(lat_new.dtype), lat_new], axis=1)
    krs = jnp.concatenate([kr_past.astype(kr_new.dtype), kr_new], axis=1)
    kv = (lat @ w_kv_b).reshape(B, past_len + T, MLA_HEADS, QK_NOPE + V_HEAD)
    k_nope, v = kv[..., :QK_NOPE], kv[..., QK_NOPE:]
    o_a = mla_attention(q_nope, q_rope, k_nope, krs, v, q_pos, jnp.arange(past_len + T))

    out = jnp.concatenate([o_h, o_a.reshape(B, T, D_MLA)], axis=-1) @ w_out
    return out, s_new, lat_new, kr_new


def conv_ffn(x, conv_buf, w_up, w_gate, conv_w, conv_b, w_down):
    T = x.shape[1]
    u = x @ w_up
    ext = jnp.concatenate([conv_buf.astype(u.dtype), u], axis=1)
    a = conv_b + sum(ext[:, j:j + T] * conv_w[j] for j in range(CONV_W))
    h = jax.nn.silu(a) * (x @ w_gate)
    return h @ w_down, ext[:, -(CONV_W - 1):]


def encoder(x, hgrn_state, lat_cache, kr_cache, conv_cache, blk, lb_param, ln_in_g, ln_in_b,
            w_in, hgrn_norm_g, q_a_g, w_q_b, kv_a_g, w_kv_b, w_out, ln1_g, ln1_b,
            w_ffn_up, w_ffn_gate, conv_w, conv_b, w_ffn_down, ln2_g, ln2_b):
    lbs = jnp.cumsum(jax.nn.softmax(lb_param.astype(jnp.float32), axis=0), axis=0)
    h = layer_norm(x, ln_in_g, ln_in_b)
    s_out, lat_out, kr_out, conv_out = [], [], [], []
    for l in range(DEPTH):
        mix, s_new, lat_new, kr_new = token_mixer(h, hgrn_state[l], lat_cache[l], kr_cache[l], blk, lbs[l],
                                                  w_in[l], hgrn_norm_g[l], q_a_g[l], w_q_b[l],
                                                  kv_a_g[l], w_kv_b[l], w_out[l])
        h = layer_norm(ALPHA * h + mix, ln1_g[l], ln1_b[l])
        ff, conv_new = conv_ffn(h, conv_cache[l], w_ffn_up[l], w_ffn_gate[l], conv_w[l], conv_b[l], w_ffn_down[l])
        h = layer_norm(ALPHA * h + ff, ln2_g[l], ln2_b[l])
        s_out.append(s_new); lat_out.append(lat_new); kr_out.append(kr_new); conv_out.append(conv_new)
    return h, jnp.stack(s_out), jnp.stack(lat_out), jnp.stack(kr_out), jnp.stack(conv_out)


def setup_inputs(seed: int = 0) -> dict:
    key = jax.random.key(seed)
    ks = jax.random.split(key, 32)

    def nrm(k, shape, scale):
        return jax.random.normal(k, shape, dtype=jnp.float32) * scale

    return {
        "x_prompt": nrm(ks[0], (BATCH, SEQ, D_MODEL), 1.0),
        "x_sample": nrm(ks[1], (DEC_BATCH, DEC_SEQ, D_MODEL), 1.0),
        "cache_kv_latent": nrm(ks[2], (DEPTH, DEC_BATCH, PAST_LEN, KV_LORA), 1.0),
        "cache_k_rope": nrm(ks[3], (DEPTH, DEC_BATCH, PAST_LEN, QK_ROPE), 1.0),
        "state_hgrn": nrm(ks[4], (DEPTH, DEC_BATCH, HGRN_HEADS, HGRN_DK, HGRN_DV), 0.5),
        "cache_ffn_conv": nrm(ks[5], (DEPTH, DEC_BATCH, CONV_W - 1, D_FF), 1.0),
        "lb_param": nrm(ks[6], (DEPTH + 1, D_HGRN), 1.0),
        "ln_in_g": 1.0 + nrm(ks[7], (D_MODEL,), 0.02),
        "ln_in_b": nrm(ks[8], (D_MODEL,), 0.02),
        "w_in": nrm(ks[9], (DEPTH, D_MODEL, D_IN), D_MODEL ** -0.5),
        "hgrn_norm_g": 1.0 + nrm(ks[10], (DEPTH, D_HGRN), 0.02),
        "q_a_g": 1.0 + nrm(ks[11], (DEPTH, Q_LORA), 0.02),
        "w_q_b": nrm(ks[12], (DEPTH, Q_LORA, MLA_HEADS * (QK_NOPE + QK_ROPE)), Q_LORA ** -0.5),
        "kv_a_g": 1.0 + nrm(ks[13], (DEPTH, KV_LORA), 0.02),
        "w_kv_b": nrm(ks[14], (DEPTH, KV_LORA, MLA_HEADS * (QK_NOPE + V_HEAD)), KV_LORA ** -0.5),
        "w_out": nrm(ks[15], (DEPTH, D_MIX, D_MODEL), BETA * D_MIX ** -0.5),
        "ln1_g": 1.0 + nrm(ks[16], (DEPTH, D_MODEL), 0.02),
        "ln1_b": nrm(ks[17], (DEPTH, D_MODEL), 0.02),
        "w_ffn_up": nrm(ks[18], (DEPTH, D_MODEL, D_FF), D_MODEL ** -0.5),
        "w_ffn_gate": nrm(ks[19], (DEPTH, D_MODEL, D_FF), D_MODEL ** -0.5),
        "conv_w": nrm(ks[20], (DEPTH, CONV_W, D_FF), CONV_W ** -0.5),
        "conv_b": nrm(ks[21], (DEPTH, D_FF), 0.02),
        "w_ffn_down": nrm(ks[22], (DEPTH, D_FF, D_MODEL), BETA * D_FF ** -0.5),
        "ln2_g": 1.0 + nrm(ks[23], (DEPTH, D_MODEL), 0.02),
        "ln2_b": nrm(ks[24], (DEPTH, D_MODEL), 0.02),
    }


def reference(x_prompt, x_sample, cache_kv_latent, cache_k_rope, state_hgrn, cache_ffn_conv,
              lb_param, ln_in_g, ln_in_b, w_in, hgrn_norm_g, q_a_g, w_q_b, kv_a_g, w_kv_b, w_out,
              ln1_g, ln1_b, w_ffn_up, w_ffn_gate, conv_w, conv_b, w_ffn_down, ln2_g, ln2_b):
    B, T = x_prompt.shape[0], x_prompt.shape[1]
    p_s0 = jnp.zeros((DEPTH, B, HGRN_HEADS, HGRN_DK, HGRN_DV), jnp.float32)
    p_lat0 = jnp.zeros((DEPTH, B, 0, KV_LORA), x_prompt.dtype)
    p_kr0 = jnp.zeros((DEPTH, B, 0, QK_ROPE), x_prompt.dtype)
    p_conv0 = jnp.zeros((DEPTH, B, CONV_W - 1, D_FF), x_prompt.dtype)
    y_prompt, p_state_hgrn, p_kv_latent, p_k_rope, p_ffn_conv = encoder(
        x_prompt, p_s0, p_lat0, p_kr0, p_conv0, CHUNK, lb_param, ln_in_g, ln_in_b,
        w_in, hgrn_norm_g, q_a_g, w_q_b, kv_a_g, w_kv_b, w_out, ln1_g, ln1_b,
        w_ffn_up, w_ffn_gate, conv_w, conv_b, w_ffn_down, ln2_g, ln2_b)
    y_sample, s_state_hgrn, s_kv_latent, s_k_rope, s_ffn_conv = encoder(
        x_sample, state_hgrn, cache_kv_latent, cache_k_rope, cache_ffn_conv, x_sample.shape[1],
        lb_param, ln_in_g, ln_in_b,
        w_in, hgrn_norm_g, q_a_g, w_q_b, kv_a_g, w_kv_b, w_out, ln1_g, ln1_b,
        w_ffn_up, w_ffn_gate, conv_w, conv_b, w_ffn_down, ln2_g, ln2_b)
    return (y_prompt, y_sample, p_kv_latent, p_k_rope, p_state_hgrn, p_ffn_conv,
            s_kv_latent, s_k_rope, s_state_hgrn, s_ffn_conv)
```

```python
import contextlib
import numpy as np
import concourse.bass as bass
import concourse.mybir as mybir
from concourse.bass_utils import run_bass_kernel_spmd

F32 = mybir.dt.float32
BF16 = mybir.dt.bfloat16
AF = mybir.ActivationFunctionType
ALU = mybir.AluOpType
EPOCH = 30000

D = 2048
NH = 8
DFF = 5632
NFC = 44
ALPHA = 2.0 ** 0.25
LN_EPS = 1e-5
RMS_EPS = 1e-6
SCALE = 192.0 ** -0.5
NT = 2
TB = 128 * NT
NPRE = 16
NMAIN = 16
RING = 4
STAGE = 99


class View:
    __slots__ = ("T", "ap", "parts")

    def __init__(self, T, ap, parts):
        self.T, self.ap, self.parts = T, ap, parts


class TT:
    def __init__(self, name, t, nparts=1):
        self.name, self.t, self.nparts = name, t, nparts
        self.w = [None] * nparts
        self.r = [dict() for _ in range(nparts)]
        self.dsem = None
        self.dval = 0

    def v(self, key=None, parts=None, f=None):
        ap = self.t[key] if key is not None else self.t[:]
        if f is not None:
            ap = f(ap)
        if parts is None:
            parts = range(self.nparts)
        elif isinstance(parts, int):
            parts = (parts,)
        return View(self, ap, parts)


class Sub:
    def __init__(self, parent, ap):
        self.parent, self.t = parent, ap

    def v(self, key=None, parts=None, f=None):
        ap = self.t[key] if key is not None else self.t
        return View(self.parent, ap, range(self.parent.nparts))


class Eng:
    def __init__(self, name):
        self.name = name
        self.q = []
        self.sems = []
        self.cnt = 0
        self.waited = {}


class K:
    def __init__(self, nc, es):
        self.nc, self.es = nc, es
        self.E = {n: Eng(n) for n in ("pe", "act", "dve", "pool", "sp")}
        self.nsem = 0
        self.final_events = []
        self.bank_rr = 0
        self.pair_rr = 0
        self.ninstr = 0

    def new_sem(self, name):
        self.nsem += 1
        return self.es.enter_context(self.nc.semaphore(f"{name}_{self.nsem}"))

    def sb(self, name, shape, dtype, nparts=1):
        t = self.es.enter_context(self.nc.sbuf_tensor("sb_" + name, list(shape), dtype))
        return TT(name, t, nparts)

    def _cur_sem(self, e):
        if not e.sems or e.cnt >= EPOCH:
            e.sems.append(self.new_sem("e" + e.name))
            e.cnt = 0
        return e.sems[-1]

    def _deps(self, reads, writes, eng):
        deps = []
        for v in reads:
            T = v.T
            for p in v.parts:
                w = T.w[p]
                if w is not None:
                    deps.append(w)
        for v in writes:
            T = v.T
            for p in v.parts:
                w = T.w[p]
                if w is not None and not (eng == "pe" and w[2] == "pe"):
                    deps.append(w)
                deps.extend(T.r[p].values())
        return deps

    def _emit_waits(self, e, deps):
        need = {}
        for (sem, val, _n) in deps:
            kk = id(sem)
            if e.waited.get(kk, 0) >= val:
                continue
            if kk not in need or need[kk][1] < val:
                need[kk] = (sem, val)
        for kk, (sem, val) in need.items():
            e.waited[kk] = val
            e.q.append(lambda h, sem=sem, val=val: h.wait_ge(sem, val))

    def _record(self, reads, writes, ev):
        for v in reads:
            for p in v.parts:
                v.T.r[p][id(ev[0])] = ev
        for v in writes:
            for p in v.parts:
                v.T.w[p] = ev
                v.T.r[p] = {}

    def op(self, eng, fn, reads, writes):
        e = self.E[eng]
        self._emit_waits(e, self._deps(reads, writes, eng))
        sem = self._cur_sem(e)
        e.cnt += 1
        e.q.append(lambda h, sem=sem: fn(h).then_inc(sem, 1))
        ev = (sem, e.cnt, eng)
        self._record(reads, writes, ev)
        self.ninstr += 1
        return ev

    def dma(self, eng, pairs, reads, writes, final=False, owner=None):
        e = self.E[eng]
        self._emit_waits(e, self._deps(reads, writes, eng))
        tv = owner if owner is not None else (writes + reads)[0].T
        if tv.dsem is None:
            tv.dsem = self.new_sem("d" + tv.name)
        sem = tv.dsem
        for (o, i) in pairs:
            e.q.append(lambda h, o=o, i=i, sem=sem: h.dma_start(out=o, in_=i).then_inc(sem, 16))
        tv.dval += 16 * len(pairs)
        ev = (sem, tv.dval, "dma")
        self._record(reads, writes, ev)
        if final:
            self.final_events.append(ev)
        self.ninstr += len(pairs)
        return ev

    def init_psum(self):
        t = self.es.enter_context(self.nc.psum_tensor("psum_all", [128, 4096], F32))
        self.P = TT("psum_all", t, 8)

    def bank(self):
        i = self.bank_rr
        self.bank_rr = (self.bank_rr + 1) % 8
        return i

    def pair(self):
        j = self.pair_rr
        self.pair_rr = (self.pair_rr + 1) % 4
        self.bank_rr = (2 * j + 2) % 8
        return 2 * j

    def pv(self, bank, n=512, off=0, bf=False, nb=1, np_=128):
        parts = tuple(range(bank, bank + nb))
        if not bf:
            ap = self.P.t[0:np_, bank * 512 + off: bank * 512 + off + n]
        else:
            ap = self.P.t[0:np_, bank * 512: (bank + nb) * 512].bitcast(BF16)[:, off: off + n]
        return View(self.P, ap, parts)

    def finish(self):
        e = self.E["sp"]
        self._emit_waits(e, self.final_events)
        block = self.es.enter_context(self.nc.Block())
        E = self.E

        def run(q):
            def f(h):
                for c in q:
                    c(h)
            return f
        block.tensor(run(E["pe"].q))
        block.scalar(run(E["act"].q))
        block.vector(run(E["dve"].q))
        block.gpsimd(run(E["pool"].q))
        block.sync(run(E["sp"].q))

    def mm(self, out, lhsT, rhs, start, stop):
        self.op("pe", lambda h: h.matmul(out.ap, lhsT=lhsT.ap, rhs=rhs.ap, start=start, stop=stop),
                [lhsT, rhs], [out])

    def tr(self, out, in_, ident):
        self.op("pe", lambda h: h.transpose(out=out.ap, in_=in_.ap, identity=ident.ap), [in_, ident], [out])

    def act(self, out, in_, func, scale=None, bias=None, accum=None, eng="act"):
        reads = [in_]
        kw = {}
        if scale is not None:
            if isinstance(scale, View):
                reads.append(scale)
                kw["scale"] = scale.ap
            else:
                kw["scale"] = float(scale)
        if bias is not None:
            if isinstance(bias, View):
                reads.append(bias)
                kw["bias"] = bias.ap
            else:
                kw["bias"] = float(bias)
        writes = [out]
        if accum is not None:
            writes.append(accum)
            kw["accum_out"] = accum.ap
        self.op("act", lambda h: h.activation(out=out.ap, in_=in_.ap, func=func, **kw), reads, writes)

    def tt(self, eng, out, in0, in1, op):
        self.op(eng, lambda h: h.tensor_tensor(out=out.ap, in0=in0.ap, in1=in1.ap, op=op), [in0, in1], [out])

    def ts(self, eng, out, in0, s1, op0, s2=None, op1=None):
        reads = [in0]
        a1 = s1.ap if isinstance(s1, View) else float(s1)
        if isinstance(s1, View):
            reads.append(s1)
        a2 = None
        if s2 is not None:
            a2 = s2.ap if isinstance(s2, View) else float(s2)
            if isinstance(s2, View):
                reads.append(s2)
        if op1 is None:
            self.op(eng, lambda h: h.tensor_scalar(out=out.ap, in0=in0.ap, scalar1=a1, scalar2=None, op0=op0),
                    reads, [out])
        else:
            self.op(eng, lambda h: h.tensor_scalar(out=out.ap, in0=in0.ap, scalar1=a1, scalar2=a2, op0=op0, op1=op1),
                    reads, [out])

    def stt(self, out, in0, scalar, in1, op0, op1):
        reads = [in0, in1]
        a = scalar.ap if isinstance(scalar, View) else float(scalar)
        if isinstance(scalar, View):
            reads.append(scalar)
        self.op("dve", lambda h: h.scalar_tensor_tensor(out=out.ap, in0=in0.ap, scalar=a, in1=in1.ap, op0=op0, op1=op1),
                reads, [out])

    def copy(self, eng, out, in_):
        if eng == "act":
            self.act(out, in_, AF.Copy)
        else:
            self.op(eng, lambda h: h.tensor_copy(out=out.ap, in_=in_.ap), [in_], [out])

    def memset(self, eng, out, val):
        self.op(eng, lambda h: h.memset(out.ap, val), [], [out])


def build_program(nc, es):
    k = K(nc, es)
    k.init_psum()
    S_ = slice(None)

    def din(name, shape):
        return nc.dram_tensor(name, list(shape), F32, kind="ExternalInput").ap()

    def dout(name, shape):
        return nc.dram_tensor(name, list(shape), F32, kind="ExternalOutput").ap()

    xp = din("xp", [NPRE, 128, D]); xm = din("xm", [NMAIN, 128, D]); xs = din("xs", [128, D])
    w_hq = din("w_hq", [D, 1024]); w_hf = din("w_hf", [D, 1024]); w_hi = din("w_hi", [D, 1024]); w_hg = din("w_hg", [D, 1024])
    w_cq = din("w_cq", [D, 512]); w_ckr = din("w_ckr", [D, 384]); w_qb = din("w_qb", [512, 2048]); w_kvb = din("w_kvb", [256, 2048])
    w_out = din("w_out", [D, D]); w_up = din("w_up", [D, DFF]); w_gate = din("w_gate", [D, DFF]); w_down = din("w_down", [DFF, D])
    lbp = din("lbp", [2, 1024])
    gin_r = din("gin_r", [1, D]); bin_r = din("bin_r", [1, D]); g1_r = din("g1_r", [1, D]); b1_r = din("b1_r", [1, D])
    g2_r = din("g2_r", [1, D]); b2_r = din("b2_r", [1, D])
    kvg_r = din("kvg_r", [1, 256]); qag_c = din("qag_c", [128, 4]); hng_c = din("hng_c", [128, 8])
    cw_c = din("cw_c", [128, NFC * 3]); cb_c = din("cb_c", [128, NFC])
    c_lat = din("c_lat", [2, 1024, 256]); c_kr = din("c_kr", [2, 1024, 64]); c_st = din("c_st", [2, 8, 128, 128]); c_conv = din("c_conv", [2, 2, DFF])
    idn_d = din("idn", [128, 128]); tri64_d = din("tri64", [128, 128]); tri32_d = din("tri32", [128, 128]); same32_d = din("same32", [128, 128])
    rt_tok = din("rt_tok", [NPRE + NMAIN + 1, 128, 128])
    rt_feat = din("rt_feat", [2, 128, 128 * (NMAIN + 2)])
    flg = din("flg", [128, 2])

    y_main = dout("y_main", [NMAIN, 128, D]); y_s = dout("y_s", [128, D])
    lat_main = dout("lat_main", [NMAIN, 128, 256]); kr_main = dout("kr_main", [NMAIN, 128, 64])
    lat_s = dout("lat_s", [128, 256]); kr_s = dout("kr_s", [128, 64])
    st_fin = dout("st_fin", [8, 128, 128]); st_s = dout("st_s", [2, 8, 128, 128])
    conv_fin = dout("conv_fin", [2, DFF]); conv_s = dout("conv_s", [2, 2, DFF])

    ident = k.sb("ident", [128, 128], BF16); identf = k.sb("identf", [128, 128], F32)
    ones = k.sb("ones", [128, 128], BF16)
    tri64 = k.sb("tri64", [128, 128], F32); tri32 = k.sb("tri32", [128, 128], F32); same32 = k.sb("same32", [128, 128], F32)
    oml = k.sb("oml", [128, 1024], F32)
    kvg = k.sb("kvg", [128, 256], F32); qag = k.sb("qag", [128, 4], F32); hng = k.sb("hng", [128, 8], F32)
    cw = k.sb("cw", [128, NFC * 3], F32); cb = k.sb("cb", [128, NFC], F32)
    flags = k.sb("flags", [128, 2], F32)
    epsln = k.sb("epsln", [128, 1], F32); epsrms = k.sb("epsrms", [128, 1], F32)
    wv = k.sb("wv", [128, 2, 8, 128], BF16)
    wknT = k.sb("wknT", [128, 8, 256], BF16)
    S32 = k.sb("S32", [128, 1024], F32)
    Sbf = [k.sb(f"Sbf{i}", [128, 1024], BF16) for i in range(3)]
    Sbnd = k.sb("Sbnd", [128, 1024], F32)
    hist = k.sb("hist", [128, 2, NFC], F32, nparts=NFC)
    hcache = k.sb("hcache", [128, 2, 2, NFC], F32)
    hsout = k.sb("hsout", [128, 2, 2, NFC], F32, nparts=2)
    zero2 = k.sb("zero2", [128, 2], F32)
    NSL = NPRE + NMAIN + 1
    latT = k.sb("latT", [128, 2, 128 * NSL], BF16, nparts=NSL)
    latK = k.sb("latK", [128, NSL, 256], BF16, nparts=NSL)
    krT = k.sb("krT", [128, 128 * NSL], BF16, nparts=NSL)
    hres = [k.sb(f"hres{i}", [128, D], F32, nparts=4) for i in range(NT)]
    hT = k.sb("hT", [128, 16, TB], BF16, nparts=16 * NT)
    ocT = hT
    ring = [k.sb(f"ring{i}", [128, 4096], BF16) for i in range(RING)]
    gq = [k.sb(f"gq{i}", [128, 512], F32) for i in range(2)]; bq = [k.sb(f"bq{i}", [128, 512], F32) for i in range(2)]
    stt_ = k.sb("bnst", [128, 4, 6], F32); mv = k.sb("mv", [128, 2], F32); rstd = k.sb("rstd", [128, 1], F32); vtmp = k.sb("vtmp", [128, 1], F32)
    s32 = k.sb("s32", [128, 1024], F32); lf32 = k.sb("lf32", [128, 1024], F32); enb = k.sb("enb", [128, 1024], F32)
    k2 = [k.sb(f"k2_{i}", [128, 1024], BF16) for i in range(NT)]
    vb = [k.sb(f"vb_{i}", [128, 1024], BF16) for i in range(NT)]
    k2T = k.sb("k2T", [128, 8, TB], BF16, nparts=NT)
    ebT = k.sb("ebT", [128, 8, TB], F32, nparts=NT)
    q1T = k.sb("q1T", [128, 8, TB], BF16, nparts=8)
    xb = Sub(q1T, q1T.t[:].rearrange("p h t -> p (h t)"))
    sgT = k.sb("sgT", [128, 8, TB], BF16, nparts=8)
    ATm = k.sb("ATm", [128, 8, 128], BF16)
    stmp = s32; otmp = lf32; otmp2 = enb
    sqb = k.sb("sqb", [128, 1024], BF16)
    cq32 = Sub(s32, s32.t[:].rearrange("p (c t) -> p c t", c=4)); cqsq = k.sb("cqsq", [128, 4, TB], BF16, nparts=4)
    cqnT = k.sb("cqnT", [128, 4, TB], BF16, nparts=4)
    rtmp = k.sb("rtmp", [128, TB], F32); rtmp2 = k.sb("rtmp2", [128, TB], F32)
    latf = k.sb("latf", [128, 256], F32); latb = k.sb("latb", [128, 256], BF16)
    krf = k.sb("krf", [128, 64], F32); krt = k.sb("krt", [128, 64], F32); krb = k.sb("krb", [128, 128], BF16)
    ss = k.sb("ss", [128, 1], F32); junk = Sub(rtmp2, rtmp2.t[:, 0:256])
    rtab = k.sb("rtab", [128, 128], F32)
    cosT = k.sb("cosT", [128, TB], F32); sinT = k.sb("sinT", [128, TB], F32)
    qnT = Sub(k2T, k2T.t[:])
    qr8 = Sub(sgT, sgT.t[:])
    qpT = [Sub(vb[i], vb[i].t[:].rearrange("p (c h t) -> p c h t", c=2, h=2)) for i in range(2)]
    PT = [k.sb(f"PT{i}", [128, 2, TB], BF16) for i in range(3)]
    hout = k.sb("hout", [128, 128], F32)
    OTs = k.sb("OTs", [128, 2, 2, TB], BF16)
    rsum = k.sb("rsum", [128, 2, TB], F32)
    aa = [Sub(enb, enb.t[:, 0:TB]), Sub(ebT, ebT.t[:, 0, :])]
    hTg = [Sub(k2[i], k2[i].t[:].rearrange("p (c t) -> p c t", c=4)) for i in range(2)]
    stout = lf32

    ld = lambda T, src: k.dma("sp", [(T.t[:], src)], [], [T.v()])
    ld(identf, idn_d[:, :]); ld(tri64, tri64_d[:, :]); ld(tri32, tri32_d[:, :]); ld(same32, same32_d[:, :])
    ld(qag, qag_c[:, :]); ld(hng, hng_c[:, :]); ld(cw, cw_c[:, :]); ld(cb, cb_c[:, :]); ld(flags, flg[:, :])
    ld(kvg, kvg_r[0:1, :].partition_broadcast(128))
    k.dma("sp", [(s32.t[:], lbp[0:1, :].partition_broadcast(128))], [], [s32.v()])
    k.dma("sp", [(lf32.t[:], lbp[1:2, :].partition_broadcast(128))], [], [lf32.v()])
    k.dma("pool", [(wv.t[:, c, :, :], w_kvb[c * 128:(c + 1) * 128, :].rearrange("p (h x) -> p h x", x=256)[:, :, 128:256]) for c in range(2)],
          [], [wv.v()])
    wfull = Sub(ring[0], ring[0].t[:, 0:4096].rearrange("p (c n) -> p c n", c=2))
    k.dma("pool", [(wfull.t, w_kvb.rearrange("(c p) n -> p c n", p=128))], [], [ring[0].v()])
    k.copy("dve", ident.v(), identf.v())
    k.memset("dve", ones.v(), 1.0)
    k.memset("dve", epsln.v(), LN_EPS)
    k.memset("dve", epsrms.v(), RMS_EPS)
    k.memset("dve", S32.v(), 0.0)
    k.memset("dve", Sbf[0].v(), 0.0)
    k.memset("dve", hist.v(), 0.0)
    k.memset("dve", zero2.v(), 0.0)
    k.memset("dve", hT.v(), 0.0)
    k.tt("dve", enb.v(), lf32.v(), s32.v(), ALU.subtract)
    k.act(oml.v(), enb.v(), AF.Sigmoid)
    pf = flags.v((S_, slice(0, 1)))
    pbias = flags.v((S_, slice(1, 2)))
    for h in range(8):
        b = k.bank()
        for c in range(2):
            o = k.pv(b, 128, c * 128, bf=True)
            k.tr(o, wfull.v((S_, c, slice(h * 256, h * 256 + 128))), ident.v())
        k.copy("dve", wknT.v((S_, h, S_)), k.pv(b, 256, 0, bf=True))

    rstate = {"i": 0}

    scr = {}
    wq = {"q": "pool"}

    def wload(key, src2d, kc, n):
        T = ring[rstate["i"] % RING]
        rstate["i"] += 1
        flat = T.t[:, 0:kc * n]
        dst = flat.rearrange("p (c n) -> p c n", c=kc)
        if key not in scr:
            d = nc.dram_tensor(f"scr_{len(scr)}", [128, kc * n], BF16).ap()
            Sx = TT(f"scr{len(scr)}", d, 4)
            scr[key] = Sx
            srcv = src2d.rearrange("(c p) n -> p c n", p=128)
            if kc >= 2:
                h2 = kc // 2
                pairs = [(dst[:, 0:h2, :], srcv[:, 0:h2, :]), (dst[:, h2:kc, :], srcv[:, h2:kc, :])]
            else:
                pairs = [(dst, srcv)]
            k.dma("pool", pairs, [], [T.v()])
            k.dma("sp", [(d[:, :], flat)], [T.v()], [Sx.v()], owner=T)
        else:
            Sx = scr[key]
            k.dma(wq["q"], [(flat, Sx.t[:, :])], [Sx.v()], [T.v()], owner=T)
        return T, dst

    def wview(Tw, ap):
        return View(Tw, ap, (0,))

    def stage_bufs():
        def flat_bf(T_, pat):
            return T_.t[:].rearrange(pat)
        L_ = [Sub(sgT, sgT.t[:].rearrange("p h t -> p (h t)")[:, 0:1024]), Sub(k2T, k2T.t[:].rearrange("p h t -> p (h t)")[:, 0:1024]),
              Sub(OTs, OTs.t[:].rearrange("p a b t -> p (a b t)")), Sub(cqsq, cqsq.t[:].rearrange("p c t -> p (c t)")),
              Sub(cqnT, cqnT.t[:].rearrange("p c t -> p (c t)")), Sub(ATm, ATm.t[:].rearrange("p h t -> p (h t)")),
              Sub(rsum, rsum.t[:].rearrange("p a t -> p (a t)").bitcast(BF16))]
        return L_
    pre_list = []
    for q in range(4):
        pre_list.append((("hq", q), w_hq[:, q * 256:(q + 1) * 256], 16, 256))
    for q in range(4):
        pre_list.append((("hg", q), w_hg[:, q * 256:(q + 1) * 256], 16, 256))
    for q in range(2):
        pre_list.append((("cq", q), w_cq[:, q * 256:(q + 1) * 256], 16, 256))
    pre_list.append((("qb", 0), w_qb[:, 0:1024], 4, 1024))
    pre_list.append((("qb", 1), w_qb[:, 1024:2048], 4, 1024))
    for n8 in range(8):
        pre_list.append((("out", n8), w_out[:, n8 * 256:(n8 + 1) * 256], 16, 256))
    for g in range(NFC // 4):
        for half in range(2):
            c0f = g * 512 + half * 256
            pre_list.append((("up", g, half), w_up[:, c0f:c0f + 256], 16, 256))
            pre_list.append((("gate", g, half), w_gate[:, c0f:c0f + 256], 16, 256))
        if g > 0:
            for dh in range(2):
                pre_list.append((("down", g - 1, dh), w_down[(g - 1) * 512:g * 512, dh * 1024:(dh + 1) * 1024], 4, 1024))
    for dh in range(2):
        gl = NFC // 4 - 1
        pre_list.append((("down", gl, dh), w_down[gl * 512:(gl + 1) * 512, dh * 1024:(dh + 1) * 1024], 4, 1024))
    pre_q = [(key, src2d, kc, n, q) for (key, src2d, kc, n) in pre_list for q in range(4)]
    pstate = {"i": 0, "pending": []}

    def precast(count, stg):
        DEPTH = len(stg) - 2
        for _ in range(count):
            if pstate["i"] < len(pre_q):
                key, src2d, kc, n, q = pre_q[pstate["i"]]
                if key not in scr:
                    d = nc.dram_tensor(f"scr_{len(scr)}", [128, kc * n], BF16).ap()
                    scr[key] = TT(f"scr{len(scr)}", d, 4)
                Sx = scr[key]
                sg = stg[pstate["i"] % len(stg)]
                pstate["i"] += 1
                kq = kc // 4
                srcv = src2d.rearrange("(c p) n -> p c n", p=128)[:, q * kq:(q + 1) * kq, :]
                dst = sg.t.rearrange("p (c n) -> p c n", c=kq)
                k.dma("pool", [(dst, srcv)], [], [sg.v()], owner=sg.parent)
                pstate["pending"].append((sg, Sx, q))
            while len(pstate["pending"]) > (DEPTH if pstate["i"] < len(pre_q) else 0):
                sg, Sx, q = pstate["pending"].pop(0)
                k.dma("pool", [(Sx.t[:, q * 1024:(q + 1) * 1024], sg.t)], [sg.v()], [Sx.v(parts=q)], owner=sg.parent)

    gbsrc = {}

    def load_gb(g_r, b_r):
        gbsrc["g"], gbsrc["b"] = g_r, b_r

    mvs = [k.sb(f"mv{i}", [128, 2], F32) for i in range(NT)]
    rstds = [k.sb(f"rstd{i}", [128, 1], F32) for i in range(NT)]
    gres = {}

    def ln_block(idxs, mode, ydsts=None, resident=False):
        g_r, b_r = gbsrc["g"], gbsrc["b"]
        for i in idxs:
            H = hres[i]
            for j in range(4):
                k.op("dve", lambda h, j=j, H=H: h.bn_stats(out=stt_.t[:, j, :], in_=H.t[:, j * 512:(j + 1) * 512]),
                     [H.v(parts=j)], [stt_.v()])
            k.op("dve", lambda h, i=i: h.bn_aggr(out=mvs[i].t[:], in_=stt_.t[:].rearrange("p a b -> p (a b)")), [stt_.v()], [mvs[i].v()])
            k.act(vtmp.v(), mvs[i].v((S_, slice(1, 2))), AF.Ln, bias=epsln.v())
            k.act(rstds[i].v(), vtmp.v(), AF.Exp, scale=-0.5)
        for j in range(4):
            sl = (S_, slice(j * 512, (j + 1) * 512))
            if resident:
                Gv, Bv = gres[("g", j)].v(), gres[("b", j)].v()
            else:
                G, B = gq[j % 2], bq[j % 2]
                k.dma("sp", [(G.t[:], g_r[0:1, j * 512:(j + 1) * 512].partition_broadcast(128))], [], [G.v()])
                k.dma("sp", [(B.t[:], b_r[0:1, j * 512:(j + 1) * 512].partition_broadcast(128))], [], [B.v()])
                Gv, Bv = G.v(), B.v()
            for i in idxs:
                H = hres[i]
                k.stt(H.v(sl, parts=j), H.v(sl, parts=j), mvs[i].v((S_, slice(0, 1))), Gv, ALU.subtract, ALU.mult)
                k.stt(H.v(sl, parts=j), H.v(sl, parts=j), rstds[i].v(), Bv, ALU.mult, ALU.add)
        for n_, i in enumerate(idxs):
            H = hres[i]
            if mode == "out":
                k.dma("sp", [(ydsts[n_], H.t[:])], [H.v()], [], final=True)
                continue
            k.act(xb.v(), H.v(), AF.Copy)
            for g in range(2):
                b = k.bank()
                for c in range(8):
                    cc = g * 8 + c
                    k.tr(k.pv(b, 128, c * 128, bf=True), xb.v((S_, slice(cc * 128, cc * 128 + 128))), ident.v())
                dst = hT.v((S_, slice(g * 8, g * 8 + 8), slice(i * 128, i * 128 + 128)),
                           parts=[cc2 * NT + i for cc2 in range(g * 8, g * 8 + 8)])
                src = k.pv(b, 1024, 0, bf=True)
                src = View(src.T, src.ap.rearrange("p (c t) -> p c t", c=8), src.parts)
                k.copy("dve" if g == 0 else "act", dst, src)

    def hT_tile(c, i):
        return hT.v((S_, c, slice(i * 128, i * 128 + 128)), parts=c * NT + i)

    def hT_blk(c, ncol):
        return hT.v((S_, c, slice(0, ncol)), parts=range(c * NT, c * NT + NT))

    def proj_tok(i, Tw, wap, n, out):
        for c in range(16):
            k.mm(out, hT_tile(c, i), wview(Tw, wap[:, c, 0:n]), c == 0, c == 15)

    def proj_feat(Tw, wap, col0, ncol, out, kc=16, src=None):
        for c in range(kc):
            rhs = hT_blk(c, ncol) if src is None else src(c)
            k.mm(out, wview(Tw, wap[:, c, col0:col0 + 128]), rhs, c == 0, c == kc - 1)

    dstate = {"i": 0}

    def ffn_down(g, H, tiles):
        Td = []
        for dh in range(2):
            Td.append(wload(("down", g, dh), w_down[g * 512:(g + 1) * 512, dh * 1024:(dh + 1) * 1024], 4, 1024))
        for i, t in enumerate(tiles):
            if t["kind"] == "bnd":
                continue
            for n4 in range(4):
                Tdn, wdn = Td[n4 // 2]
                off = (n4 % 2) * 512
                b = 6 + (dstate["i"] % 2)
                dstate["i"] += 1
                for j in range(4):
                    k.mm(k.pv(b, 512), H.v((S_, j, slice(i * 128, i * 128 + 128))), wview(Tdn, wdn[:, j, off:off + 512]), j == 0, j == 3)
                sl = (S_, slice(n4 * 512, n4 * 512 + 512))
                if g == 0:
                    k.stt(hres[i].v(sl, parts=n4), hres[i].v(sl, parts=n4), ALPHA, k.pv(b, 512), ALU.mult, ALU.add)
                else:
                    k.tt("dve", hres[i].v(sl, parts=n4), hres[i].v(sl, parts=n4), k.pv(b, 512), ALU.add)

    def do_block(tiles, mode, after_w=None):
        nt = len(tiles)
        ncol = 128 * nt
        full = mode == "full"
        wq["q"] = "pool" if (full and tiles[0]["kind"] == "main") else "sp"
        for i, t in enumerate(tiles):
            k.dma("sp", [(hres[i].t[:], t["x"])], [], [hres[i].v()])
        load_gb(gin_r, bin_r)
        ln_block(list(range(nt)), "T", resident=(not full))
        pbs = [k.pair() for _ in tiles]
        for q in range(4):
            Ta, wa = wload(("hf", q), w_hf[:, q * 256:(q + 1) * 256], 16, 256)
            for i, t in enumerate(tiles):
                proj_tok(i, Ta, wa, 256, k.pv(pbs[i] + q // 2, 256, (q % 2) * 256))
        for i, t in enumerate(tiles):
            tri = tri64 if t["L"] == 64 else tri32
            pb = pbs[i]
            zf = k.pv(pb, 1024, nb=2)
            k.act(s32.v(), zf, AF.Sigmoid, scale=-1.0)
            k.tt("dve", s32.v(), s32.v(), oml.v(), ALU.mult)
            k.act(lf32.v(), s32.v(), AF.Ln, scale=-1.0, bias=1.0)
            pb2 = k.pair()
            for hh in range(2):
                k.mm(k.pv(pb2 + hh, 512), tri.v(), lf32.v((S_, slice(hh * 512, hh * 512 + 512))), True, True)
            k.act(enb.v(), k.pv(pb2, 1024, nb=2), AF.Exp, scale=-1.0)
            k.tt("dve", k2[i].v(), s32.v(), enb.v(), ALU.mult)
            if full:
                b = k.bank()
                for h in range(8):
                    k.tr(k.pv(b, 128, h * 128, bf=True), k2[i].v((S_, slice(h * 128, h * 128 + 128))), ident.v())
                src = k.pv(b, 1024, 0, bf=True)
                src = View(src.T, src.ap.rearrange("p (c t) -> p c t", c=8), src.parts)
                k.copy("dve", k2T.v((S_, S_, slice(i * 128, i * 128 + 128)), parts=i), src)
            pb3 = k.pair()
            for h in range(8):
                o = k.pv(pb3 + h // 4, 128, (h % 4) * 128)
                k.mm(o, lf32.v((S_, slice(h * 128, h * 128 + 128))), tri.v(), True, True)
            src = k.pv(pb3, 1024, nb=2)
            src = View(src.T, src.ap.rearrange("p (c t) -> p c t", c=8), src.parts)
            k.act(ebT.v((S_, S_, slice(i * 128, i * 128 + 128)), parts=i), src, AF.Exp)
        pbs = [k.pair() for _ in tiles]
        for q in range(4):
            Ta, wa = wload(("hi", q), w_hi[:, q * 256:(q + 1) * 256], 16, 256)
            for i, t in enumerate(tiles):
                proj_tok(i, Ta, wa, 256, k.pv(pbs[i] + q // 2, 256, (q % 2) * 256))
        for i, t in enumerate(tiles):
            k.copy("act", vb[i].v(), k.pv(pbs[i], 1024, nb=2))
        bks = [k.bank() for _ in tiles]
        for q in range(2):
            Tc, wc = wload(("ckr", q), w_ckr[:, q * 192:(q + 1) * 192], 16, 192)
            for i, t in enumerate(tiles):
                proj_tok(i, Tc, wc, 192, k.pv(bks[i], 192, q * 192))
        if after_w is not None:
            after_w()
        for i, t in enumerate(tiles):
            b = bks[i]
            k.dma("sp", [(rtab.t[:], rt_tok[t["rti"], :, :])], [], [rtab.v()])
            k.act(junk.v(), k.pv(b, 256), AF.Square, accum=ss.v())
            k.act(vtmp.v(), ss.v(), AF.Ln, scale=1.0 / 256, bias=epsrms.v())
            k.act(rstd.v(), vtmp.v(), AF.Exp, scale=-0.5)
            k.stt(latf.v(), k.pv(b, 256), rstd.v(), kvg.v(), ALU.mult, ALU.mult)
            k.tt("dve", krf.v(), k.pv(b, 64, 256), rtab.v((S_, slice(0, 64))), ALU.mult)
            k.tt("dve", krt.v(), k.pv(b, 64, 320), rtab.v((S_, slice(64, 128))), ALU.mult)
            k.tt("dve", krf.v(), krf.v(), krt.v(), ALU.add)
            if t.get("lat_out") is not None:
                k.dma("sp", [(t["lat_out"], latf.t[:])], [latf.v()], [], final=True)
                k.dma("sp", [(t["kr_out"], krf.t[:])], [krf.v()], [], final=True)
            if t.get("slot") is not None:
                LT, LK, KT, slot = t["keys"]
                k.copy("act", latb.v(), latf.v())
                k.copy("act", krb.v((S_, slice(0, 64))), krf.v())
                k.copy("act", krb.v((S_, slice(64, 128))), krf.v())
                k.copy("dve", LK.v((S_, slot, S_), parts=slot), latb.v())
                b2 = k.bank()
                for c in range(2):
                    k.tr(k.pv(b2, 128, c * 128, bf=True), latb.v((S_, slice(c * 128, c * 128 + 128))), ident.v())
                k.tr(k.pv(b2, 128, 256, bf=True), krb.v(), ident.v())
                src = k.pv(b2, 256, 0, bf=True)
                src = View(src.T, src.ap.rearrange("p (c t) -> p c t", c=2), src.parts)
                k.copy("dve", LT.v((S_, S_, slice(slot * 128, slot * 128 + 128)), parts=slot), src)
                k.copy("dve", KT.v((S_, slice(slot * 128, slot * 128 + 128)), parts=slot), k.pv(b2, 128, 256, bf=True))
        if full:
            for q in range(4):
                Ta, wa = wload(("hq", q), w_hq[:, q * 256:(q + 1) * 256], 16, 256)
                for hh in range(2):
                    h = 2 * q + hh
                    b = k.bank()
                    proj_feat(Ta, wa, hh * 128, ncol, k.pv(b, ncol))
                    k.tt("dve", q1T.v((S_, h, slice(0, ncol)), parts=h), k.pv(b, ncol),
                         ebT.v((S_, h, slice(0, ncol))), ALU.mult)
            for q in range(4):
                Ta, wa = wload(("hg", q), w_hg[:, q * 256:(q + 1) * 256], 16, 256)
                for hh in range(2):
                    h = 2 * q + hh
                    b = k.bank()
                    proj_feat(Ta, wa, hh * 128, ncol, k.pv(b, ncol))
                    k.act(sgT.v((S_, h, slice(0, ncol)), parts=h), k.pv(b, ncol), AF.Silu)
            bs_ = k.bank()
            for q in range(2):
                Ta, wa = wload(("cq", q), w_cq[:, q * 256:(q + 1) * 256], 16, 256)
                for hh in range(2):
                    c = 2 * q + hh
                    b = k.bank() if (q, hh) != (0, 0) else k.bank()
                    proj_feat(Ta, wa, hh * 128, ncol, k.pv(b, ncol))
                    k.copy("act", cq32.v((S_, c, slice(0, ncol)), parts=c), k.pv(b, ncol))
                    k.act(cqsq.v((S_, c, slice(0, ncol)), parts=c), k.pv(b, ncol), AF.Square)
            for c in range(4):
                k.mm(k.pv(bs_, ncol), ones.v(), cqsq.v((S_, c, slice(0, ncol)), parts=c), c == 0, c == 3)
            k.act(rtmp.v((S_, slice(0, ncol))), k.pv(bs_, ncol), AF.Ln, scale=1.0 / 512, bias=epsrms.v())
            k.act(rtmp.v((S_, slice(0, ncol))), rtmp.v((S_, slice(0, ncol))), AF.Exp, scale=-0.5)
            for c in range(4):
                k.stt(cqnT.v((S_, c, slice(0, ncol)), parts=c), cq32.v((S_, c, slice(0, ncol)), parts=c),
                      qag.v((S_, slice(c, c + 1))), rtmp.v((S_, slice(0, ncol))), ALU.mult, ALU.mult)
        h3 = lambda T_: View(T_, T_.t[:].rearrange("p (h v) -> p h v", h=8), (0,))
        for i, t in enumerate(tiles):
            L = t["L"]
            nch = 128 // L
            kind = t["kind"]
            tri = tri64 if L == 64 else tri32
            starts = []
            if kind == "bnd":
                k.copy("dve", S32.v(), Sbnd.v())
                k.copy("act", Sbf[st["cur"]].v(), S32.v())
            if kind == "main" and t["first"]:
                k.ts("dve", S32.v(), S32.v(), pf, ALU.mult)
                k.copy("act", Sbf[st["cur"]].v(), S32.v())
            for c in range(nch):
                if kind == "smp":
                    if c >= 2:
                        starts.append(Sbf[st["cur"]])
                        continue
                    k.dma("sp", [(S32.t[:].rearrange("p (h v) -> p h v", h=8), c_st[c].rearrange("h k v -> k h v"))], [], [S32.v()])
                    st["cur"] = (st["cur"] + 1) % 3
                    k.copy("act", Sbf[st["cur"]].v(), S32.v())
                starts.append(Sbf[st["cur"]])
                rows = slice(c * L, (c + 1) * L)
                pb = k.pair()
                for h in range(8):
                    o = k.pv(pb + h // 4, 128, (h % 4) * 128)
                    k.mm(o, k2[i].v((rows, slice(h * 128, h * 128 + 128))), vb[i].v((rows, slice(h * 128, h * 128 + 128))), True, True)
                k.tt("dve", stmp.v(), k.pv(pb, 1024, nb=2), S32.v(), ALU.add)
                col = i * 128 + (c + 1) * L - 1
                ebl = View(ebT, ebT.t[:, :, col:col + 1].to_broadcast([128, 8, 128]), (i,))
                if kind == "smp":
                    k.tt("dve", h3(stout), h3(stmp), ebl, ALU.mult)
                    k.dma("sp", [(st_s[c].rearrange("h k v -> k h v"), stout.t[:].rearrange("p (h v) -> p h v", h=8))],
                          [stout.v()], [], final=True)
                else:
                    k.tt("dve", h3(S32), h3(stmp), ebl, ALU.mult)
                    st["cur"] = (st["cur"] + 1) % 3
                    k.copy("act", Sbf[st["cur"]].v(), S32.v())
            if kind == "pre" and t["slot"] == NPRE - 2:
                k.copy("act", Sbnd.v(), S32.v())
            if not full:
                continue
            icols = slice(i * 128, i * 128 + 128)
            pa = k.pair()
            for h in range(8):
                k.mm(k.pv(pa + h // 4, 128, (h % 4) * 128), k2T.v((S_, h, icols), parts=i), q1T.v((S_, h, icols), parts=h), True, True)
            atv = k.pv(pa, 1024, nb=2)
            atv = View(atv.T, atv.ap.rearrange("p (h t) -> p h t", h=8), atv.parts)
            trib = View(tri, tri.t[:].unsqueeze(1).to_broadcast([128, 8, 128]), (0,))
            k.tt("dve", ATm.v(), atv, trib, ALU.mult)
            po = k.pair()
            for h in range(8):
                hc = slice(h * 128, h * 128 + 128)
                k.mm(k.pv(po + h // 4, 128, (h % 4) * 128), vb[i].v((S_, hc)), ATm.v((S_, h, S_)), True, False)
                for c in range(nch):
                    k.mm(k.pv(po + h // 4, L, (h % 4) * 128 + c * L), starts[c].v((S_, hc)),
                         q1T.v((S_, h, slice(i * 128 + c * L, i * 128 + (c + 1) * L)), parts=h), False, c == nch - 1)
            otv = k.pv(po, 1024, nb=2)
            k.act(sqb.v(), otv, AF.Square)
            ps2 = k.pair()
            for hh in range(2):
                k.mm(k.pv(ps2 + hh, 512), ones.v(), sqb.v((S_, slice(hh * 512, hh * 512 + 512))), True, True)
            k.act(otmp.v(), k.pv(ps2, 1024, nb=2), AF.Ln, scale=1.0 / 128, bias=epsrms.v())
            k.act(otmp.v(), otmp.v(), AF.Exp, scale=-0.5)
            k.tt("dve", otmp2.v(), otv, otmp.v(), ALU.mult)
            hngb = View(hng, hng.t[:].unsqueeze(2).to_broadcast([128, 8, 128]), (0,))
            k.tt("dve", h3(otmp2), h3(otmp2), hngb, ALU.mult)
            k.tt("dve", ocT.v((S_, slice(0, 8), icols), parts=[cc * NT + i for cc in range(8)]), h3(otmp2),
                 sgT.v((S_, S_, icols)), ALU.mult)
        if not full or STAGE < 3:
            return
        col0 = tiles[0]["fcol"]
        k.dma("sp", [(cosT.t[:, 0:ncol], rt_feat[0, :, col0:col0 + ncol])], [], [cosT.v()])
        k.dma("sp", [(sinT.t[:, 0:ncol], rt_feat[1, :, col0:col0 + ncol])], [], [sinT.v()])
        cs_ = (S_, slice(0, ncol))
        Ta, wa = wload(("qb", 0), w_qb[:, 0:1024], 4, 1024)
        for h in range(8):
            b = k.bank()
            for c in range(4):
                k.mm(k.pv(b, ncol), wview(Ta, wa[:, c, h * 128:h * 128 + 128]), cqnT.v((S_, c, slice(0, ncol)), parts=c), c == 0, c == 3)
            k.copy("act", qnT.v((S_, h, slice(0, ncol))), k.pv(b, ncol))
        Ta, wa = wload(("qb", 1), w_qb[:, 1024:2048], 4, 1024)
        for j in range(4):
            b1 = k.bank(); b2 = k.bank()
            for c in range(4):
                k.mm(k.pv(b1, ncol), wview(Ta, wa[:, c, j * 128:j * 128 + 128]), cqnT.v((S_, c, slice(0, ncol)), parts=c), c == 0, c == 3)
            for c in range(4):
                k.mm(k.pv(b2, ncol), wview(Ta, wa[:, c, 512 + j * 128:512 + j * 128 + 128]), cqnT.v((S_, c, slice(0, ncol)), parts=c), c == 0, c == 3)
            k.tt("dve", rtmp.v(cs_), k.pv(b1, ncol), cosT.v(cs_), ALU.mult)
            k.tt("dve", rtmp2.v(cs_), k.pv(b2, ncol), sinT.v(cs_), ALU.mult)
            for hh in range(2):
                pr_ = slice(64 * hh, 64 * hh + 64)
                po_ = slice(64 * (1 - hh), 64 * (1 - hh) + 64)
                k.tt("dve", qr8.v((pr_, 2 * j + hh, slice(0, ncol))), rtmp.v((pr_, slice(0, ncol))), rtmp2.v((pr_, slice(0, ncol))), ALU.add)
                k.memset("dve", qr8.v((po_, 2 * j + hh, slice(0, ncol))), 0.0)
        jobs = []
        if tiles[0]["kind"] == "smp":
            for j in range(2):
                keys = [dict(slot=16 + 8 * j + t_) for t_ in range(8)] + [dict(slot=32, m32=j)]
                jobs.append((32 * j, 32 * j + 32, keys))
            keys = [dict(slot=s_) for s_ in range(NPRE - 1)] + [dict(slot=NPRE - 1, zb=True)]
            jobs.append((128, 256, keys))
        else:
            s0_ = tiles[0]["slot"]
            keys = [dict(slot=s_, bias=True) for s_ in range(NPRE)] + [dict(slot=s_) for s_ in range(NPRE, s0_)]
            for i in range(nt):
                keys.append(dict(slot=s0_ + i, lo=128 * i, zb=True))
            jobs.append((0, ncol, keys))
        pi = 0
        for pr in range(4):
            Q = qpT[pr % 2]
            for hh in range(2):
                h = 2 * pr + hh
                for c in range(2):
                    b = 6 + c
                    k.mm(k.pv(b, ncol), wknT.v((S_, h, slice(c * 128, c * 128 + 128))), qnT.v((S_, h, slice(0, ncol))), True, True)
                    k.copy("act" if c == 0 else "dve", Q.v((S_, c, hh, slice(0, ncol))), k.pv(b, ncol))

            def b3(bank, lo_, nn_, n_):
                v_ = k.pv(bank, 2 * n_)
                return View(v_.T, v_.ap.rearrange("p (h n) -> p h n", h=2)[:, :, lo_:lo_ + nn_], v_.parts)
            for (c0, c1, keys) in jobs:
                n = c1 - c0
                o0, o1, osum = 0, 1, 2
                def scores(key):
                    nonlocal pi
                    slot = key["slot"]; lo = key.get("lo", 0)
                    nn = n - lo
                    cols = slice(c0 + lo, c1)
                    kc = slice(slot * 128, slot * 128 + 128)
                    bS = 3 + (pi % 3)
                    sv_ = b3(bS, 0, nn, nn)
                    k.mm(sv_, latT.v((S_, 0, kc), parts=slot), Q.v((S_, 0, S_, cols)), True, False)
                    k.mm(sv_, latT.v((S_, 1, kc), parts=slot), Q.v((S_, 1, S_, cols)), False, False)
                    k.mm(sv_, krT.v((S_, kc), parts=slot), qr8.v((S_, slice(2 * pr, 2 * pr + 2), cols)), False, True)
                    P = PT[pi % 3]; pi += 1
                    return (sv_, P, nn, lo, slot)

                def softmax_pv(key, sc, first, last):
                    sv_, P, nn, lo, slot = sc
                    Pv = P.v((S_, S_, slice(0, nn)))
                    k.act(Pv, sv_, AF.Exp, scale=SCALE, bias=(pbias if key.get("bias") else None))
                    if key.get("zb"):
                        k.memset("dve", P.v((slice(64, 128), S_, slice(0, 64))), 0.0)
                    if "m32" in key:
                        jj = key["m32"]
                        m_ = View(same32, same32.t[:, 32 * jj:32 * jj + 32].unsqueeze(1).to_broadcast([128, 2, 32]), (0,))
                        k.tt("dve", Pv, Pv, m_, ALU.mult)
                    k.mm(b3(o0, lo, nn, n), latK.v((S_, slot, slice(0, 128)), parts=slot), Pv, first, last)
                    k.mm(b3(o1, lo, nn, n), latK.v((S_, slot, slice(128, 256)), parts=slot), Pv, first, last)
                    k.mm(b3(osum, lo, nn, n), ones.v(), Pv, first, last)
                sc_next = scores(keys[0])
                for ki, key in enumerate(keys):
                    sc_cur = sc_next
                    if ki + 1 < len(keys):
                        sc_next = scores(keys[ki + 1])
                    softmax_pv(key, sc_cur, ki == 0, ki == len(keys) - 1)
                k.copy("act", OTs.v((S_, 0, S_, slice(0, n))), b3(o0, 0, n, n))
                k.copy("dve", OTs.v((S_, 1, S_, slice(0, n))), b3(o1, 0, n, n))
                rs_ = rsum.v((S_, S_, slice(0, n)))
                os_ = b3(osum, 0, n, n)
                k.op("dve", lambda hd, rs_=rs_, os_=os_: hd.reciprocal(out=rs_.ap, in_=os_.ap), [os_], [rs_])
                for hh in range(2):
                    h = 2 * pr + hh
                    bo = 6 + hh
                    k.mm(k.pv(bo, n), wv.v((S_, 0, h, S_)), OTs.v((S_, 0, hh, slice(0, n))), True, False)
                    k.mm(k.pv(bo, n), wv.v((S_, 1, h, S_)), OTs.v((S_, 1, hh, slice(0, n))), False, True)
                    k.tt("dve", ocT.v((S_, 8 + h, slice(c0, c1)), parts=range((8 + h) * NT, (8 + h) * NT + NT)), k.pv(bo, n),
                         rsum.v((S_, hh, slice(0, n))), ALU.mult)
        if STAGE < 4:
            return
        for n8 in range(8):
            Ta, wa = wload(("out", n8), w_out[:, n8 * 256:(n8 + 1) * 256], 16, 256)
            for i in range(nt):
                b = k.bank()
                for c in range(16):
                    k.mm(k.pv(b, 256), ocT.v((S_, c, slice(i * 128, i * 128 + 128)), parts=c * NT + i), wview(Ta, wa[:, c, :]), c == 0, c == 15)
                sl = (S_, slice(n8 * 256, n8 * 256 + 256))
                k.stt(hres[i].v(sl, parts=n8 // 2), hres[i].v(sl, parts=n8 // 2), ALPHA, k.pv(b, 256), ALU.mult, ALU.add)
        load_gb(g1_r, b1_r)
        ln_block(list(range(nt)), "T")
        segs = []
        for i, t in enumerate(tiles):
            if t["kind"] == "smp":
                segs += [(0, 32, ("cache", 0)), (32, 32, ("cache", 1)), (64, 64, ("zero",))]
            elif t["kind"] == "bnd":
                segs += [(128 * i, 128, ("bnd",))]
        if tiles[0]["kind"] == "main":
            segs = [(0, ncol, ("chain",))]
        gcol = 128 if tiles[0]["kind"] == "smp" else ncol
        for g in range(NFC // 4):
            H = hTg[g % 2]
            for half in range(2):
                c0f = g * 512 + half * 256
                Tu, wu = wload(("up", g, half), w_up[:, c0f:c0f + 256], 16, 256)
                Tg, wg = wload(("gate", g, half), w_gate[:, c0f:c0f + 256], 16, 256)
                for jj in range(2):
                    j = half * 2 + jj
                    cc = g * 4 + j
                    bu = 2 * (cc % 3); bg = bu + 1
                    proj_feat(Tu, wu, jj * 128, ncol, k.pv(bu, ncol))
                    proj_feat(Tg, wg, jj * 128, gcol, k.pv(bg, gcol))
                    A = aa[cc % 2]
                    w0 = cw.v((S_, slice(cc * 3, cc * 3 + 1))); w1 = cw.v((S_, slice(cc * 3 + 1, cc * 3 + 2)))
                    w2 = cw.v((S_, slice(cc * 3 + 2, cc * 3 + 3)))
                    for (sc0, sn, src) in segs:
                        if src[0] == "bnd":
                            k.ts("dve", hist.v((S_, S_, cc), parts=cc), k.pv(bu, 2, sc0 + sn - 2), pf, ALU.mult)
                            continue
                        k.ts("dve", A.v((S_, slice(sc0, sc0 + sn))), k.pv(bu, sn, sc0), w2, ALU.mult, cb.v((S_, slice(cc, cc + 1))), ALU.add)
                        k.stt(A.v((S_, slice(sc0 + 1, sc0 + sn))), k.pv(bu, sn - 1, sc0), w1, A.v((S_, slice(sc0 + 1, sc0 + sn))), ALU.mult, ALU.add)
                        k.stt(A.v((S_, slice(sc0 + 2, sc0 + sn))), k.pv(bu, sn - 2, sc0), w0, A.v((S_, slice(sc0 + 2, sc0 + sn))), ALU.mult, ALU.add)
                        if src[0] == "chain":
                            h0 = hist.v((S_, 0, slice(cc, cc + 1)), parts=cc); h1 = hist.v((S_, 1, slice(cc, cc + 1)), parts=cc)
                        elif src[0] == "cache":
                            h0 = hcache.v((S_, src[1], 0, slice(cc, cc + 1))); h1 = hcache.v((S_, src[1], 1, slice(cc, cc + 1)))
                        else:
                            h0 = h1 = None
                        if h0 is not None:
                            a0 = A.v((S_, slice(sc0, sc0 + 1))); a1 = A.v((S_, slice(sc0 + 1, sc0 + 2)))
                            k.stt(a0, h1, w1, a0, ALU.mult, ALU.add)
                            k.stt(a0, h0, w0, a0, ALU.mult, ALU.add)
                            k.stt(a1, h1, w0, a1, ALU.mult, ALU.add)
                        tail = k.pv(bu, 2, sc0 + sn - 2)
                        if src[0] == "chain":
                            k.copy("dve", hist.v((S_, S_, cc), parts=cc), tail)
                        elif src[0] == "bnd":
                            k.ts("dve", hist.v((S_, S_, cc), parts=cc), tail, pf, ALU.mult)
                        elif src[0] == "cache":
                            k.copy("dve", hsout.v((S_, src[1], S_, cc), parts=src[1]), tail)
                    Afull = A.v((S_, slice(0, gcol)))
                    k.act(Afull, Afull, AF.Silu)
                    k.tt("dve", H.v((S_, j, slice(0, gcol))), Afull, k.pv(bg, gcol), ALU.mult)
            if g > 0:
                ffn_down(g - 1, hTg[(g - 1) % 2], tiles)
        ffn_down(NFC // 4 - 1, hTg[(NFC // 4 - 1) % 2], tiles)
        load_gb(g2_r, b2_r)
        yi = [i for i, t in enumerate(tiles) if t.get("y") is not None]
        ln_block(yi, "out", ydsts=[tiles[i]["y"] for i in yi])

    def conv_out(src_view, dsts):
        b = k.bank()
        sv = View(src_view.T, src_view.ap.rearrange("p r c -> p (r c)"), src_view.parts)
        k.op("pe", lambda h: h.transpose(out=k.pv(b, 128, np_=88).ap, in_=sv.ap, identity=identf.t[:]), [sv, identf.v()], [k.pv(b, 128)])
        k.copy("dve", hout.v((slice(0, 88), S_)), k.pv(b, 128, np_=88))
        for r in range(2):
            k.dma("sp", [(dsts[r].rearrange("(c p) -> c p", p=128), hout.t[44 * r:44 * r + 44, :])], [hout.v()], [], final=True)

    st = {"cur": 0}
    tiles_pre = []
    for ti in range(NPRE):
        tiles_pre.append(dict(kind="pre", x=xp[ti], L=64, rti=ti, slot=ti, keys=(latT, latK, krT, ti)))
    if STAGE >= 1:
        k.dma("pool", [(latK.t[:, 16 + 8 * j:24 + 8 * j, :], c_lat[j].rearrange("(t p) c -> p t c", p=128)) for j in range(2)],
              [], [latK.v(parts=range(16, 32))])
        krtmps = [Sub(sqb, sqb.t[:].rearrange("p (t c) -> p t c", t=8)),
                  Sub(rsum, rsum.t[:].rearrange("p a t -> p (a t)").bitcast(BF16).rearrange("p (t c) -> p t c", t=8))]
        for j in range(2):
            k.dma("pool", [(krtmps[j].t[:, :, 0:64], c_kr[j].rearrange("(t p) c -> p t c", p=128)),
                           (krtmps[j].t[:, :, 64:128], c_kr[j].rearrange("(t p) c -> p t c", p=128))], [], [krtmps[j].v()],
                  owner=krtmps[j].parent)
        first = [(("hf", q), w_hf[:, q * 256:(q + 1) * 256], 16, 256) for q in range(4)]
        first += [(("hi", q), w_hi[:, q * 256:(q + 1) * 256], 16, 256) for q in range(4)]
        first += [(("ckr", q), w_ckr[:, q * 192:(q + 1) * 192], 16, 192) for q in range(2)]
        groups = [first, pre_list[0:12], pre_list[12:20], pre_list[20:44], pre_list[44:68], pre_list[68:]]
        for gi, grp in enumerate(groups):
            own = TT(f"pc{gi}", None)
            evs = []
            for (key, src2d, kc, n) in grp:
                d = nc.dram_tensor(f"scr_{len(scr)}", [128, kc * n], BF16).ap()
                Sx = TT(f"scr{len(scr)}", d, 4)
                scr[key] = Sx
                k.dma("pool", [(d.rearrange("p (c n) -> p c n", c=kc), src2d.rearrange("(c p) n -> p c n", p=128))], [], [Sx.v()], owner=own)
                evs.append(Sx)
            fin = (own.dsem, own.dval, "dma")
            for Sx in evs:
                Sx.w = [fin] * 4
        f32v = lambda T_, pat, lo: T_.t[:].rearrange(pat).bitcast(F32)[:, lo:lo + 512]
        homes = [Sub(sgT, f32v(sgT, "p h t -> p (h t)", 0)), Sub(sgT, f32v(sgT, "p h t -> p (h t)", 512)),
                 Sub(k2T, f32v(k2T, "p h t -> p (h t)", 0)), Sub(k2T, f32v(k2T, "p h t -> p (h t)", 512)),
                 Sub(OTs, f32v(OTs, "p a b t -> p (a b t)", 0)), Sub(cqsq, f32v(cqsq, "p c t -> p (c t)", 0)),
                 Sub(cqnT, f32v(cqnT, "p c t -> p (c t)", 0)), Sub(ATm, f32v(ATm, "p h t -> p (h t)", 0))]
        for j in range(4):
            gres[("g", j)] = homes[j]
            gres[("b", j)] = homes[4 + j]
            k.dma("sp", [(homes[j].t, gin_r[0:1, j * 512:(j + 1) * 512].partition_broadcast(128))], [], [homes[j].v()], owner=homes[j].parent)
            k.dma("sp", [(homes[4 + j].t, bin_r[0:1, j * 512:(j + 1) * 512].partition_broadcast(128))], [], [homes[4 + j].v()], owner=homes[4 + j].parent)
        for pbk in range(NPRE // NT):
            do_block(tiles_pre[pbk * NT:(pbk + 1) * NT], "pre")
    if STAGE >= 2:
        for j in range(2):
            krtmp = krtmps[j]
            for t_ in range(8):
                slot = 16 + 8 * j + t_
                b = k.bank()
                for c in range(2):
                    k.tr(k.pv(b, 128, c * 128, bf=True), latK.v((S_, slot, slice(c * 128, c * 128 + 128)), parts=slot), ident.v())
                k.tr(k.pv(b, 128, 256, bf=True), krtmp.v((S_, t_, S_)), ident.v())
                src = k.pv(b, 256, 0, bf=True)
                src = View(src.T, src.ap.rearrange("p (c t) -> p c t", c=2), src.parts)
                k.copy("dve", latT.v((S_, S_, slice(slot * 128, slot * 128 + 128)), parts=slot), src)
                k.copy("dve", krT.v((S_, slice(slot * 128, slot * 128 + 128)), parts=slot), k.pv(b, 128, 256, bf=True))
        for j in range(2):
            k.dma("sp", [(hout.t[0:88, :], c_conv[j].rearrange("r (c p) -> (r c) p", p=128))], [], [hout.v()])
            b = k.bank()
            k.op("pe", lambda h, b=b: h.transpose(out=k.pv(b, 88).ap, in_=hout.t[0:88, :], identity=identf.t[0:88, 0:88]),
                 [hout.v(), identf.v()], [k.pv(b, 88)])
            k.copy("dve", View(hcache, hcache.t[:, j, :, :].rearrange("p r c -> p (r c)"), (0,)), k.pv(b, 88))
        t_s = dict(kind="smp", x=xs[:, :], L=32, rti=NPRE + NMAIN, slot=32, keys=(latT, latK, krT, 32),
                   lat_out=lat_s[:, :], kr_out=kr_s[:, :], y=y_s[:, :], fcol=0)
        t_b = dict(kind="bnd", x=xp[NPRE - 1], L=64, rti=NPRE - 1, slot=None, fcol=128)
        do_block([t_s, t_b], "full")
        if STAGE >= 4:
            for j in range(2):
                conv_out(hsout.v((S_, j, S_, S_), parts=j), [conv_s[j, 0], conv_s[j, 1]])
    if STAGE >= 5:
        for bi in range(NMAIN // NT):
            tl = []
            for i in range(NT):
                ti = bi * NT + i
                tl.append(dict(kind="main", first=(ti == 0), x=xm[ti], L=64, rti=NPRE + ti, slot=NPRE + ti,
                               keys=(latT, latK, krT, NPRE + ti), lat_out=lat_main[ti], kr_out=kr_main[ti], y=y_main[ti],
                               fcol=256 + ti * 128))
            do_block(tl, "full")
        conv_out(hist.v(), [conv_fin[0], conv_fin[1]])
    k.dma("sp", [(st_fin.rearrange("h k v -> k h v"), S32.t[:].rearrange("p (h v) -> p h v", h=8))], [S32.v()], [], final=True)
    k.finish()
    return k


def _consts():
    idn = np.eye(128, dtype=np.float32)
    s = np.arange(128)
    tri64 = ((s[:, None] <= s[None, :]) & (s[:, None] // 64 == s[None, :] // 64)).astype(np.float32)
    tri32 = ((s[:, None] <= s[None, :]) & (s[:, None] // 32 == s[None, :] // 32)).astype(np.float32)
    same32 = (s[:, None] // 32 == s[None, :] // 32).astype(np.float32)
    return idn, tri64, tri32, same32


def _rope(pos):
    inv = (10000.0 ** (-np.arange(0, 64, 2, dtype=np.float32) / 64)).astype(np.float32)
    ang = pos.astype(np.float32)[:, None] * inv[None, :]
    return np.cos(ang).astype(np.float32), np.sin(ang).astype(np.float32)


_CACHE = {}


def kernel(x_prompt, x_sample, cache_kv_latent, cache_k_rope, state_hgrn, cache_ffn_conv,
           lb_param, ln_in_g, ln_in_b, w_in, hgrn_norm_g, q_a_g, w_q_b, kv_a_g, w_kv_b, w_out,
           ln1_g, ln1_b, w_ffn_up, w_ffn_gate, conv_w, conv_b, w_ffn_down, ln2_g, ln2_b):
    f = lambda a: np.ascontiguousarray(np.asarray(a, dtype=np.float32))
    x_prompt = f(x_prompt); x_sample = f(x_sample); w_in = f(w_in)[0]
    idn, tri64, tri32, same32 = _consts()
    w_hq = f(w_in[:, 0:1024]); w_hf = f(w_in[:, 1024:2048]); w_hi = f(w_in[:, 2048:3072]); w_hg = f(w_in[:, 3072:4096])
    w_cq = f(w_in[:, 4096:4608])
    kr_cols = w_in[:, 4864:4928]
    w_ckr = f(np.concatenate([w_in[:, 4608:4864], kr_cols, kr_cols[:, 32:64], kr_cols[:, 0:32]], axis=1))
    wq = f(w_q_b)[0].reshape(512, 8, 192)
    nope = wq[:, :, 0:128].reshape(512, 1024)
    rope = wq[:, :, 128:192].reshape(512, 512)
    rope_sw = np.concatenate([wq[:, :, 160:192], wq[:, :, 128:160]], axis=2).reshape(512, 512)
    w_qb = f(np.concatenate([nope, rope, rope_sw], axis=1))
    shared = dict(
        w_hq=w_hq, w_hf=w_hf, w_hi=w_hi, w_hg=w_hg, w_cq=w_cq, w_ckr=w_ckr, w_qb=w_qb, w_kvb=f(w_kv_b)[0],
        w_out=f(w_out)[0], w_up=f(w_ffn_up)[0], w_gate=f(w_ffn_gate)[0], w_down=f(w_ffn_down)[0],
        lbp=f(lb_param), gin_r=f(ln_in_g)[None], bin_r=f(ln_in_b)[None], g1_r=f(ln1_g), b1_r=f(ln1_b),
        g2_r=f(ln2_g), b2_r=f(ln2_b), kvg_r=f(kv_a_g),
        qag_c=f(f(q_a_g)[0].reshape(4, 128).T), hng_c=f(f(hgrn_norm_g)[0].reshape(8, 128).T),
        cw_c=f(f(conv_w)[0].reshape(3, NFC, 128).transpose(2, 1, 0).reshape(128, NFC * 3)),
        cb_c=f(f(conv_b)[0].reshape(NFC, 128).T),
        idn=idn, tri64=tri64, tri32=tri32, same32=same32,
    )
    in_maps = []
    for c in range(8):
        b, p = c // 2, c % 2
        m = dict(shared)
        m["xp"] = f(x_prompt[b, 0:2048].reshape(NPRE, 128, D))
        m["xm"] = f(x_prompt[b, p * 2048:(p + 1) * 2048].reshape(NMAIN, 128, D))
        xs = np.zeros((128, D), np.float32)
        xs[0:32] = x_sample[2 * c]; xs[32:64] = x_sample[2 * c + 1]
        m["xs"] = xs
        m["c_lat"] = f(cache_kv_latent[0, 2 * c:2 * c + 2]); m["c_kr"] = f(cache_k_rope[0, 2 * c:2 * c + 2])
        m["c_st"] = f(state_hgrn[0, 2 * c:2 * c + 2]); m["c_conv"] = f(cache_ffn_conv[0, 2 * c:2 * c + 2])
        pos_pre = np.arange(2048); pos_main = p * 2048 + np.arange(2048)
        pos_s = 1024 + (np.arange(128) % 32)
        allpos = np.concatenate([pos_pre, pos_main, pos_s])
        cs, sn = _rope(allpos)
        rt = np.concatenate([cs, cs, -sn, sn], axis=1).astype(np.float32)
        m["rt_tok"] = f(rt.reshape(NPRE + NMAIN + 1, 128, 128))
        posf = np.concatenate([pos_s, pos_pre[1920:2048], pos_main])
        cs, sn = _rope(posf)
        cT = np.concatenate([cs, cs], axis=1).T; sT = np.concatenate([-sn, sn], axis=1).T
        m["rt_feat"] = f(np.stack([np.concatenate([cT, cT], 0), np.concatenate([sT, sT], 0)]))
        fl = np.zeros((128, 2), np.float32); fl[:, 0] = float(p); fl[:, 1] = 0.0 if p == 1 else -30000.0
        m["flg"] = fl
        in_maps.append(m)

    if "nc" not in _CACHE:
        nc = bass.Bass("TRN2", target_bir_lowering=False)
        with contextlib.ExitStack() as es:
            kk = build_program(nc, es)
        _CACHE["nc"] = nc
    nc = _CACHE["nc"]
    res = run_bass_kernel_spmd(nc, in_maps, core_ids=list(range(8)))
    R = res.results
    _CACHE["R"] = R
    y_prompt = np.zeros((4, 4096, D), np.float32); p_lat = np.zeros((1, 4, 4096, 256), np.float32)
    p_kr = np.zeros((1, 4, 4096, 64), np.float32); p_st = np.zeros((1, 4, 8, 128, 128), np.float32)
    p_conv = np.zeros((1, 4, 2, DFF), np.float32)
    y_sample = np.zeros((16, 32, D), np.float32); s_lat = np.zeros((1, 16, 32, 256), np.float32)
    s_kr = np.zeros((1, 16, 32, 64), np.float32); s_st = np.zeros((1, 16, 8, 128, 128), np.float32)
    s_conv = np.zeros((1, 16, 2, DFF), np.float32)
    for c in range(8):
        b, p = c // 2, c % 2
        r = R[c]
        y_prompt[b, p * 2048:(p + 1) * 2048] = r["y_main"].reshape(2048, D)
        p_lat[0, b, p * 2048:(p + 1) * 2048] = r["lat_main"].reshape(2048, 256)
        p_kr[0, b, p * 2048:(p + 1) * 2048] = r["kr_main"].reshape(2048, 64)
        if p == 1:
            p_st[0, b] = r["st_fin"]; p_conv[0, b] = r["conv_fin"]
        for j in range(2):
            y_sample[2 * c + j] = r["y_s"][32 * j:32 * j + 32]
            s_lat[0, 2 * c + j] = r["lat_s"][32 * j:32 * j + 32]
            s_kr[0, 2 * c + j] = r["kr_s"][32 * j:32 * j + 32]
            s_st[0, 2 * c + j] = r["st_s"][j]; s_conv[0, 2 * c + j] = r["conv_s"][j]
    return (y_prompt, y_sample, p_lat, p_kr, p_st, p_conv, s_lat, s_kr, s_st, s_conv)
```

```python
import contextlib
import numpy as np
import concourse.bass as bass
import concourse.mybir as mybir
from concourse.bass_utils import run_bass_kernel_spmd

F32 = mybir.dt.float32
BF16 = mybir.dt.bfloat16
AF = mybir.ActivationFunctionType
ALU = mybir.AluOpType
EPOCH = 30000

D = 2048
NH = 8
DFF = 5632
NFC = 44
ALPHA = 2.0 ** 0.25
LN_EPS = 1e-5
RMS_EPS = 1e-6
SCALE = 192.0 ** -0.5
NT = 2
TB = 128 * NT
NPRE = 16
NMAIN = 16
RING = 4
STAGE = 99


class View:
    __slots__ = ("T", "ap", "parts")

    def __init__(self, T, ap, parts):
        self.T, self.ap, self.parts = T, ap, parts


class TT:
    def __init__(self, name, t, nparts=1):
        self.name, self.t, self.nparts = name, t, nparts
        self.w = [None] * nparts
        self.r = [dict() for _ in range(nparts)]
        self.dsem = None
        self.dval = 0

    def v(self, key=None, parts=None, f=None):
        ap = self.t[key] if key is not None else self.t[:]
        if f is not None:
            ap = f(ap)
        if parts is None:
            parts = range(self.nparts)
        elif isinstance(parts, int):
            parts = (parts,)
        return View(self, ap, parts)


class Sub:
    def __init__(self, parent, ap):
        self.parent, self.t = parent, ap

    def v(self, key=None, parts=None, f=None):
        ap = self.t[key] if key is not None else self.t
        return View(self.parent, ap, range(self.parent.nparts))


class Eng:
    def __init__(self, name):
        self.name = name
        self.q = []
        self.sems = []
        self.cnt = 0
        self.waited = {}


class K:
    def __init__(self, nc, es):
        self.nc, self.es = nc, es
        self.E = {n: Eng(n) for n in ("pe", "act", "dve", "pool", "sp")}
        self.nsem = 0
        self.final_events = []
        self.bank_rr = 0
        self.pair_rr = 0
        self.ninstr = 0

    def new_sem(self, name):
        self.nsem += 1
        return self.es.enter_context(self.nc.semaphore(f"{name}_{self.nsem}"))

    def sb(self, name, shape, dtype, nparts=1):
        t = self.es.enter_context(self.nc.sbuf_tensor("sb_" + name, list(shape), dtype))
        return TT(name, t, nparts)

    def _cur_sem(self, e):
        if not e.sems or e.cnt >= EPOCH:
            e.sems.append(self.new_sem("e" + e.name))
            e.cnt = 0
        return e.sems[-1]

    def _deps(self, reads, writes, eng):
        deps = []
        for v in reads:
            T = v.T
            for p in v.parts:
                w = T.w[p]
                if w is not None:
                    deps.append(w)
        for v in writes:
            T = v.T
            for p in v.parts:
                w = T.w[p]
                if w is not None and not (eng == "pe" and w[2] == "pe"):
                    deps.append(w)
                deps.extend(T.r[p].values())
        return deps

    def _emit_waits(self, e, deps):
        need = {}
        for (sem, val, _n) in deps:
            kk = id(sem)
            if e.waited.get(kk, 0) >= val:
                continue
            if kk not in need or need[kk][1] < val:
                need[kk] = (sem, val)
        for kk, (sem, val) in need.items():
            e.waited[kk] = val
            e.q.append(lambda h, sem=sem, val=val: h.wait_ge(sem, val))

    def _record(self, reads, writes, ev):
        for v in reads:
            for p in v.parts:
                v.T.r[p][id(ev[0])] = ev
        for v in writes:
            for p in v.parts:
                v.T.w[p] = ev
                v.T.r[p] = {}

    def op(self, eng, fn, reads, writes):
        e = self.E[eng]
        self._emit_waits(e, self._deps(reads, writes, eng))
        sem = self._cur_sem(e)
        e.cnt += 1
        e.q.append(lambda h, sem=sem: fn(h).then_inc(sem, 1))
        ev = (sem, e.cnt, eng)
        self._record(reads, writes, ev)
        self.ninstr += 1
        return ev

    def dma(self, eng, pairs, reads, writes, final=False, owner=None):
        e = self.E[eng]
        self._emit_waits(e, self._deps(reads, writes, eng))
        tv = owner if owner is not None else (writes + reads)[0].T
        if tv.dsem is None:
            tv.dsem = self.new_sem("d" + tv.name)
        sem = tv.dsem
        for (o, i) in pairs:
            e.q.append(lambda h, o=o, i=i, sem=sem: h.dma_start(out=o, in_=i).then_inc(sem, 16))
        tv.dval += 16 * len(pairs)
        ev = (sem, tv.dval, "dma")
        self._record(reads, writes, ev)
        if final:
            self.final_events.append(ev)
        self.ninstr += len(pairs)
        return ev

    def init_psum(self):
        t = self.es.enter_context(self.nc.psum_tensor("psum_all", [128, 4096], F32))
        self.P = TT("psum_all", t, 8)

    def bank(self):
        i = self.bank_rr
        self.bank_rr = (self.bank_rr + 1) % 8
        return i

    def pair(self):
        j = self.pair_rr
        self.pair_rr = (self.pair_rr + 1) % 4
        self.bank_rr = (2 * j + 2) % 8
        return 2 * j

    def pv(self, bank, n=512, off=0, bf=False, nb=1, np_=128):
        parts = tuple(range(bank, bank + nb))
        if not bf:
            ap = self.P.t[0:np_, bank * 512 + off: bank * 512 + off + n]
        else:
            ap = self.P.t[0:np_, bank * 512: (bank + nb) * 512].bitcast(BF16)[:, off: off + n]
        return View(self.P, ap, parts)

    def finish(self):
        e = self.E["sp"]
        self._emit_waits(e, self.final_events)
        block = self.es.enter_context(self.nc.Block())
        E = self.E

        def run(q):
            def f(h):
                for c in q:
                    c(h)
            return f
        block.tensor(run(E["pe"].q))
        block.scalar(run(E["act"].q))
        block.vector(run(E["dve"].q))
        block.gpsimd(run(E["pool"].q))
        block.sync(run(E["sp"].q))

    def mm(self, out, lhsT, rhs, start, stop):
        self.op("pe", lambda h: h.matmul(out.ap, lhsT=lhsT.ap, rhs=rhs.ap, start=start, stop=stop),
                [lhsT, rhs], [out])

    def tr(self, out, in_, ident):
        self.op("pe", lambda h: h.transpose(out=out.ap, in_=in_.ap, identity=ident.ap), [in_, ident], [out])

    def act(self, out, in_, func, scale=None, bias=None, accum=None, eng="act"):
        reads = [in_]
        kw = {}
        if scale is not None:
            if isinstance(scale, View):
                reads.append(scale)
                kw["scale"] = scale.ap
            else:
                kw["scale"] = float(scale)
        if bias is not None:
            if isinstance(bias, View):
                reads.append(bias)
                kw["bias"] = bias.ap
            else:
                kw["bias"] = float(bias)
        writes = [out]
        if accum is not None:
            writes.append(accum)
            kw["accum_out"] = accum.ap
        self.op("act", lambda h: h.activation(out=out.ap, in_=in_.ap, func=func, **kw), reads, writes)

    def tt(self, eng, out, in0, in1, op):
        self.op(eng, lambda h: h.tensor_tensor(out=out.ap, in0=in0.ap, in1=in1.ap, op=op), [in0, in1], [out])

    def ts(self, eng, out, in0, s1, op0, s2=None, op1=None):
        reads = [in0]
        a1 = s1.ap if isinstance(s1, View) else float(s1)
        if isinstance(s1, View):
            reads.append(s1)
        a2 = None
        if s2 is not None:
            a2 = s2.ap if isinstance(s2, View) else float(s2)
            if isinstance(s2, View):
                reads.append(s2)
        if op1 is None:
            self.op(eng, lambda h: h.tensor_scalar(out=out.ap, in0=in0.ap, scalar1=a1, scalar2=None, op0=op0),
                    reads, [out])
        else:
            self.op(eng, lambda h: h.tensor_scalar(out=out.ap, in0=in0.ap, scalar1=a1, scalar2=a2, op0=op0, op1=op1),
                    reads, [out])

    def stt(self, out, in0, scalar, in1, op0, op1):
        reads = [in0, in1]
        a = scalar.ap if isinstance(scalar, View) else float(scalar)
        if isinstance(scalar, View):
            reads.append(scalar)
        self.op("dve", lambda h: h.scalar_tensor_tensor(out=out.ap, in0=in0.ap, scalar=a, in1=in1.ap, op0=op0, op1=op1),
                reads, [out])

    def copy(self, eng, out, in_):
        if eng == "act":
            self.act(out, in_, AF.Copy)
        else:
            self.op(eng, lambda h: h.tensor_copy(out=out.ap, in_=in_.ap), [in_], [out])

    def memset(self, eng, out, val):
        self.op(eng, lambda h: h.memset(out.ap, val), [], [out])


def build_program(nc, es):
    k = K(nc, es)
    k.init_psum()
    S_ = slice(None)

    def din(name, shape):
        return nc.dram_tensor(name, list(shape), F32, kind="ExternalInput").ap()

    def dout(name, shape):
        return nc.dram_tensor(name, list(shape), F32, kind="ExternalOutput").ap()

    xp = din("xp", [NPRE, 128, D]); xm = din("xm", [NMAIN, 128, D]); xs = din("xs", [128, D])
    w_hq = din("w_hq", [D, 1024]); w_hf = din("w_hf", [D, 1024]); w_hi = din("w_hi", [D, 1024]); w_hg = din("w_hg", [D, 1024])
    w_cq = din("w_cq", [D, 512]); w_ckr = din("w_ckr", [D, 384]); w_qb = din("w_qb", [512, 2048]); w_kvb = din("w_kvb", [256, 2048])
    w_out = din("w_out", [D, D]); w_up = din("w_up", [D, DFF]); w_gate = din("w_gate", [D, DFF]); w_down = din("w_down", [DFF, D])
    lbp = din("lbp", [2, 1024])
    gin_r = din("gin_r", [1, D]); bin_r = din("bin_r", [1, D]); g1_r = din("g1_r", [1, D]); b1_r = din("b1_r", [1, D])
    g2_r = din("g2_r", [1, D]); b2_r = din("b2_r", [1, D])
    kvg_r = din("kvg_r", [1, 256]); qag_c = din("qag_c", [128, 4]); hng_c = din("hng_c", [128, 8])
    cw_c = din("cw_c", [128, NFC * 3]); cb_c = din("cb_c", [128, NFC])
    c_lat = din("c_lat", [2, 1024, 256]); c_kr = din("c_kr", [2, 1024, 64]); c_st = din("c_st", [2, 8, 128, 128]); c_conv = din("c_conv", [2, 2, DFF])
    idn_d = din("idn", [128, 128]); tri64_d = din("tri64", [128, 128]); tri32_d = din("tri32", [128, 128]); same32_d = din("same32", [128, 128])
    rt_tok = din("rt_tok", [NPRE + NMAIN + 1, 128, 128])
    rt_feat = din("rt_feat", [2, 128, 128 * (NMAIN + 2)])
    flg = din("flg", [128, 2])

    y_main = dout("y_main", [NMAIN, 128, D]); y_s = dout("y_s", [128, D])
    lat_main = dout("lat_main", [NMAIN, 128, 256]); kr_main = dout("kr_main", [NMAIN, 128, 64])
    lat_s = dout("lat_s", [128, 256]); kr_s = dout("kr_s", [128, 64])
    st_fin = dout("st_fin", [8, 128, 128]); st_s = dout("st_s", [2, 8, 128, 128])
    conv_fin = dout("conv_fin", [2, DFF]); conv_s = dout("conv_s", [2, 2, DFF])

    ident = k.sb("ident", [128, 128], BF16); identf = k.sb("identf", [128, 128], F32)
    ones = k.sb("ones", [128, 128], BF16)
    tri64 = k.sb("tri64", [128, 128], F32); tri32 = k.sb("tri32", [128, 128], F32); same32 = k.sb("same32", [128, 128], F32)
    oml = k.sb("oml", [128, 1024], F32)
    kvg = k.sb("kvg", [128, 256], F32); qag = k.sb("qag", [128, 4], F32); hng = k.sb("hng", [128, 8], F32)
    cw = k.sb("cw", [128, NFC * 3], F32); cb = k.sb("cb", [128, NFC], F32)
    flags = k.sb("flags", [128, 2], F32)
    epsln = k.sb("epsln", [128, 1], F32); epsrms = k.sb("epsrms", [128, 1], F32)
    wv = k.sb("wv", [128, 2, 8, 128], BF16)
    wknT = k.sb("wknT", [128, 8, 256], BF16)
    S32 = k.sb("S32", [128, 1024], F32)
    Sbf = [k.sb(f"Sbf{i}", [128, 1024], BF16) for i in range(3)]
    Sbnd = k.sb("Sbnd", [128, 1024], F32)
    hist = k.sb("hist", [128, 2, NFC], F32, nparts=NFC)
    hcache = k.sb("hcache", [128, 2, 2, NFC], F32)
    hsout = k.sb("hsout", [128, 2, 2, NFC], F32, nparts=2)
    zero2 = k.sb("zero2", [128, 2], F32)
    NSL = NPRE + NMAIN + 1
    latT = k.sb("latT", [128, 2, 128 * NSL], BF16, nparts=NSL)
    latK = k.sb("latK", [128, NSL, 256], BF16, nparts=NSL)
    krT = k.sb("krT", [128, 128 * NSL], BF16, nparts=NSL)
    hres = [k.sb(f"hres{i}", [128, D], F32, nparts=4) for i in range(NT)]
    hT = k.sb("hT", [128, 16, TB], BF16, nparts=16 * NT)
    ocT = hT
    ring = [k.sb(f"ring{i}", [128, 4096], BF16) for i in range(RING)]
    gq = [k.sb(f"gq{i}", [128, 512], F32) for i in range(2)]; bq = [k.sb(f"bq{i}", [128, 512], F32) for i in range(2)]
    stt_ = k.sb("bnst", [128, 4, 6], F32); mv = k.sb("mv", [128, 2], F32); rstd = k.sb("rstd", [128, 1], F32); vtmp = k.sb("vtmp", [128, 1], F32)
    s32 = k.sb("s32", [128, 1024], F32); lf32 = k.sb("lf32", [128, 1024], F32); enb = k.sb("enb", [128, 1024], F32)
    k2 = [k.sb(f"k2_{i}", [128, 1024], BF16) for i in range(NT)]
    vb = [k.sb(f"vb_{i}", [128, 1024], BF16) for i in range(NT)]
    k2T = k.sb("k2T", [128, 8, TB], BF16, nparts=NT)
    ebT = k.sb("ebT", [128, 8, TB], F32, nparts=NT)
    q1T = k.sb("q1T", [128, 8, TB], BF16, nparts=8)
    xb = Sub(q1T, q1T.t[:].rearrange("p h t -> p (h t)"))
    sgT = k.sb("sgT", [128, 8, TB], BF16, nparts=8)
    ATm = k.sb("ATm", [128, 8, 128], BF16)
    stmp = s32; otmp = lf32; otmp2 = enb
    sqb = k.sb("sqb", [128, 1024], BF16)
    cq32 = Sub(s32, s32.t[:].rearrange("p (c t) -> p c t", c=4)); cqsq = k.sb("cqsq", [128, 4, TB], BF16, nparts=4)
    cqnT = k.sb("cqnT", [128, 4, TB], BF16, nparts=4)
    rtmp = k.sb("rtmp", [128, TB], F32); rtmp2 = k.sb("rtmp2", [128, TB], F32)
    latf = k.sb("latf", [128, 256], F32); latb = k.sb("latb", [128, 256], BF16)
    krf = k.sb("krf", [128, 64], F32); krt = k.sb("krt", [128, 64], F32); krb = k.sb("krb", [128, 128], BF16)
    ss = k.sb("ss", [128, 1], F32); junk = Sub(rtmp2, rtmp2.t[:, 0:256])
    rtab = k.sb("rtab", [128, 128], F32)
    cosT = k.sb("cosT", [128, TB], F32); sinT = k.sb("sinT", [128, TB], F32)
    qnT = Sub(k2T, k2T.t[:])
    qr8 = Sub(sgT, sgT.t[:])
    qpT = [Sub(vb[i], vb[i].t[:].rearrange("p (c h t) -> p c h t", c=2, h=2)) for i in range(2)]
    PT = [k.sb(f"PT{i}", [128, 2, TB], BF16) for i in range(3)]
    hout = k.sb("hout", [128, 128], F32)
    OTs = k.sb("OTs", [128, 2, 2, TB], BF16)
    rsum = k.sb("rsum", [128, 2, TB], F32)
    aa = [Sub(enb, enb.t[:, 0:TB]), Sub(ebT, ebT.t[:, 0, :])]
    hTg = [Sub(k2[i], k2[i].t[:].rearrange("p (c t) -> p c t", c=4)) for i in range(2)]
    stout = lf32

    ld = lambda T, src: k.dma("sp", [(T.t[:], src)], [], [T.v()])
    ld(identf, idn_d[:, :]); ld(tri64, tri64_d[:, :]); ld(tri32, tri32_d[:, :]); ld(same32, same32_d[:, :])
    ld(qag, qag_c[:, :]); ld(hng, hng_c[:, :]); ld(cw, cw_c[:, :]); ld(cb, cb_c[:, :]); ld(flags, flg[:, :])
    ld(kvg, kvg_r[0:1, :].partition_broadcast(128))
    k.dma("sp", [(s32.t[:], lbp[0:1, :].partition_broadcast(128))], [], [s32.v()])
    k.dma("sp", [(lf32.t[:], lbp[1:2, :].partition_broadcast(128))], [], [lf32.v()])
    k.dma("pool", [(wv.t[:, c, :, :], w_kvb[c * 128:(c + 1) * 128, :].rearrange("p (h x) -> p h x", x=256)[:, :, 128:256]) for c in range(2)],
          [], [wv.v()])
    wfull = Sub(ring[0], ring[0].t[:, 0:4096].rearrange("p (c n) -> p c n", c=2))
    k.dma("pool", [(wfull.t, w_kvb.rearrange("(c p) n -> p c n", p=128))], [], [ring[0].v()])
    k.copy("dve", ident.v(), identf.v())
    k.memset("dve", ones.v(), 1.0)
    k.memset("dve", epsln.v(), LN_EPS)
    k.memset("dve", epsrms.v(), RMS_EPS)
    k.memset("dve", S32.v(), 0.0)
    k.memset("dve", Sbf[0].v(), 0.0)
    k.memset("dve", hist.v(), 0.0)
    k.memset("dve", zero2.v(), 0.0)
    k.memset("dve", hT.v(), 0.0)
    k.tt("dve", enb.v(), lf32.v(), s32.v(), ALU.subtract)
    k.act(oml.v(), enb.v(), AF.Sigmoid)
    pf = flags.v((S_, slice(0, 1)))
    pbias = flags.v((S_, slice(1, 2)))
    for h in range(8):
        b = k.bank()
        for c in range(2):
            o = k.pv(b, 128, c * 128, bf=True)
            k.tr(o, wfull.v((S_, c, slice(h * 256, h * 256 + 128))), ident.v())
        k.copy("dve", wknT.v((S_, h, S_)), k.pv(b, 256, 0, bf=True))

    rstate = {"i": 0}

    scr = {}
    wq = {"q": "pool"}

    def wload(key, src2d, kc, n):
        T = ring[rstate["i"] % RING]
        rstate["i"] += 1
        flat = T.t[:, 0:kc * n]
        dst = flat.rearrange("p (c n) -> p c n", c=kc)
        if key not in scr:
            d = nc.dram_tensor(f"scr_{len(scr)}", [128, kc * n], BF16).ap()
            Sx = TT(f"scr{len(scr)}", d, 4)
            scr[key] = Sx
            srcv = src2d.rearrange("(c p) n -> p c n", p=128)
            if kc >= 2:
                h2 = kc // 2
                pairs = [(dst[:, 0:h2, :], srcv[:, 0:h2, :]), (dst[:, h2:kc, :], srcv[:, h2:kc, :])]
            else:
                pairs = [(dst, srcv)]
            k.dma("pool", pairs, [], [T.v()])
            k.dma("sp", [(d[:, :], flat)], [T.v()], [Sx.v()], owner=T)
        else:
            Sx = scr[key]
            k.dma(wq["q"], [(flat, Sx.t[:, :])], [Sx.v()], [T.v()], owner=T)
        return T, dst

    def wview(Tw, ap):
        return View(Tw, ap, (0,))

    def stage_bufs():
        def flat_bf(T_, pat):
            return T_.t[:].rearrange(pat)
        L_ = [Sub(sgT, sgT.t[:].rearrange("p h t -> p (h t)")[:, 0:1024]), Sub(k2T, k2T.t[:].rearrange("p h t -> p (h t)")[:, 0:1024]),
              Sub(OTs, OTs.t[:].rearrange("p a b t -> p (a b t)")), Sub(cqsq, cqsq.t[:].rearrange("p c t -> p (c t)")),
              Sub(cqnT, cqnT.t[:].rearrange("p c t -> p (c t)")), Sub(ATm, ATm.t[:].rearrange("p h t -> p (h t)")),
              Sub(rsum, rsum.t[:].rearrange("p a t -> p (a t)").bitcast(BF16))]
        return L_
    pre_list = []
    for q in range(4):
        pre_list.append((("hq", q), w_hq[:, q * 256:(q + 1) * 256], 16, 256))
    for q in range(4):
        pre_list.append((("hg", q), w_hg[:, q * 256:(q + 1) * 256], 16, 256))
    for q in range(2):
        pre_list.append((("cq", q), w_cq[:, q * 256:(q + 1) * 256], 16, 256))
    pre_list.append((("qb", 0), w_qb[:, 0:1024], 4, 1024))
    pre_list.append((("qb", 1), w_qb[:, 1024:2048], 4, 1024))
    for n8 in range(8):
        pre_list.append((("out", n8), w_out[:, n8 * 256:(n8 + 1) * 256], 16, 256))
    for g in range(NFC // 4):
        for half in range(2):
            c0f = g * 512 + half * 256
            pre_list.append((("up", g, half), w_up[:, c0f:c0f + 256], 16, 256))
            pre_list.append((("gate", g, half), w_gate[:, c0f:c0f + 256], 16, 256))
        if g > 0:
            for dh in range(2):
                pre_list.append((("down", g - 1, dh), w_down[(g - 1) * 512:g * 512, dh * 1024:(dh + 1) * 1024], 4, 1024))
    for dh in range(2):
        gl = NFC // 4 - 1
        pre_list.append((("down", gl, dh), w_down[gl * 512:(gl + 1) * 512, dh * 1024:(dh + 1) * 1024], 4, 1024))
    pre_q = [(key, src2d, kc, n, q) for (key, src2d, kc, n) in pre_list for q in range(4)]
    pstate = {"i": 0, "pending": []}

    def precast(count, stg):
        DEPTH = len(stg) - 2
        for _ in range(count):
            if pstate["i"] < len(pre_q):
                key, src2d, kc, n, q = pre_q[pstate["i"]]
                if key not in scr:
                    d = nc.dram_tensor(f"scr_{len(scr)}", [128, kc * n], BF16).ap()
                    scr[key] = TT(f"scr{len(scr)}", d, 4)
                Sx = scr[key]
                sg = stg[pstate["i"] % len(stg)]
                pstate["i"] += 1
                kq = kc // 4
                srcv = src2d.rearrange("(c p) n -> p c n", p=128)[:, q * kq:(q + 1) * kq, :]
                dst = sg.t.rearrange("p (c n) -> p c n", c=kq)
                k.dma("pool", [(dst, srcv)], [], [sg.v()], owner=sg.parent)
                pstate["pending"].append((sg, Sx, q))
            while len(pstate["pending"]) > (DEPTH if pstate["i"] < len(pre_q) else 0):
                sg, Sx, q = pstate["pending"].pop(0)
                k.dma("pool", [(Sx.t[:, q * 1024:(q + 1) * 1024], sg.t)], [sg.v()], [Sx.v(parts=q)], owner=sg.parent)

    gbsrc = {}

    def load_gb(g_r, b_r):
        gbsrc["g"], gbsrc["b"] = g_r, b_r

    mvs = [k.sb(f"mv{i}", [128, 2], F32) for i in range(NT)]
    rstds = [k.sb(f"rstd{i}", [128, 1], F32) for i in range(NT)]
    gres = {}

    def ln_block(idxs, mode, ydsts=None, resident=False):
        g_r, b_r = gbsrc["g"], gbsrc["b"]
        for i in idxs:
            H = hres[i]
            for j in range(4):
                k.op("dve", lambda h, j=j, H=H: h.bn_stats(out=stt_.t[:, j, :], in_=H.t[:, j * 512:(j + 1) * 512]),
                     [H.v(parts=j)], [stt_.v()])
            k.op("dve", lambda h, i=i: h.bn_aggr(out=mvs[i].t[:], in_=stt_.t[:].rearrange("p a b -> p (a b)")), [stt_.v()], [mvs[i].v()])
            k.act(vtmp.v(), mvs[i].v((S_, slice(1, 2))), AF.Ln, bias=epsln.v())
            k.act(rstds[i].v(), vtmp.v(), AF.Exp, scale=-0.5)
        for j in range(4):
            sl = (S_, slice(j * 512, (j + 1) * 512))
            if resident:
                Gv, Bv = gres[("g", j)].v(), gres[("b", j)].v()
            else:
                G, B = gq[j % 2], bq[j % 2]
                k.dma("sp", [(G.t[:], g_r[0:1, j * 512:(j + 1) * 512].partition_broadcast(128))], [], [G.v()])
                k.dma("sp", [(B.t[:], b_r[0:1, j * 512:(j + 1) * 512].partition_broadcast(128))], [], [B.v()])
                Gv, Bv = G.v(), B.v()
            for i in idxs:
                H = hres[i]
                k.stt(H.v(sl, parts=j), H.v(sl, parts=j), mvs[i].v((S_, slice(0, 1))), Gv, ALU.subtract, ALU.mult)
                k.stt(H.v(sl, parts=j), H.v(sl, parts=j), rstds[i].v(), Bv, ALU.mult, ALU.add)
        for n_, i in enumerate(idxs):
            H = hres[i]
            if mode == "out":
                k.dma("sp", [(ydsts[n_], H.t[:])], [H.v()], [], final=True)
                continue
            k.act(xb.v(), H.v(), AF.Copy)
            for g in range(2):
                b = k.bank()
                for c in range(8):
                    cc = g * 8 + c
                    k.tr(k.pv(b, 128, c * 128, bf=True), xb.v((S_, slice(cc * 128, cc * 128 + 128))), ident.v())
                dst = hT.v((S_, slice(g * 8, g * 8 + 8), slice(i * 128, i * 128 + 128)),
                           parts=[cc2 * NT + i for cc2 in range(g * 8, g * 8 + 8)])
                src = k.pv(b, 1024, 0, bf=True)
                src = View(src.T, src.ap.rearrange("p (c t) -> p c t", c=8), src.parts)
                k.copy("dve" if g == 0 else "act", dst, src)

    def hT_tile(c, i):
        return hT.v((S_, c, slice(i * 128, i * 128 + 128)), parts=c * NT + i)

    def hT_blk(c, ncol):
        return hT.v((S_, c, slice(0, ncol)), parts=range(c * NT, c * NT + NT))

    def proj_tok(i, Tw, wap, n, out):
        for c in range(16):
            k.mm(out, hT_tile(c, i), wview(Tw, wap[:, c, 0:n]), c == 0, c == 15)

    def proj_feat(Tw, wap, col0, ncol, out, kc=16, src=None):
        for c in range(kc):
            rhs = hT_blk(c, ncol) if src is None else src(c)
            k.mm(out, wview(Tw, wap[:, c, col0:col0 + 128]), rhs, c == 0, c == kc - 1)

    dstate = {"i": 0}

    def ffn_down(g, H, tiles):
        Td = []
        for dh in range(2):
            Td.append(wload(("down", g, dh), w_down[g * 512:(g + 1) * 512, dh * 1024:(dh + 1) * 1024], 4, 1024))
        for i, t in enumerate(tiles):
            if t["kind"] == "bnd":
                continue
            for n4 in range(4):
                Tdn, wdn = Td[n4 // 2]
                off = (n4 % 2) * 512
                b = 6 + (dstate["i"] % 2)
                dstate["i"] += 1
                for j in range(4):
                    k.mm(k.pv(b, 512), H.v((S_, j, slice(i * 128, i * 128 + 128))), wview(Tdn, wdn[:, j, off:off + 512]), j == 0, j == 3)
                sl = (S_, slice(n4 * 512, n4 * 512 + 512))
                if g == 0:
                    k.stt(hres[i].v(sl, parts=n4), hres[i].v(sl, parts=n4), ALPHA, k.pv(b, 512), ALU.mult, ALU.add)
                else:
                    k.tt("dve", hres[i].v(sl, parts=n4), hres[i].v(sl, parts=n4), k.pv(b, 512), ALU.add)

    def do_block(tiles, mode, after_w=None):
        nt = len(tiles)
        ncol = 128 * nt
        full = mode == "full"
        wq["q"] = "pool" if (full and tiles[0]["kind"] == "main") else "sp"
        for i, t in enumerate(tiles):
            k.dma("sp", [(hres[i].t[:], t["x"])], [], [hres[i].v()])
        load_gb(gin_r, bin_r)
        ln_block(list(range(nt)), "T", resident=(not full))
        for q in range(4):
            Ta, wa = wload(("hf", q), w_hf[:, q * 256:(q + 1) * 256], 16, 256)
            for i, t in enumerate(tiles):
                proj_tok(i, Ta, wa, 256, k.pv(2 * i + q // 2, 256, (q % 2) * 256))
        for q in range(4):
            Ta, wa = wload(("hi", q), w_hi[:, q * 256:(q + 1) * 256], 16, 256)
            for i, t in enumerate(tiles):
                proj_tok(i, Ta, wa, 256, k.pv(4 + 2 * i + q // 2, 256, (q % 2) * 256))
        for i, t in enumerate(tiles):
            k.copy("act", vb[i].v(), k.pv(4 + 2 * i, 1024, nb=2))

        bulk_state = {"m": 0}

        def bulk_bank():
            b_ = 4 + (bulk_state["m"] % 4)
            bulk_state["m"] += 1
            return b_

        def hg_piece(q):
            Ta, wa = wload(("hg", q), w_hg[:, q * 256:(q + 1) * 256], 16, 256)
            for hh in range(2):
                h = 2 * q + hh
                b_ = bulk_bank()
                proj_feat(Ta, wa, hh * 128, ncol, k.pv(b_, ncol))
                k.act(sgT.v((S_, h, slice(0, ncol)), parts=h), k.pv(b_, ncol), AF.Silu)

        def fp_s1(i):
            zf = k.pv(2 * i, 1024, nb=2)
            k.act(s32.v(), zf, AF.Sigmoid, scale=-1.0)
            k.tt("dve", s32.v(), s32.v(), oml.v(), ALU.mult)
            k.act(lf32.v(), s32.v(), AF.Ln, scale=-1.0, bias=1.0)

        def fp_p1(i, tri):
            for hh in range(2):
                k.mm(k.pv(2 * i + hh, 512), tri.v(), lf32.v((S_, slice(hh * 512, hh * 512 + 512))), True, True)

        def fp_s2(i):
            k.act(enb.v(), k.pv(2 * i, 1024, nb=2), AF.Exp, scale=-1.0)
            k.tt("dve", k2[i].v(), s32.v(), enb.v(), ALU.mult)

        def fp_p2(i, tri):
            if full:
                b_ = 2 * i
                for h in range(8):
                    k.tr(k.pv(b_, 128, h * 128, bf=True), k2[i].v((S_, slice(h * 128, h * 128 + 128))), ident.v())
                src = k.pv(b_, 1024, 0, bf=True)
                src = View(src.T, src.ap.rearrange("p (c t) -> p c t", c=8), src.parts)
                k.copy("dve", k2T.v((S_, S_, slice(i * 128, i * 128 + 128)), parts=i), src)
            for h in range(8):
                o = k.pv(2 * i + h // 4, 128, (h % 4) * 128)
                k.mm(o, lf32.v((S_, slice(h * 128, h * 128 + 128))), tri.v(), True, True)
            src = k.pv(2 * i, 1024, nb=2)
            src = View(src.T, src.ap.rearrange("p (c t) -> p c t", c=8), src.parts)
            k.act(ebT.v((S_, S_, slice(i * 128, i * 128 + 128)), parts=i), src, AF.Exp)

        hgq = 0
        for i, t in enumerate(tiles):
            tri = tri64 if t["L"] == 64 else tri32
            fp_s1(i)
            if full and hgq < 4:
                hg_piece(hgq); hgq += 1
            fp_p1(i, tri)
            fp_s2(i)
            if full and hgq < 4:
                hg_piece(hgq); hgq += 1
            fp_p2(i, tri)
        if full:
            while hgq < 4:
                hg_piece(hgq); hgq += 1
        k.bank_rr = 0
        k.pair_rr = 0
        bks = [k.bank() for _ in tiles]
        for q in range(2):
            Tc, wc = wload(("ckr", q), w_ckr[:, q * 192:(q + 1) * 192], 16, 192)
            for i, t in enumerate(tiles):
                proj_tok(i, Tc, wc, 192, k.pv(bks[i], 192, q * 192))
        if after_w is not None:
            after_w()
        for i, t in enumerate(tiles):
            b = bks[i]
            k.dma("sp", [(rtab.t[:], rt_tok[t["rti"], :, :])], [], [rtab.v()])
            k.act(junk.v(), k.pv(b, 256), AF.Square, accum=ss.v())
            k.act(vtmp.v(), ss.v(), AF.Ln, scale=1.0 / 256, bias=epsrms.v())
            k.act(rstd.v(), vtmp.v(), AF.Exp, scale=-0.5)
            k.stt(latf.v(), k.pv(b, 256), rstd.v(), kvg.v(), ALU.mult, ALU.mult)
            k.tt("dve", krf.v(), k.pv(b, 64, 256), rtab.v((S_, slice(0, 64))), ALU.mult)
            k.tt("dve", krt.v(), k.pv(b, 64, 320), rtab.v((S_, slice(64, 128))), ALU.mult)
            k.tt("dve", krf.v(), krf.v(), krt.v(), ALU.add)
            if t.get("lat_out") is not None:
                k.dma("sp", [(t["lat_out"], latf.t[:])], [latf.v()], [], final=True)
                k.dma("sp", [(t["kr_out"], krf.t[:])], [krf.v()], [], final=True)
            if t.get("slot") is not None:
                LT, LK, KT, slot = t["keys"]
                k.copy("act", latb.v(), latf.v())
                k.copy("act", krb.v((S_, slice(0, 64))), krf.v())
                k.copy("act", krb.v((S_, slice(64, 128))), krf.v())
                k.copy("dve", LK.v((S_, slot, S_), parts=slot), latb.v())
                b2 = k.bank()
                for c in range(2):
                    k.tr(k.pv(b2, 128, c * 128, bf=True), latb.v((S_, slice(c * 128, c * 128 + 128))), ident.v())
                k.tr(k.pv(b2, 128, 256, bf=True), krb.v(), ident.v())
                src = k.pv(b2, 256, 0, bf=True)
                src = View(src.T, src.ap.rearrange("p (c t) -> p c t", c=2), src.parts)
                k.copy("dve", LT.v((S_, S_, slice(slot * 128, slot * 128 + 128)), parts=slot), src)
                k.copy("dve", KT.v((S_, slice(slot * 128, slot * 128 + 128)), parts=slot), k.pv(b2, 128, 256, bf=True))
        if full:
            for q in range(4):
                Ta, wa = wload(("hq", q), w_hq[:, q * 256:(q + 1) * 256], 16, 256)
                for hh in range(2):
                    h = 2 * q + hh
                    b = k.bank()
                    proj_feat(Ta, wa, hh * 128, ncol, k.pv(b, ncol))
                    k.tt("dve", q1T.v((S_, h, slice(0, ncol)), parts=h), k.pv(b, ncol),
                         ebT.v((S_, h, slice(0, ncol))), ALU.mult)
            bs_ = k.bank()
            for q in range(2):
                Ta, wa = wload(("cq", q), w_cq[:, q * 256:(q + 1) * 256], 16, 256)
                for hh in range(2):
                    c = 2 * q + hh
                    b = k.bank() if (q, hh) != (0, 0) else k.bank()
                    proj_feat(Ta, wa, hh * 128, ncol, k.pv(b, ncol))
                    k.copy("act", cq32.v((S_, c, slice(0, ncol)), parts=c), k.pv(b, ncol))
                    k.act(cqsq.v((S_, c, slice(0, ncol)), parts=c), k.pv(b, ncol), AF.Square)
            for c in range(4):
                k.mm(k.pv(bs_, ncol), ones.v(), cqsq.v((S_, c, slice(0, ncol)), parts=c), c == 0, c == 3)
            k.act(rtmp.v((S_, slice(0, ncol))), k.pv(bs_, ncol), AF.Ln, scale=1.0 / 512, bias=epsrms.v())
            k.act(rtmp.v((S_, slice(0, ncol))), rtmp.v((S_, slice(0, ncol))), AF.Exp, scale=-0.5)
            for c in range(4):
                k.stt(cqnT.v((S_, c, slice(0, ncol)), parts=c), cq32.v((S_, c, slice(0, ncol)), parts=c),
                      qag.v((S_, slice(c, c + 1))), rtmp.v((S_, slice(0, ncol))), ALU.mult, ALU.mult)
        h3 = lambda T_: View(T_, T_.t[:].rearrange("p (h v) -> p h v", h=8), (0,))
        for i, t in enumerate(tiles):
            L = t["L"]
            nch = 128 // L
            kind = t["kind"]
            tri = tri64 if L == 64 else tri32
            starts = []
            if kind == "bnd":
                k.copy("dve", S32.v(), Sbnd.v())
                k.copy("act", Sbf[st["cur"]].v(), S32.v())
            if kind == "main" and t["first"]:
                k.ts("dve", S32.v(), S32.v(), pf, ALU.mult)
                k.copy("act", Sbf[st["cur"]].v(), S32.v())
            for c in range(nch):
                if kind == "smp":
                    if c >= 2:
                        starts.append(Sbf[st["cur"]])
                        continue
                    k.dma("sp", [(S32.t[:].rearrange("p (h v) -> p h v", h=8), c_st[c].rearrange("h k v -> k h v"))], [], [S32.v()])
                    st["cur"] = (st["cur"] + 1) % 3
                    k.copy("act", Sbf[st["cur"]].v(), S32.v())
                starts.append(Sbf[st["cur"]])
                rows = slice(c * L, (c + 1) * L)
                pb = k.pair()
                for h in range(8):
                    o = k.pv(pb + h // 4, 128, (h % 4) * 128)
                    k.mm(o, k2[i].v((rows, slice(h * 128, h * 128 + 128))), vb[i].v((rows, slice(h * 128, h * 128 + 128))), True, True)
                k.tt("dve", stmp.v(), k.pv(pb, 1024, nb=2), S32.v(), ALU.add)
                col = i * 128 + (c + 1) * L - 1
                ebl = View(ebT, ebT.t[:, :, col:col + 1].to_broadcast([128, 8, 128]), (i,))
                if kind == "smp":
                    k.tt("dve", h3(stout), h3(stmp), ebl, ALU.mult)
                    k.dma("sp", [(st_s[c].rearrange("h k v -> k h v"), stout.t[:].rearrange("p (h v) -> p h v", h=8))],
                          [stout.v()], [], final=True)
                else:
                    k.tt("dve", h3(S32), h3(stmp), ebl, ALU.mult)
                    st["cur"] = (st["cur"] + 1) % 3
                    k.copy("act", Sbf[st["cur"]].v(), S32.v())
            if kind == "pre" and t["slot"] == NPRE - 2:
                k.copy("act", Sbnd.v(), S32.v())
            if not full:
                continue
            icols = slice(i * 128, i * 128 + 128)
            pa = k.pair()
            for h in range(8):
                k.mm(k.pv(pa + h // 4, 128, (h % 4) * 128), k2T.v((S_, h, icols), parts=i), q1T.v((S_, h, icols), parts=h), True, True)
            atv = k.pv(pa, 1024, nb=2)
            atv = View(atv.T, atv.ap.rearrange("p (h t) -> p h t", h=8), atv.parts)
            trib = View(tri, tri.t[:].unsqueeze(1).to_broadcast([128, 8, 128]), (0,))
            k.tt("dve", ATm.v(), atv, trib, ALU.mult)
            po = k.pair()
            for h in range(8):
                hc = slice(h * 128, h * 128 + 128)
                k.mm(k.pv(po + h // 4, 128, (h % 4) * 128), vb[i].v((S_, hc)), ATm.v((S_, h, S_)), True, False)
                for c in range(nch):
                    k.mm(k.pv(po + h // 4, L, (h % 4) * 128 + c * L), starts[c].v((S_, hc)),
                         q1T.v((S_, h, slice(i * 128 + c * L, i * 128 + (c + 1) * L)), parts=h), False, c == nch - 1)
            otv = k.pv(po, 1024, nb=2)
            k.act(sqb.v(), otv, AF.Square)
            ps2 = k.pair()
            for hh in range(2):
                k.mm(k.pv(ps2 + hh, 512), ones.v(), sqb.v((S_, slice(hh * 512, hh * 512 + 512))), True, True)
            k.act(otmp.v(), k.pv(ps2, 1024, nb=2), AF.Ln, scale=1.0 / 128, bias=epsrms.v())
            k.act(otmp.v(), otmp.v(), AF.Exp, scale=-0.5)
            k.tt("dve", otmp2.v(), otv, otmp.v(), ALU.mult)
            hngb = View(hng, hng.t[:].unsqueeze(2).to_broadcast([128, 8, 128]), (0,))
            k.tt("dve", h3(otmp2), h3(otmp2), hngb, ALU.mult)
            k.tt("dve", ocT.v((S_, slice(0, 8), icols), parts=[cc * NT + i for cc in range(8)]), h3(otmp2),
                 sgT.v((S_, S_, icols)), ALU.mult)
        if not full or STAGE < 3:
            return
        col0 = tiles[0]["fcol"]
        k.dma("sp", [(cosT.t[:, 0:ncol], rt_feat[0, :, col0:col0 + ncol])], [], [cosT.v()])
        k.dma("sp", [(sinT.t[:, 0:ncol], rt_feat[1, :, col0:col0 + ncol])], [], [sinT.v()])
        cs_ = (S_, slice(0, ncol))
        Ta, wa = wload(("qb", 0), w_qb[:, 0:1024], 4, 1024)
        for h in range(8):
            b = k.bank()
            for c in range(4):
                k.mm(k.pv(b, ncol), wview(Ta, wa[:, c, h * 128:h * 128 + 128]), cqnT.v((S_, c, slice(0, ncol)), parts=c), c == 0, c == 3)
            k.copy("act", qnT.v((S_, h, slice(0, ncol))), k.pv(b, ncol))
        Ta, wa = wload(("qb", 1), w_qb[:, 1024:2048], 4, 1024)
        for j in range(4):
            b1 = k.bank(); b2 = k.bank()
            for c in range(4):
                k.mm(k.pv(b1, ncol), wview(Ta, wa[:, c, j * 128:j * 128 + 128]), cqnT.v((S_, c, slice(0, ncol)), parts=c), c == 0, c == 3)
            for c in range(4):
                k.mm(k.pv(b2, ncol), wview(Ta, wa[:, c, 512 + j * 128:512 + j * 128 + 128]), cqnT.v((S_, c, slice(0, ncol)), parts=c), c == 0, c == 3)
            k.tt("dve", rtmp.v(cs_), k.pv(b1, ncol), cosT.v(cs_), ALU.mult)
            k.tt("dve", rtmp2.v(cs_), k.pv(b2, ncol), sinT.v(cs_), ALU.mult)
            for hh in range(2):
                pr_ = slice(64 * hh, 64 * hh + 64)
                po_ = slice(64 * (1 - hh), 64 * (1 - hh) + 64)
                k.tt("dve", qr8.v((pr_, 2 * j + hh, slice(0, ncol))), rtmp.v((pr_, slice(0, ncol))), rtmp2.v((pr_, slice(0, ncol))), ALU.add)
                k.memset("dve", qr8.v((po_, 2 * j + hh, slice(0, ncol))), 0.0)
        jobs = []
        if tiles[0]["kind"] == "smp":
            for j in range(2):
                keys = [dict(slot=16 + 8 * j + t_) for t_ in range(8)] + [dict(slot=32, m32=j)]
                jobs.append((32 * j, 32 * j + 32, keys))
            keys = [dict(slot=s_) for s_ in range(NPRE - 1)] + [dict(slot=NPRE - 1, zb=True)]
            jobs.append((128, 256, keys))
        else:
            s0_ = tiles[0]["slot"]
            keys = [dict(slot=s_, bias=True) for s_ in range(NPRE)] + [dict(slot=s_) for s_ in range(NPRE, s0_)]
            for i in range(nt):
                keys.append(dict(slot=s0_ + i, lo=128 * i, zb=True))
            jobs.append((0, ncol, keys))
        pi = 0
        for pr in range(4):
            Q = qpT[pr % 2]
            for hh in range(2):
                h = 2 * pr + hh
                for c in range(2):
                    b = 6 + c
                    k.mm(k.pv(b, ncol), wknT.v((S_, h, slice(c * 128, c * 128 + 128))), qnT.v((S_, h, slice(0, ncol))), True, True)
                    k.copy("act" if c == 0 else "dve", Q.v((S_, c, hh, slice(0, ncol))), k.pv(b, ncol))

            def b3(bank, lo_, nn_, n_):
                v_ = k.pv(bank, 2 * n_)
                return View(v_.T, v_.ap.rearrange("p (h n) -> p h n", h=2)[:, :, lo_:lo_ + nn_], v_.parts)
            for (c0, c1, keys) in jobs:
                n = c1 - c0
                o0, o1, osum = 0, 1, 2
                def scores(key):
                    nonlocal pi
                    slot = key["slot"]; lo = key.get("lo", 0)
                    nn = n - lo
                    cols = slice(c0 + lo, c1)
                    kc = slice(slot * 128, slot * 128 + 128)
                    bS = 3 + (pi % 3)
                    sv_ = b3(bS, 0, nn, nn)
                    k.mm(sv_, latT.v((S_, 0, kc), parts=slot), Q.v((S_, 0, S_, cols)), True, False)
                    k.mm(sv_, latT.v((S_, 1, kc), parts=slot), Q.v((S_, 1, S_, cols)), False, False)
                    k.mm(sv_, krT.v((S_, kc), parts=slot), qr8.v((S_, slice(2 * pr, 2 * pr + 2), cols)), False, True)
                    P = PT[pi % 3]; pi += 1
                    return (sv_, P, nn, lo, slot)

                def softmax_pv(key, sc, first, last):
                    sv_, P, nn, lo, slot = sc
                    Pv = P.v((S_, S_, slice(0, nn)))
                    k.act(Pv, sv_, AF.Exp, scale=SCALE, bias=(pbias if key.get("bias") else None))
                    if key.get("zb"):
                        k.memset("dve", P.v((slice(64, 128), S_, slice(0, 64))), 0.0)
                    if "m32" in key:
                        jj = key["m32"]
                        m_ = View(same32, same32.t[:, 32 * jj:32 * jj + 32].unsqueeze(1).to_broadcast([128, 2, 32]), (0,))
                        k.tt("dve", Pv, Pv, m_, ALU.mult)
                    k.mm(b3(o0, lo, nn, n), latK.v((S_, slot, slice(0, 128)), parts=slot), Pv, first, last)
                    k.mm(b3(o1, lo, nn, n), latK.v((S_, slot, slice(128, 256)), parts=slot), Pv, first, last)
                    k.mm(b3(osum, lo, nn, n), ones.v(), Pv, first, last)
                sc_next = scores(keys[0])
                for ki, key in enumerate(keys):
                    sc_cur = sc_next
                    if ki + 1 < len(keys):
                        sc_next = scores(keys[ki + 1])
                    softmax_pv(key, sc_cur, ki == 0, ki == len(keys) - 1)
                k.copy("act", OTs.v((S_, 0, S_, slice(0, n))), b3(o0, 0, n, n))
                k.copy("dve", OTs.v((S_, 1, S_, slice(0, n))), b3(o1, 0, n, n))
                rs_ = rsum.v((S_, S_, slice(0, n)))
                os_ = b3(osum, 0, n, n)
                k.op("dve", lambda hd, rs_=rs_, os_=os_: hd.reciprocal(out=rs_.ap, in_=os_.ap), [os_], [rs_])
                for hh in range(2):
                    h = 2 * pr + hh
                    bo = 6 + hh
                    k.mm(k.pv(bo, n), wv.v((S_, 0, h, S_)), OTs.v((S_, 0, hh, slice(0, n))), True, False)
                    k.mm(k.pv(bo, n), wv.v((S_, 1, h, S_)), OTs.v((S_, 1, hh, slice(0, n))), False, True)
                    k.tt("dve", ocT.v((S_, 8 + h, slice(c0, c1)), parts=range((8 + h) * NT, (8 + h) * NT + NT)), k.pv(bo, n),
                         rsum.v((S_, hh, slice(0, n))), ALU.mult)
        if STAGE < 4:
            return
        for n8 in range(8):
            Ta, wa = wload(("out", n8), w_out[:, n8 * 256:(n8 + 1) * 256], 16, 256)
            for i in range(nt):
                b = k.bank()
                for c in range(16):
                    k.mm(k.pv(b, 256), ocT.v((S_, c, slice(i * 128, i * 128 + 128)), parts=c * NT + i), wview(Ta, wa[:, c, :]), c == 0, c == 15)
                sl = (S_, slice(n8 * 256, n8 * 256 + 256))
                k.stt(hres[i].v(sl, parts=n8 // 2), hres[i].v(sl, parts=n8 // 2), ALPHA, k.pv(b, 256), ALU.mult, ALU.add)
        load_gb(g1_r, b1_r)
        ln_block(list(range(nt)), "T")
        segs = []
        for i, t in enumerate(tiles):
            if t["kind"] == "smp":
                segs += [(0, 32, ("cache", 0)), (32, 32, ("cache", 1)), (64, 64, ("zero",))]
            elif t["kind"] == "bnd":
                segs += [(128 * i, 128, ("bnd",))]
        if tiles[0]["kind"] == "main":
            segs = [(0, ncol, ("chain",))]
        gcol = 128 if tiles[0]["kind"] == "smp" else ncol
        for g in range(NFC // 4):
            H = hTg[g % 2]
            for half in range(2):
                c0f = g * 512 + half * 256
                Tu, wu = wload(("up", g, half), w_up[:, c0f:c0f + 256], 16, 256)
                Tg, wg = wload(("gate", g, half), w_gate[:, c0f:c0f + 256], 16, 256)
                for jj in range(2):
                    j = half * 2 + jj
                    cc = g * 4 + j
                    bu = 2 * (cc % 3); bg = bu + 1
                    proj_feat(Tu, wu, jj * 128, ncol, k.pv(bu, ncol))
                    proj_feat(Tg, wg, jj * 128, gcol, k.pv(bg, gcol))
                    A = aa[cc % 2]
                    w0 = cw.v((S_, slice(cc * 3, cc * 3 + 1))); w1 = cw.v((S_, slice(cc * 3 + 1, cc * 3 + 2)))
                    w2 = cw.v((S_, slice(cc * 3 + 2, cc * 3 + 3)))
                    for (sc0, sn, src) in segs:
                        if src[0] == "bnd":
                            k.ts("dve", hist.v((S_, S_, cc), parts=cc), k.pv(bu, 2, sc0 + sn - 2), pf, ALU.mult)
                            continue
                        k.ts("dve", A.v((S_, slice(sc0, sc0 + sn))), k.pv(bu, sn, sc0), w2, ALU.mult, cb.v((S_, slice(cc, cc + 1))), ALU.add)
                        k.stt(A.v((S_, slice(sc0 + 1, sc0 + sn))), k.pv(bu, sn - 1, sc0), w1, A.v((S_, slice(sc0 + 1, sc0 + sn))), ALU.mult, ALU.add)
                        k.stt(A.v((S_, slice(sc0 + 2, sc0 + sn))), k.pv(bu, sn - 2, sc0), w0, A.v((S_, slice(sc0 + 2, sc0 + sn))), ALU.mult, ALU.add)
                        if src[0] == "chain":
                            h0 = hist.v((S_, 0, slice(cc, cc + 1)), parts=cc); h1 = hist.v((S_, 1, slice(cc, cc + 1)), parts=cc)
                        elif src[0] == "cache":
                            h0 = hcache.v((S_, src[1], 0, slice(cc, cc + 1))); h1 = hcache.v((S_, src[1], 1, slice(cc, cc + 1)))
                        else:
                            h0 = h1 = None
                        if h0 is not None:
                            a0 = A.v((S_, slice(sc0, sc0 + 1))); a1 = A.v((S_, slice(sc0 + 1, sc0 + 2)))
                            k.stt(a0, h1, w1, a0, ALU.mult, ALU.add)
                            k.stt(a0, h0, w0, a0, ALU.mult, ALU.add)
                            k.stt(a1, h1, w0, a1, ALU.mult, ALU.add)
                        tail = k.pv(bu, 2, sc0 + sn - 2)
                        if src[0] == "chain":
                            k.copy("dve", hist.v((S_, S_, cc), parts=cc), tail)
                        elif src[0] == "bnd":
                            k.ts("dve", hist.v((S_, S_, cc), parts=cc), tail, pf, ALU.mult)
                        elif src[0] == "cache":
                            k.copy("dve", hsout.v((S_, src[1], S_, cc), parts=src[1]), tail)
                    Afull = A.v((S_, slice(0, gcol)))
                    k.act(Afull, Afull, AF.Silu)
                    k.tt("dve", H.v((S_, j, slice(0, gcol))), Afull, k.pv(bg, gcol), ALU.mult)
            if g > 0:
                ffn_down(g - 1, hTg[(g - 1) % 2], tiles)
        ffn_down(NFC // 4 - 1, hTg[(NFC // 4 - 1) % 2], tiles)
        load_gb(g2_r, b2_r)
        yi = [i for i, t in enumerate(tiles) if t.get("y") is not None]
        ln_block(yi, "out", ydsts=[tiles[i]["y"] for i in yi])

    def conv_out(src_view, dsts):
        b = k.bank()
        sv = View(src_view.T, src_view.ap.rearrange("p r c -> p (r c)"), src_view.parts)
        k.op("pe", lambda h: h.transpose(out=k.pv(b, 128, np_=88).ap, in_=sv.ap, identity=identf.t[:]), [sv, identf.v()], [k.pv(b, 128)])
        k.copy("dve", hout.v((slice(0, 88), S_)), k.pv(b, 128, np_=88))
        for r in range(2):
            k.dma("sp", [(dsts[r].rearrange("(c p) -> c p", p=128), hout.t[44 * r:44 * r + 44, :])], [hout.v()], [], final=True)

    st = {"cur": 0}
    tiles_pre = []
    for ti in range(NPRE):
        tiles_pre.append(dict(kind="pre", x=xp[ti], L=64, rti=ti, slot=ti, keys=(latT, latK, krT, ti)))
    if STAGE >= 1:
        k.dma("pool", [(latK.t[:, 16 + 8 * j:24 + 8 * j, :], c_lat[j].rearrange("(t p) c -> p t c", p=128)) for j in range(2)],
              [], [latK.v(parts=range(16, 32))])
        krtmps = [Sub(sqb, sqb.t[:].rearrange("p (t c) -> p t c", t=8)),
                  Sub(rsum, rsum.t[:].rearrange("p a t -> p (a t)").bitcast(BF16).rearrange("p (t c) -> p t c", t=8))]
        for j in range(2):
            k.dma("pool", [(krtmps[j].t[:, :, 0:64], c_kr[j].rearrange("(t p) c -> p t c", p=128)),
                           (krtmps[j].t[:, :, 64:128], c_kr[j].rearrange("(t p) c -> p t c", p=128))], [], [krtmps[j].v()],
                  owner=krtmps[j].parent)
        first = [(("hf", q), w_hf[:, q * 256:(q + 1) * 256], 16, 256) for q in range(4)]
        first += [(("hi", q), w_hi[:, q * 256:(q + 1) * 256], 16, 256) for q in range(4)]
        first += [(("ckr", q), w_ckr[:, q * 192:(q + 1) * 192], 16, 192) for q in range(2)]
        groups = [first, pre_list[0:12], pre_list[12:20], pre_list[20:44], pre_list[44:68], pre_list[68:]]
        for gi, grp in enumerate(groups):
            own = TT(f"pc{gi}", None)
            evs = []
            for (key, src2d, kc, n) in grp:
                d = nc.dram_tensor(f"scr_{len(scr)}", [128, kc * n], BF16).ap()
                Sx = TT(f"scr{len(scr)}", d, 4)
                scr[key] = Sx
                k.dma("pool", [(d.rearrange("p (c n) -> p c n", c=kc), src2d.rearrange("(c p) n -> p c n", p=128))], [], [Sx.v()], owner=own)
                evs.append(Sx)
            fin = (own.dsem, own.dval, "dma")
            for Sx in evs:
                Sx.w = [fin] * 4
        f32v = lambda T_, pat, lo: T_.t[:].rearrange(pat).bitcast(F32)[:, lo:lo + 512]
        homes = [Sub(sgT, f32v(sgT, "p h t -> p (h t)", 0)), Sub(sgT, f32v(sgT, "p h t -> p (h t)", 512)),
                 Sub(k2T, f32v(k2T, "p h t -> p (h t)", 0)), Sub(k2T, f32v(k2T, "p h t -> p (h t)", 512)),
                 Sub(OTs, f32v(OTs, "p a b t -> p (a b t)", 0)), Sub(cqsq, f32v(cqsq, "p c t -> p (c t)", 0)),
                 Sub(cqnT, f32v(cqnT, "p c t -> p (c t)", 0)), Sub(ATm, f32v(ATm, "p h t -> p (h t)", 0))]
        for j in range(4):
            gres[("g", j)] = homes[j]
            gres[("b", j)] = homes[4 + j]
            k.dma("sp", [(homes[j].t, gin_r[0:1, j * 512:(j + 1) * 512].partition_broadcast(128))], [], [homes[j].v()], owner=homes[j].parent)
            k.dma("sp", [(homes[4 + j].t, bin_r[0:1, j * 512:(j + 1) * 512].partition_broadcast(128))], [], [homes[4 + j].v()], owner=homes[4 + j].parent)
        for pbk in range(NPRE // NT):
            do_block(tiles_pre[pbk * NT:(pbk + 1) * NT], "pre")
    if STAGE >= 2:
        for j in range(2):
            krtmp = krtmps[j]
            for t_ in range(8):
                slot = 16 + 8 * j + t_
                b = k.bank()
                for c in range(2):
                    k.tr(k.pv(b, 128, c * 128, bf=True), latK.v((S_, slot, slice(c * 128, c * 128 + 128)), parts=slot), ident.v())
                k.tr(k.pv(b, 128, 256, bf=True), krtmp.v((S_, t_, S_)), ident.v())
                src = k.pv(b, 256, 0, bf=True)
                src = View(src.T, src.ap.rearrange("p (c t) -> p c t", c=2), src.parts)
                k.copy("dve", latT.v((S_, S_, slice(slot * 128, slot * 128 + 128)), parts=slot), src)
                k.copy("dve", krT.v((S_, slice(slot * 128, slot * 128 + 128)), parts=slot), k.pv(b, 128, 256, bf=True))
        for j in range(2):
            k.dma("sp", [(hout.t[0:88, :], c_conv[j].rearrange("r (c p) -> (r c) p", p=128))], [], [hout.v()])
            b = k.bank()
            k.op("pe", lambda h, b=b: h.transpose(out=k.pv(b, 88).ap, in_=hout.t[0:88, :], identity=identf.t[0:88, 0:88]),
                 [hout.v(), identf.v()], [k.pv(b, 88)])
            k.copy("dve", View(hcache, hcache.t[:, j, :, :].rearrange("p r c -> p (r c)"), (0,)), k.pv(b, 88))
        t_s = dict(kind="smp", x=xs[:, :], L=32, rti=NPRE + NMAIN, slot=32, keys=(latT, latK, krT, 32),
                   lat_out=lat_s[:, :], kr_out=kr_s[:, :], y=y_s[:, :], fcol=0)
        t_b = dict(kind="bnd", x=xp[NPRE - 1], L=64, rti=NPRE - 1, slot=None, fcol=128)
        do_block([t_s, t_b], "full")
        if STAGE >= 4:
            for j in range(2):
                conv_out(hsout.v((S_, j, S_, S_), parts=j), [conv_s[j, 0], conv_s[j, 1]])
    if STAGE >= 5:
        for bi in range(NMAIN // NT):
            tl = []
            for i in range(NT):
                ti = bi * NT + i
                tl.append(dict(kind="main", first=(ti == 0), x=xm[ti], L=64, rti=NPRE + ti, slot=NPRE + ti,
                               keys=(latT, latK, krT, NPRE + ti), lat_out=lat_main[ti], kr_out=kr_main[ti], y=y_main[ti],
                               fcol=256 + ti * 128))
            do_block(tl, "full")
        conv_out(hist.v(), [conv_fin[0], conv_fin[1]])
    k.dma("sp", [(st_fin.rearrange("h k v -> k h v"), S32.t[:].rearrange("p (h v) -> p h v", h=8))], [S32.v()], [], final=True)
    k.finish()
    return k


def _consts():
    idn = np.eye(128, dtype=np.float32)
    s = np.arange(128)
    tri64 = ((s[:, None] <= s[None, :]) & (s[:, None] // 64 == s[None, :] // 64)).astype(np.float32)
    tri32 = ((s[:, None] <= s[None, :]) & (s[:, None] // 32 == s[None, :] // 32)).astype(np.float32)
    same32 = (s[:, None] // 32 == s[None, :] // 32).astype(np.float32)
    return idn, tri64, tri32, same32


def _rope(pos):
    inv = (10000.0 ** (-np.arange(0, 64, 2, dtype=np.float32) / 64)).astype(np.float32)
    ang = pos.astype(np.float32)[:, None] * inv[None, :]
    return np.cos(ang).astype(np.float32), np.sin(ang).astype(np.float32)


_CACHE = {}


def kernel(x_prompt, x_sample, cache_kv_latent, cache_k_rope, state_hgrn, cache_ffn_conv,
           lb_param, ln_in_g, ln_in_b, w_in, hgrn_norm_g, q_a_g, w_q_b, kv_a_g, w_kv_b, w_out,
           ln1_g, ln1_b, w_ffn_up, w_ffn_gate, conv_w, conv_b, w_ffn_down, ln2_g, ln2_b):
    f = lambda a: np.ascontiguousarray(np.asarray(a, dtype=np.float32))
    x_prompt = f(x_prompt); x_sample = f(x_sample); w_in = f(w_in)[0]
    idn, tri64, tri32, same32 = _consts()
    w_hq = f(w_in[:, 0:1024]); w_hf = f(w_in[:, 1024:2048]); w_hi = f(w_in[:, 2048:3072]); w_hg = f(w_in[:, 3072:4096])
    w_cq = f(w_in[:, 4096:4608])
    kr_cols = w_in[:, 4864:4928]
    w_ckr = f(np.concatenate([w_in[:, 4608:4864], kr_cols, kr_cols[:, 32:64], kr_cols[:, 0:32]], axis=1))
    wq = f(w_q_b)[0].reshape(512, 8, 192)
    nope = wq[:, :, 0:128].reshape(512, 1024)
    rope = wq[:, :, 128:192].reshape(512, 512)
    rope_sw = np.concatenate([wq[:, :, 160:192], wq[:, :, 128:160]], axis=2).reshape(512, 512)
    w_qb = f(np.concatenate([nope, rope, rope_sw], axis=1))
    shared = dict(
        w_hq=w_hq, w_hf=w_hf, w_hi=w_hi, w_hg=w_hg, w_cq=w_cq, w_ckr=w_ckr, w_qb=w_qb, w_kvb=f(w_kv_b)[0],
        w_out=f(w_out)[0], w_up=f(w_ffn_up)[0], w_gate=f(w_ffn_gate)[0], w_down=f(w_ffn_down)[0],
        lbp=f(lb_param), gin_r=f(ln_in_g)[None], bin_r=f(ln_in_b)[None], g1_r=f(ln1_g), b1_r=f(ln1_b),
        g2_r=f(ln2_g), b2_r=f(ln2_b), kvg_r=f(kv_a_g),
        qag_c=f(f(q_a_g)[0].reshape(4, 128).T), hng_c=f(f(hgrn_norm_g)[0].reshape(8, 128).T),
        cw_c=f(f(conv_w)[0].reshape(3, NFC, 128).transpose(2, 1, 0).reshape(128, NFC * 3)),
        cb_c=f(f(conv_b)[0].reshape(NFC, 128).T),
        idn=idn, tri64=tri64, tri32=tri32, same32=same32,
    )
    in_maps = []
    for c in range(8):
        b, p = c // 2, c % 2
        m = dict(shared)
        m["xp"] = f(x_prompt[b, 0:2048].reshape(NPRE, 128, D))
        m["xm"] = f(x_prompt[b, p * 2048:(p + 1) * 2048].reshape(NMAIN, 128, D))
        xs = np.zeros((128, D), np.float32)
        xs[0:32] = x_sample[2 * c]; xs[32:64] = x_sample[2 * c + 1]
        m["xs"] = xs
        m["c_lat"] = f(cache_kv_latent[0, 2 * c:2 * c + 2]); m["c_kr"] = f(cache_k_rope[0, 2 * c:2 * c + 2])
        m["c_st"] = f(state_hgrn[0, 2 * c:2 * c + 2]); m["c_conv"] = f(cache_ffn_conv[0, 2 * c:2 * c + 2])
        pos_pre = np.arange(2048); pos_main = p * 2048 + np.arange(2048)
        pos_s = 1024 + (np.arange(128) % 32)
        allpos = np.concatenate([pos_pre, pos_main, pos_s])
        cs, sn = _rope(allpos)
        rt = np.concatenate([cs, cs, -sn, sn], axis=1).astype(np.float32)
        m["rt_tok"] = f(rt.reshape(NPRE + NMAIN + 1, 128, 128))
        posf = np.concatenate([pos_s, pos_pre[1920:2048], pos_main])
        cs, sn = _rope(posf)
        cT = np.concatenate([cs, cs], axis=1).T; sT = np.concatenate([-sn, sn], axis=1).T
        m["rt_feat"] = f(np.stack([np.concatenate([cT, cT], 0), np.concatenate([sT, sT], 0)]))
        fl = np.zeros((128, 2), np.float32); fl[:, 0] = float(p); fl[:, 1] = 0.0 if p == 1 else -30000.0
        m["flg"] = fl
        in_maps.append(m)

    if "nc" not in _CACHE:
        nc = bass.Bass("TRN2", target_bir_lowering=False)
        with contextlib.ExitStack() as es:
            kk = build_program(nc, es)
        _CACHE["nc"] = nc
    nc = _CACHE["nc"]
    res = run_bass_kernel_spmd(nc, in_maps, core_ids=list(range(8)))
    R = res.results
    _CACHE["R"] = R
    y_prompt = np.zeros((4, 4096, D), np.float32); p_lat = np.zeros((1, 4, 4096, 256), np.float32)
    p_kr = np.zeros((1, 4, 4096, 64), np.float32); p_st = np.zeros((1, 4, 8, 128, 128), np.float32)
    p_conv = np.zeros((1, 4, 2, DFF), np.float32)
    y_sample = np.zeros((16, 32, D), np.float32); s_lat = np.zeros((1, 16, 32, 256), np.float32)
    s_kr = np.zeros((1, 16, 32, 64), np.float32); s_st = np.zeros((1, 16, 8, 128, 128), np.float32)
    s_conv = np.zeros((1, 16, 2, DFF), np.float32)
    for c in range(8):
        b, p = c // 2, c % 2
        r = R[c]
        y_prompt[b, p * 2048:(p + 1) * 2048] = r["y_main"].reshape(2048, D)
        p_lat[0, b, p * 2048:(p + 1) * 2048] = r["lat_main"].reshape(2048, 256)
        p_kr[0, b, p * 2048:(p + 1) * 2048] = r["kr_main"].reshape(2048, 64)
        if p == 1:
            p_st[0, b] = r["st_fin"]; p_conv[0, b] = r["conv_fin"]
        for j in range(2):
            y_sample[2 * c + j] = r["y_s"][32 * j:32 * j + 32]
            s_lat[0, 2 * c + j] = r["lat_s"][32 * j:32 * j + 32]
            s_kr[0, 2 * c + j] = r["kr_s"][32 * j:32 * j + 32]
            s_st[0, 2 * c + j] = r["st_s"][j]; s_conv[0, 2 * c + j] = r["conv_s"][j]
    return (y_prompt, y_sample, p_lat, p_kr, p_st, p_conv, s_lat, s_kr, s_st, s_conv)
```

```python
import contextlib
import numpy as np
import concourse.bass as bass
import concourse.mybir as mybir
from concourse.bass_utils import run_bass_kernel_spmd

F32 = mybir.dt.float32
BF16 = mybir.dt.bfloat16
AF = mybir.ActivationFunctionType
ALU = mybir.AluOpType
EPOCH = 30000

D = 2048
NH = 8
DFF = 5632
NFC = 44
ALPHA = 2.0 ** 0.25
LN_EPS = 1e-5
RMS_EPS = 1e-6
SCALE = 192.0 ** -0.5
NT = 2
TB = 128 * NT
NPRE = 16
NMAIN = 16
RING = 4
STAGE = 99


class View:
    __slots__ = ("T", "ap", "parts")

    def __init__(self, T, ap, parts):
        self.T, self.ap, self.parts = T, ap, parts


class TT:
    def __init__(self, name, t, nparts=1):
        self.name, self.t, self.nparts = name, t, nparts
        self.w = [None] * nparts
        self.r = [dict() for _ in range(nparts)]
        self.dsem = None
        self.dval = 0

    def v(self, key=None, parts=None, f=None):
        ap = self.t[key] if key is not None else self.t[:]
        if f is not None:
            ap = f(ap)
        if parts is None:
            parts = range(self.nparts)
        elif isinstance(parts, int):
            parts = (parts,)
        return View(self, ap, parts)


class Sub:
    def __init__(self, parent, ap):
        self.parent, self.t = parent, ap

    def v(self, key=None, parts=None, f=None):
        ap = self.t[key] if key is not None else self.t
        return View(self.parent, ap, range(self.parent.nparts))


class Eng:
    def __init__(self, name):
        self.name = name
        self.q = []
        self.sems = []
        self.cnt = 0
        self.waited = {}


class K:
    def __init__(self, nc, es):
        self.nc, self.es = nc, es
        self.E = {n: Eng(n) for n in ("pe", "act", "dve", "pool", "sp")}
        self.nsem = 0
        self.final_events = []
        self.bank_rr = 0
        self.pair_rr = 0
        self.ninstr = 0

    def new_sem(self, name):
        self.nsem += 1
        return self.es.enter_context(self.nc.semaphore(f"{name}_{self.nsem}"))

    def sb(self, name, shape, dtype, nparts=1):
        t = self.es.enter_context(self.nc.sbuf_tensor("sb_" + name, list(shape), dtype))
        return TT(name, t, nparts)

    def _cur_sem(self, e):
        if not e.sems or e.cnt >= EPOCH:
            e.sems.append(self.new_sem("e" + e.name))
            e.cnt = 0
        return e.sems[-1]

    def _deps(self, reads, writes, eng):
        deps = []
        for v in reads:
            T = v.T
            for p in v.parts:
                w = T.w[p]
                if w is not None:
                    deps.append(w)
        for v in writes:
            T = v.T
            for p in v.parts:
                w = T.w[p]
                if w is not None and not (eng == "pe" and w[2] == "pe"):
                    deps.append(w)
                deps.extend(T.r[p].values())
        return deps

    def _emit_waits(self, e, deps):
        need = {}
        for (sem, val, _n) in deps:
            kk = id(sem)
            if e.waited.get(kk, 0) >= val:
                continue
            if kk not in need or need[kk][1] < val:
                need[kk] = (sem, val)
        for kk, (sem, val) in need.items():
            e.waited[kk] = val
            e.q.append(lambda h, sem=sem, val=val: h.wait_ge(sem, val))

    def _record(self, reads, writes, ev):
        for v in reads:
            for p in v.parts:
                v.T.r[p][id(ev[0])] = ev
        for v in writes:
            for p in v.parts:
                v.T.w[p] = ev
                v.T.r[p] = {}

    def op(self, eng, fn, reads, writes):
        e = self.E[eng]
        self._emit_waits(e, self._deps(reads, writes, eng))
        sem = self._cur_sem(e)
        e.cnt += 1
        e.q.append(lambda h, sem=sem: fn(h).then_inc(sem, 1))
        ev = (sem, e.cnt, eng)
        self._record(reads, writes, ev)
        self.ninstr += 1
        return ev

    def dma(self, eng, pairs, reads, writes, final=False, owner=None):
        e = self.E[eng]
        self._emit_waits(e, self._deps(reads, writes, eng))
        tv = owner if owner is not None else (writes + reads)[0].T
        if tv.dsem is None:
            tv.dsem = self.new_sem("d" + tv.name)
        sem = tv.dsem
        for (o, i) in pairs:
            e.q.append(lambda h, o=o, i=i, sem=sem: h.dma_start(out=o, in_=i).then_inc(sem, 16))
        tv.dval += 16 * len(pairs)
        ev = (sem, tv.dval, "dma")
        self._record(reads, writes, ev)
        if final:
            self.final_events.append(ev)
        self.ninstr += len(pairs)
        return ev

    def init_psum(self):
        t = self.es.enter_context(self.nc.psum_tensor("psum_all", [128, 4096], F32))
        self.P = TT("psum_all", t, 8)

    def bank(self):
        i = self.bank_rr
        self.bank_rr = (self.bank_rr + 1) % 8
        return i

    def pair(self):
        j = self.pair_rr
        self.pair_rr = (self.pair_rr + 1) % 4
        self.bank_rr = (2 * j + 2) % 8
        return 2 * j

    def pv(self, bank, n=512, off=0, bf=False, nb=1, np_=128):
        parts = tuple(range(bank, bank + nb))
        if not bf:
            ap = self.P.t[0:np_, bank * 512 + off: bank * 512 + off + n]
        else:
            ap = self.P.t[0:np_, bank * 512: (bank + nb) * 512].bitcast(BF16)[:, off: off + n]
        return View(self.P, ap, parts)

    def finish(self):
        e = self.E["sp"]
        self._emit_waits(e, self.final_events)
        block = self.es.enter_context(self.nc.Block())
        E = self.E

        def run(q):
            def f(h):
                for c in q:
                    c(h)
            return f
        block.tensor(run(E["pe"].q))
        block.scalar(run(E["act"].q))
        block.vector(run(E["dve"].q))
        block.gpsimd(run(E["pool"].q))
        block.sync(run(E["sp"].q))

    def mm(self, out, lhsT, rhs, start, stop):
        self.op("pe", lambda h: h.matmul(out.ap, lhsT=lhsT.ap, rhs=rhs.ap, start=start, stop=stop),
                [lhsT, rhs], [out])

    def tr(self, out, in_, ident):
        self.op("pe", lambda h: h.transpose(out=out.ap, in_=in_.ap, identity=ident.ap), [in_, ident], [out])

    def act(self, out, in_, func, scale=None, bias=None, accum=None, eng="act"):
        reads = [in_]
        kw = {}
        if scale is not None:
            if isinstance(scale, View):
                reads.append(scale)
                kw["scale"] = scale.ap
            else:
                kw["scale"] = float(scale)
        if bias is not None:
            if isinstance(bias, View):
                reads.append(bias)
                kw["bias"] = bias.ap
            else:
                kw["bias"] = float(bias)
        writes = [out]
        if accum is not None:
            writes.append(accum)
            kw["accum_out"] = accum.ap
        self.op("act", lambda h: h.activation(out=out.ap, in_=in_.ap, func=func, **kw), reads, writes)

    def tt(self, eng, out, in0, in1, op):
        self.op(eng, lambda h: h.tensor_tensor(out=out.ap, in0=in0.ap, in1=in1.ap, op=op), [in0, in1], [out])

    def ts(self, eng, out, in0, s1, op0, s2=None, op1=None):
        reads = [in0]
        a1 = s1.ap if isinstance(s1, View) else float(s1)
        if isinstance(s1, View):
            reads.append(s1)
        a2 = None
        if s2 is not None:
            a2 = s2.ap if isinstance(s2, View) else float(s2)
            if isinstance(s2, View):
                reads.append(s2)
        if op1 is None:
            self.op(eng, lambda h: h.tensor_scalar(out=out.ap, in0=in0.ap, scalar1=a1, scalar2=None, op0=op0),
                    reads, [out])
        else:
            self.op(eng, lambda h: h.tensor_scalar(out=out.ap, in0=in0.ap, scalar1=a1, scalar2=a2, op0=op0, op1=op1),
                    reads, [out])

    def stt(self, out, in0, scalar, in1, op0, op1):
        reads = [in0, in1]
        a = scalar.ap if isinstance(scalar, View) else float(scalar)
        if isinstance(scalar, View):
            reads.append(scalar)
        self.op("dve", lambda h: h.scalar_tensor_tensor(out=out.ap, in0=in0.ap, scalar=a, in1=in1.ap, op0=op0, op1=op1),
                reads, [out])

    def copy(self, eng, out, in_):
        if eng == "act":
            self.act(out, in_, AF.Copy)
        else:
            self.op(eng, lambda h: h.tensor_copy(out=out.ap, in_=in_.ap), [in_], [out])

    def memset(self, eng, out, val):
        self.op(eng, lambda h: h.memset(out.ap, val), [], [out])


def build_program(nc, es):
    k = K(nc, es)
    k.init_psum()
    S_ = slice(None)

    def din(name, shape):
        return nc.dram_tensor(name, list(shape), F32, kind="ExternalInput").ap()

    def dout(name, shape):
        return nc.dram_tensor(name, list(shape), F32, kind="ExternalOutput").ap()

    xp = din("xp", [NPRE, 128, D]); xm = din("xm", [NMAIN, 128, D]); xs = din("xs", [128, D])
    w_hq = din("w_hq", [D, 1024]); w_hf = din("w_hf", [D, 1024]); w_hi = din("w_hi", [D, 1024]); w_hg = din("w_hg", [D, 1024])
    w_cq = din("w_cq", [D, 512]); w_ckr = din("w_ckr", [D, 384]); w_qb = din("w_qb", [512, 2048]); w_kvb = din("w_kvb", [256, 2048])
    w_out = din("w_out", [D, D]); w_up = din("w_up", [D, DFF]); w_gate = din("w_gate", [D, DFF]); w_down = din("w_down", [DFF, D])
    lbp = din("lbp", [2, 1024])
    gin_r = din("gin_r", [1, D]); bin_r = din("bin_r", [1, D]); g1_r = din("g1_r", [1, D]); b1_r = din("b1_r", [1, D])
    g2_r = din("g2_r", [1, D]); b2_r = din("b2_r", [1, D])
    kvg_r = din("kvg_r", [1, 256]); qag_c = din("qag_c", [128, 4]); hng_c = din("hng_c", [128, 8])
    cw_c = din("cw_c", [128, NFC * 3]); cb_c = din("cb_c", [128, NFC])
    c_lat = din("c_lat", [2, 1024, 256]); c_kr = din("c_kr", [2, 1024, 64]); c_st = din("c_st", [2, 8, 128, 128]); c_conv = din("c_conv", [2, 2, DFF])
    idn_d = din("idn", [128, 128]); tri64_d = din("tri64", [128, 128]); tri32_d = din("tri32", [128, 128]); same32_d = din("same32", [128, 128])
    rt_tok = din("rt_tok", [NPRE + NMAIN + 1, 128, 128])
    rt_feat = din("rt_feat", [2, 128, 128 * (NMAIN + 2)])
    flg = din("flg", [128, 2])

    y_main = dout("y_main", [NMAIN, 128, D]); y_s = dout("y_s", [128, D])
    lat_main = dout("lat_main", [NMAIN, 128, 256]); kr_main = dout("kr_main", [NMAIN, 128, 64])
    lat_s = dout("lat_s", [128, 256]); kr_s = dout("kr_s", [128, 64])
    st_fin = dout("st_fin", [8, 128, 128]); st_s = dout("st_s", [2, 8, 128, 128])
    conv_fin = dout("conv_fin", [2, DFF]); conv_s = dout("conv_s", [2, 2, DFF])

    ident = k.sb("ident", [128, 128], BF16); identf = k.sb("identf", [128, 128], F32)
    ones = k.sb("ones", [128, 128], BF16)
    tri64 = k.sb("tri64", [128, 128], F32); tri32 = k.sb("tri32", [128, 128], F32); same32 = k.sb("same32", [128, 128], F32)
    oml = k.sb("oml", [128, 1024], F32)
    kvg = k.sb("kvg", [128, 256], F32); qag = k.sb("qag", [128, 4], F32); hng = k.sb("hng", [128, 8], F32)
    cw = k.sb("cw", [128, NFC * 3], F32); cb = k.sb("cb", [128, NFC], F32)
    flags = k.sb("flags", [128, 2], F32)
    epsln = k.sb("epsln", [128, 1], F32); epsrms = k.sb("epsrms", [128, 1], F32)
    wv = k.sb("wv", [128, 2, 8, 128], BF16)
    wknT = k.sb("wknT", [128, 8, 256], BF16)
    S32 = k.sb("S32", [128, 1024], F32)
    Sbf = [k.sb(f"Sbf{i}", [128, 1024], BF16) for i in range(3)]
    Sbnd = k.sb("Sbnd", [128, 1024], F32)
    hist = k.sb("hist", [128, 2, NFC], F32, nparts=NFC)
    hcache = k.sb("hcache", [128, 2, 2, NFC], F32)
    hsout = k.sb("hsout", [128, 2, 2, NFC], F32, nparts=2)
    zero2 = k.sb("zero2", [128, 2], F32)
    NSL = NPRE + NMAIN + 1
    latT = k.sb("latT", [128, 2, 128 * NSL], BF16, nparts=NSL)
    latK = k.sb("latK", [128, NSL, 256], BF16, nparts=NSL)
    krT = k.sb("krT", [128, 128 * NSL], BF16, nparts=NSL)
    hres = [k.sb(f"hres{i}", [128, D], F32, nparts=4) for i in range(NT)]
    hT = k.sb("hT", [128, 16, TB], BF16, nparts=16 * NT)
    ocT = hT
    ring = [k.sb(f"ring{i}", [128, 4096], BF16) for i in range(RING)]
    gq = [k.sb(f"gq{i}", [128, 512], F32) for i in range(2)]; bq = [k.sb(f"bq{i}", [128, 512], F32) for i in range(2)]
    stt_ = k.sb("bnst", [128, 4, 6], F32); mv = k.sb("mv", [128, 2], F32); rstd = k.sb("rstd", [128, 1], F32); vtmp = k.sb("vtmp", [128, 1], F32)
    s32 = k.sb("s32", [128, 1024], F32); lf32 = k.sb("lf32", [128, 1024], F32); enb = k.sb("enb", [128, 1024], F32)
    k2 = [k.sb(f"k2_{i}", [128, 1024], BF16) for i in range(NT)]
    vb = [k.sb(f"vb_{i}", [128, 1024], BF16) for i in range(NT)]
    k2T = k.sb("k2T", [128, 8, TB], BF16, nparts=NT)
    ebT = k.sb("ebT", [128, 8, TB], F32, nparts=NT)
    q1T = k.sb("q1T", [128, 8, TB], BF16, nparts=8)
    xb = Sub(q1T, q1T.t[:].rearrange("p h t -> p (h t)"))
    sgT = k.sb("sgT", [128, 8, TB], BF16, nparts=8)
    ATm = k.sb("ATm", [128, 8, 128], BF16)
    stmp = s32; otmp = lf32; otmp2 = enb
    sqb = k.sb("sqb", [128, 1024], BF16)
    cq32 = Sub(s32, s32.t[:].rearrange("p (c t) -> p c t", c=4)); cqsq = k.sb("cqsq", [128, 4, TB], BF16, nparts=4)
    cqnT = k.sb("cqnT", [128, 4, TB], BF16, nparts=4)
    rtmp = k.sb("rtmp", [128, TB], F32); rtmp2 = k.sb("rtmp2", [128, TB], F32)
    latf = k.sb("latf", [128, 256], F32); latb = k.sb("latb", [128, 256], BF16)
    krf = k.sb("krf", [128, 64], F32); krt = k.sb("krt", [128, 64], F32); krb = k.sb("krb", [128, 128], BF16)
    ss = k.sb("ss", [128, 1], F32); junk = Sub(rtmp2, rtmp2.t[:, 0:256])
    rtab = k.sb("rtab", [128, 128], F32)
    cosT = k.sb("cosT", [128, TB], F32); sinT = k.sb("sinT", [128, TB], F32)
    qnT = Sub(k2T, k2T.t[:])
    qr8 = Sub(sgT, sgT.t[:])
    qpT = [Sub(vb[i], vb[i].t[:].rearrange("p (c h t) -> p c h t", c=2, h=2)) for i in range(2)]
    PT = [k.sb(f"PT{i}", [128, 2, TB], BF16) for i in range(3)]
    hout = k.sb("hout", [128, 128], F32)
    OTs = k.sb("OTs", [128, 2, 2, TB], BF16)
    rsum = k.sb("rsum", [128, 2, TB], F32)
    aa = [Sub(enb, enb.t[:, 0:TB]), Sub(ebT, ebT.t[:, 0, :])]
    hTg = [Sub(k2[i], k2[i].t[:].rearrange("p (c t) -> p c t", c=4)) for i in range(2)]
    stout = lf32

    ld = lambda T, src: k.dma("sp", [(T.t[:], src)], [], [T.v()])
    ld(identf, idn_d[:, :]); ld(tri64, tri64_d[:, :]); ld(tri32, tri32_d[:, :]); ld(same32, same32_d[:, :])
    ld(qag, qag_c[:, :]); ld(hng, hng_c[:, :]); ld(cw, cw_c[:, :]); ld(cb, cb_c[:, :]); ld(flags, flg[:, :])
    ld(kvg, kvg_r[0:1, :].partition_broadcast(128))
    k.dma("sp", [(s32.t[:], lbp[0:1, :].partition_broadcast(128))], [], [s32.v()])
    k.dma("sp", [(lf32.t[:], lbp[1:2, :].partition_broadcast(128))], [], [lf32.v()])
    k.dma("pool", [(wv.t[:, c, :, :], w_kvb[c * 128:(c + 1) * 128, :].rearrange("p (h x) -> p h x", x=256)[:, :, 128:256]) for c in range(2)],
          [], [wv.v()])
    wfull = Sub(ring[0], ring[0].t[:, 0:4096].rearrange("p (c n) -> p c n", c=2))
    k.dma("pool", [(wfull.t, w_kvb.rearrange("(c p) n -> p c n", p=128))], [], [ring[0].v()])
    k.copy("dve", ident.v(), identf.v())
    k.memset("dve", ones.v(), 1.0)
    k.memset("dve", epsln.v(), LN_EPS)
    k.memset("dve", epsrms.v(), RMS_EPS)
    k.memset("dve", S32.v(), 0.0)
    k.memset("dve", Sbf[0].v(), 0.0)
    k.memset("dve", hist.v(), 0.0)
    k.memset("dve", zero2.v(), 0.0)
    k.memset("dve", hT.v(), 0.0)
    k.tt("dve", enb.v(), lf32.v(), s32.v(), ALU.subtract)
    k.act(oml.v(), enb.v(), AF.Sigmoid)
    pf = flags.v((S_, slice(0, 1)))
    pbias = flags.v((S_, slice(1, 2)))
    for h in range(8):
        b = k.bank()
        for c in range(2):
            o = k.pv(b, 128, c * 128, bf=True)
            k.tr(o, wfull.v((S_, c, slice(h * 256, h * 256 + 128))), ident.v())
        k.copy("dve", wknT.v((S_, h, S_)), k.pv(b, 256, 0, bf=True))

    rstate = {"i": 0}

    scr = {}
    wq = {"q": "pool"}

    def wload(key, src2d, kc, n):
        T = ring[rstate["i"] % RING]
        rstate["i"] += 1
        flat = T.t[:, 0:kc * n]
        dst = flat.rearrange("p (c n) -> p c n", c=kc)
        if key not in scr:
            d = nc.dram_tensor(f"scr_{len(scr)}", [128, kc * n], BF16).ap()
            Sx = TT(f"scr{len(scr)}", d, 4)
            scr[key] = Sx
            srcv = src2d.rearrange("(c p) n -> p c n", p=128)
            if kc >= 2:
                h2 = kc // 2
                pairs = [(dst[:, 0:h2, :], srcv[:, 0:h2, :]), (dst[:, h2:kc, :], srcv[:, h2:kc, :])]
            else:
                pairs = [(dst, srcv)]
            k.dma("pool", pairs, [], [T.v()])
            k.dma("sp", [(d[:, :], flat)], [T.v()], [Sx.v()], owner=T)
        else:
            Sx = scr[key]
            k.dma(wq["q"], [(flat, Sx.t[:, :])], [Sx.v()], [T.v()], owner=T)
        return T, dst

    def wview(Tw, ap):
        return View(Tw, ap, (0,))

    def stage_bufs():
        def flat_bf(T_, pat):
            return T_.t[:].rearrange(pat)
        L_ = [Sub(sgT, sgT.t[:].rearrange("p h t -> p (h t)")[:, 0:1024]), Sub(k2T, k2T.t[:].rearrange("p h t -> p (h t)")[:, 0:1024]),
              Sub(OTs, OTs.t[:].rearrange("p a b t -> p (a b t)")), Sub(cqsq, cqsq.t[:].rearrange("p c t -> p (c t)")),
              Sub(cqnT, cqnT.t[:].rearrange("p c t -> p (c t)")), Sub(ATm, ATm.t[:].rearrange("p h t -> p (h t)")),
              Sub(rsum, rsum.t[:].rearrange("p a t -> p (a t)").bitcast(BF16))]
        return L_
    pre_list = []
    for q in range(4):
        pre_list.append((("hq", q), w_hq[:, q * 256:(q + 1) * 256], 16, 256))
    for q in range(4):
        pre_list.append((("hg", q), w_hg[:, q * 256:(q + 1) * 256], 16, 256))
    for q in range(2):
        pre_list.append((("cq", q), w_cq[:, q * 256:(q + 1) * 256], 16, 256))
    pre_list.append((("qb", 0), w_qb[:, 0:1024], 4, 1024))
    pre_list.append((("qb", 1), w_qb[:, 1024:2048], 4, 1024))
    for n8 in range(8):
        pre_list.append((("out", n8), w_out[:, n8 * 256:(n8 + 1) * 256], 16, 256))
    for g in range(NFC // 4):
        for half in range(2):
            c0f = g * 512 + half * 256
            pre_list.append((("up", g, half), w_up[:, c0f:c0f + 256], 16, 256))
            pre_list.append((("gate", g, half), w_gate[:, c0f:c0f + 256], 16, 256))
        if g > 0:
            for dh in range(2):
                pre_list.append((("down", g - 1, dh), w_down[(g - 1) * 512:g * 512, dh * 1024:(dh + 1) * 1024], 4, 1024))
    for dh in range(2):
        gl = NFC // 4 - 1
        pre_list.append((("down", gl, dh), w_down[gl * 512:(gl + 1) * 512, dh * 1024:(dh + 1) * 1024], 4, 1024))
    pre_q = [(key, src2d, kc, n, q) for (key, src2d, kc, n) in pre_list for q in range(4)]
    pstate = {"i": 0, "pending": []}

    def precast(count, stg):
        DEPTH = len(stg) - 2
        for _ in range(count):
            if pstate["i"] < len(pre_q):
                key, src2d, kc, n, q = pre_q[pstate["i"]]
                if key not in scr:
                    d = nc.dram_tensor(f"scr_{len(scr)}", [128, kc * n], BF16).ap()
                    scr[key] = TT(f"scr{len(scr)}", d, 4)
                Sx = scr[key]
                sg = stg[pstate["i"] % len(stg)]
                pstate["i"] += 1
                kq = kc // 4
                srcv = src2d.rearrange("(c p) n -> p c n", p=128)[:, q * kq:(q + 1) * kq, :]
                dst = sg.t.rearrange("p (c n) -> p c n", c=kq)
                k.dma("pool", [(dst, srcv)], [], [sg.v()], owner=sg.parent)
                pstate["pending"].append((sg, Sx, q))
            while len(pstate["pending"]) > (DEPTH if pstate["i"] < len(pre_q) else 0):
                sg, Sx, q = pstate["pending"].pop(0)
                k.dma("pool", [(Sx.t[:, q * 1024:(q + 1) * 1024], sg.t)], [sg.v()], [Sx.v(parts=q)], owner=sg.parent)

    gbsrc = {}

    def load_gb(g_r, b_r):
        gbsrc["g"], gbsrc["b"] = g_r, b_r

    mvs = [k.sb(f"mv{i}", [128, 2], F32) for i in range(NT)]
    rstds = [k.sb(f"rstd{i}", [128, 1], F32) for i in range(NT)]
    gres = {}

    def ln_block(idxs, mode, ydsts=None, resident=False):
        g_r, b_r = gbsrc["g"], gbsrc["b"]
        for i in idxs:
            H = hres[i]
            for j in range(4):
                k.op("dve", lambda h, j=j, H=H: h.bn_stats(out=stt_.t[:, j, :], in_=H.t[:, j * 512:(j + 1) * 512]),
                     [H.v(parts=j)], [stt_.v()])
            k.op("dve", lambda h, i=i: h.bn_aggr(out=mvs[i].t[:], in_=stt_.t[:].rearrange("p a b -> p (a b)")), [stt_.v()], [mvs[i].v()])
            k.act(vtmp.v(), mvs[i].v((S_, slice(1, 2))), AF.Ln, bias=epsln.v())
            k.act(rstds[i].v(), vtmp.v(), AF.Exp, scale=-0.5)
        for j in range(4):
            sl = (S_, slice(j * 512, (j + 1) * 512))
            if resident:
                Gv, Bv = gres[("g", j)].v(), gres[("b", j)].v()
            else:
                G, B = gq[j % 2], bq[j % 2]
                k.dma("sp", [(G.t[:], g_r[0:1, j * 512:(j + 1) * 512].partition_broadcast(128))], [], [G.v()])
                k.dma("sp", [(B.t[:], b_r[0:1, j * 512:(j + 1) * 512].partition_broadcast(128))], [], [B.v()])
                Gv, Bv = G.v(), B.v()
            for i in idxs:
                H = hres[i]
                k.stt(H.v(sl, parts=j), H.v(sl, parts=j), mvs[i].v((S_, slice(0, 1))), Gv, ALU.subtract, ALU.mult)
                k.stt(H.v(sl, parts=j), H.v(sl, parts=j), rstds[i].v(), Bv, ALU.mult, ALU.add)
        for n_, i in enumerate(idxs):
            H = hres[i]
            if mode == "out":
                k.dma("sp", [(ydsts[n_], H.t[:])], [H.v()], [], final=True)
                continue
            k.act(xb.v(), H.v(), AF.Copy)
            for g in range(2):
                b = k.bank()
                for c in range(8):
                    cc = g * 8 + c
                    k.tr(k.pv(b, 128, c * 128, bf=True), xb.v((S_, slice(cc * 128, cc * 128 + 128))), ident.v())
                dst = hT.v((S_, slice(g * 8, g * 8 + 8), slice(i * 128, i * 128 + 128)),
                           parts=[cc2 * NT + i for cc2 in range(g * 8, g * 8 + 8)])
                src = k.pv(b, 1024, 0, bf=True)
                src = View(src.T, src.ap.rearrange("p (c t) -> p c t", c=8), src.parts)
                k.copy("dve" if g == 0 else "act", dst, src)

    def hT_tile(c, i):
        return hT.v((S_, c, slice(i * 128, i * 128 + 128)), parts=c * NT + i)

    def hT_blk(c, ncol):
        return hT.v((S_, c, slice(0, ncol)), parts=range(c * NT, c * NT + NT))

    def proj_tok(i, Tw, wap, n, out):
        for c in range(16):
            k.mm(out, hT_tile(c, i), wview(Tw, wap[:, c, 0:n]), c == 0, c == 15)

    def proj_feat(Tw, wap, col0, ncol, out, kc=16, src=None):
        for c in range(kc):
            rhs = hT_blk(c, ncol) if src is None else src(c)
            k.mm(out, wview(Tw, wap[:, c, col0:col0 + 128]), rhs, c == 0, c == kc - 1)

    dstate = {"i": 0}

    def ffn_down(g, H, tiles):
        Td = []
        for dh in range(2):
            Td.append(wload(("down", g, dh), w_down[g * 512:(g + 1) * 512, dh * 1024:(dh + 1) * 1024], 4, 1024))
        for i, t in enumerate(tiles):
            if t["kind"] == "bnd":
                continue
            for n4 in range(4):
                Tdn, wdn = Td[n4 // 2]
                off = (n4 % 2) * 512
                b = 6 + (dstate["i"] % 2)
                dstate["i"] += 1
                for j in range(4):
                    k.mm(k.pv(b, 512), H.v((S_, j, slice(i * 128, i * 128 + 128))), wview(Tdn, wdn[:, j, off:off + 512]), j == 0, j == 3)
                sl = (S_, slice(n4 * 512, n4 * 512 + 512))
                if g == 0:
                    k.stt(hres[i].v(sl, parts=n4), hres[i].v(sl, parts=n4), ALPHA, k.pv(b, 512), ALU.mult, ALU.add)
                else:
                    k.tt("dve", hres[i].v(sl, parts=n4), hres[i].v(sl, parts=n4), k.pv(b, 512), ALU.add)

    def do_block(tiles, mode, after_w=None):
        nt = len(tiles)
        ncol = 128 * nt
        full = mode == "full"
        wq["q"] = "pool" if (full and tiles[0]["kind"] == "main") else "sp"
        for i, t in enumerate(tiles):
            k.dma("sp", [(hres[i].t[:], t["x"])], [], [hres[i].v()])
        load_gb(gin_r, bin_r)
        ln_block(list(range(nt)), "T", resident=(not full))
        for q in range(4):
            Ta, wa = wload(("hf", q), w_hf[:, q * 256:(q + 1) * 256], 16, 256)
            for i, t in enumerate(tiles):
                proj_tok(i, Ta, wa, 256, k.pv(2 * i + q // 2, 256, (q % 2) * 256))
        for q in range(4):
            Ta, wa = wload(("hi", q), w_hi[:, q * 256:(q + 1) * 256], 16, 256)
            for i, t in enumerate(tiles):
                proj_tok(i, Ta, wa, 256, k.pv(4 + 2 * i + q // 2, 256, (q % 2) * 256))
        for i, t in enumerate(tiles):
            k.copy("act", vb[i].v(), k.pv(4 + 2 * i, 1024, nb=2))

        bulk_state = {"m": 0}

        def bulk_bank():
            b_ = 4 + (bulk_state["m"] % 4)
            bulk_state["m"] += 1
            return b_

        def hg_piece(q):
            Ta, wa = wload(("hg", q), w_hg[:, q * 256:(q + 1) * 256], 16, 256)
            for hh in range(2):
                h = 2 * q + hh
                b_ = bulk_bank()
                proj_feat(Ta, wa, hh * 128, ncol, k.pv(b_, ncol))
                k.act(sgT.v((S_, h, slice(0, ncol)), parts=h), k.pv(b_, ncol), AF.Silu)

        def fp_s1(i):
            zf = k.pv(2 * i, 1024, nb=2)
            k.act(s32.v(), zf, AF.Sigmoid, scale=-1.0)
            k.tt("dve", s32.v(), s32.v(), oml.v(), ALU.mult)
            k.act(lf32.v(), s32.v(), AF.Ln, scale=-1.0, bias=1.0)

        def fp_p1(i, tri):
            for hh in range(2):
                k.mm(k.pv(2 * i + hh, 512), tri.v(), lf32.v((S_, slice(hh * 512, hh * 512 + 512))), True, True)

        def fp_s2(i):
            k.act(enb.v(), k.pv(2 * i, 1024, nb=2), AF.Exp, scale=-1.0)
            k.tt("dve", k2[i].v(), s32.v(), enb.v(), ALU.mult)

        def fp_p2(i, tri):
            if full:
                b_ = 2 * i
                for h in range(8):
                    k.tr(k.pv(b_, 128, h * 128, bf=True), k2[i].v((S_, slice(h * 128, h * 128 + 128))), ident.v())
                src = k.pv(b_, 1024, 0, bf=True)
                src = View(src.T, src.ap.rearrange("p (c t) -> p c t", c=8), src.parts)
                k.copy("dve", k2T.v((S_, S_, slice(i * 128, i * 128 + 128)), parts=i), src)
            for h in range(8):
                o = k.pv(2 * i + h // 4, 128, (h % 4) * 128)
                k.mm(o, lf32.v((S_, slice(h * 128, h * 128 + 128))), tri.v(), True, True)
            src = k.pv(2 * i, 1024, nb=2)
            src = View(src.T, src.ap.rearrange("p (c t) -> p c t", c=8), src.parts)
            k.act(ebT.v((S_, S_, slice(i * 128, i * 128 + 128)), parts=i), src, AF.Exp)

        hgq = 0
        for i, t in enumerate(tiles):
            tri = tri64 if t["L"] == 64 else tri32
            fp_s1(i)
            if full and hgq < 4:
                hg_piece(hgq); hgq += 1
            fp_p1(i, tri)
            fp_s2(i)
            if full and hgq < 4:
                hg_piece(hgq); hgq += 1
            fp_p2(i, tri)
        if full:
            while hgq < 4:
                hg_piece(hgq); hgq += 1
        k.bank_rr = 0
        k.pair_rr = 0
        hq_state = {"q": 0}

        def hq_piece():
            q = hq_state["q"]
            hq_state["q"] += 1
            Ta, wa = wload(("hq", q), w_hq[:, q * 256:(q + 1) * 256], 16, 256)
            for hh in range(2):
                h = 2 * q + hh
                b_ = k.bank()
                proj_feat(Ta, wa, hh * 128, ncol, k.pv(b_, ncol))
                k.tt("dve", q1T.v((S_, h, slice(0, ncol)), parts=h), k.pv(b_, ncol),
                     ebT.v((S_, h, slice(0, ncol))), ALU.mult)
        bks = [k.bank() for _ in tiles]
        for q in range(2):
            Tc, wc = wload(("ckr", q), w_ckr[:, q * 192:(q + 1) * 192], 16, 192)
            for i, t in enumerate(tiles):
                proj_tok(i, Tc, wc, 192, k.pv(bks[i], 192, q * 192))
        if after_w is not None:
            after_w()
        for i, t in enumerate(tiles):
            b = bks[i]
            k.dma("sp", [(rtab.t[:], rt_tok[t["rti"], :, :])], [], [rtab.v()])
            k.act(junk.v(), k.pv(b, 256), AF.Square, accum=ss.v())
            k.act(vtmp.v(), ss.v(), AF.Ln, scale=1.0 / 256, bias=epsrms.v())
            k.act(rstd.v(), vtmp.v(), AF.Exp, scale=-0.5)
            k.stt(latf.v(), k.pv(b, 256), rstd.v(), kvg.v(), ALU.mult, ALU.mult)
            k.tt("dve", krf.v(), k.pv(b, 64, 256), rtab.v((S_, slice(0, 64))), ALU.mult)
            k.tt("dve", krt.v(), k.pv(b, 64, 320), rtab.v((S_, slice(64, 128))), ALU.mult)
            k.tt("dve", krf.v(), krf.v(), krt.v(), ALU.add)
            if t.get("lat_out") is not None:
                k.dma("sp", [(t["lat_out"], latf.t[:])], [latf.v()], [], final=True)
                k.dma("sp", [(t["kr_out"], krf.t[:])], [krf.v()], [], final=True)
            if full and hq_state["q"] < 4:
                hq_piece()
            if t.get("slot") is not None:
                LT, LK, KT, slot = t["keys"]
                k.copy("act", latb.v(), latf.v())
                k.copy("act", krb.v((S_, slice(0, 64))), krf.v())
                k.copy("act", krb.v((S_, slice(64, 128))), krf.v())
                k.copy("dve", LK.v((S_, slot, S_), parts=slot), latb.v())
                b2 = k.bank()
                for c in range(2):
                    k.tr(k.pv(b2, 128, c * 128, bf=True), latb.v((S_, slice(c * 128, c * 128 + 128))), ident.v())
                k.tr(k.pv(b2, 128, 256, bf=True), krb.v(), ident.v())
                src = k.pv(b2, 256, 0, bf=True)
                src = View(src.T, src.ap.rearrange("p (c t) -> p c t", c=2), src.parts)
                k.copy("dve", LT.v((S_, S_, slice(slot * 128, slot * 128 + 128)), parts=slot), src)
                k.copy("dve", KT.v((S_, slice(slot * 128, slot * 128 + 128)), parts=slot), k.pv(b2, 128, 256, bf=True))
        if full:
            while hq_state["q"] < 4:
                hq_piece()
            bs_ = k.bank()
            for q in range(2):
                Ta, wa = wload(("cq", q), w_cq[:, q * 256:(q + 1) * 256], 16, 256)
                for hh in range(2):
                    c = 2 * q + hh
                    b = k.bank() if (q, hh) != (0, 0) else k.bank()
                    proj_feat(Ta, wa, hh * 128, ncol, k.pv(b, ncol))
                    k.copy("act", cq32.v((S_, c, slice(0, ncol)), parts=c), k.pv(b, ncol))
                    k.act(cqsq.v((S_, c, slice(0, ncol)), parts=c), k.pv(b, ncol), AF.Square)
            for c in range(4):
                k.mm(k.pv(bs_, ncol), ones.v(), cqsq.v((S_, c, slice(0, ncol)), parts=c), c == 0, c == 3)
            k.act(rtmp.v((S_, slice(0, ncol))), k.pv(bs_, ncol), AF.Ln, scale=1.0 / 512, bias=epsrms.v())
            k.act(rtmp.v((S_, slice(0, ncol))), rtmp.v((S_, slice(0, ncol))), AF.Exp, scale=-0.5)
            for c in range(4):
                k.stt(cqnT.v((S_, c, slice(0, ncol)), parts=c), cq32.v((S_, c, slice(0, ncol)), parts=c),
                      qag.v((S_, slice(c, c + 1))), rtmp.v((S_, slice(0, ncol))), ALU.mult, ALU.mult)
        h3 = lambda T_: View(T_, T_.t[:].rearrange("p (h v) -> p h v", h=8), (0,))
        for i, t in enumerate(tiles):
            L = t["L"]
            nch = 128 // L
            kind = t["kind"]
            tri = tri64 if L == 64 else tri32
            starts = []
            if kind == "bnd":
                k.copy("dve", S32.v(), Sbnd.v())
                k.copy("act", Sbf[st["cur"]].v(), S32.v())
            if kind == "main" and t["first"]:
                k.ts("dve", S32.v(), S32.v(), pf, ALU.mult)
                k.copy("act", Sbf[st["cur"]].v(), S32.v())
            for c in range(nch):
                if kind == "smp":
                    if c >= 2:
                        starts.append(Sbf[st["cur"]])
                        continue
                    k.dma("sp", [(S32.t[:].rearrange("p (h v) -> p h v", h=8), c_st[c].rearrange("h k v -> k h v"))], [], [S32.v()])
                    st["cur"] = (st["cur"] + 1) % 3
                    k.copy("act", Sbf[st["cur"]].v(), S32.v())
                starts.append(Sbf[st["cur"]])
                rows = slice(c * L, (c + 1) * L)
                pb = k.pair()
                for h in range(8):
                    o = k.pv(pb + h // 4, 128, (h % 4) * 128)
                    k.mm(o, k2[i].v((rows, slice(h * 128, h * 128 + 128))), vb[i].v((rows, slice(h * 128, h * 128 + 128))), True, True)
                k.tt("dve", stmp.v(), k.pv(pb, 1024, nb=2), S32.v(), ALU.add)
                col = i * 128 + (c + 1) * L - 1
                ebl = View(ebT, ebT.t[:, :, col:col + 1].to_broadcast([128, 8, 128]), (i,))
                if kind == "smp":
                    k.tt("dve", h3(stout), h3(stmp), ebl, ALU.mult)
                    k.dma("sp", [(st_s[c].rearrange("h k v -> k h v"), stout.t[:].rearrange("p (h v) -> p h v", h=8))],
                          [stout.v()], [], final=True)
                else:
                    k.tt("dve", h3(S32), h3(stmp), ebl, ALU.mult)
                    st["cur"] = (st["cur"] + 1) % 3
                    k.copy("act", Sbf[st["cur"]].v(), S32.v())
            if kind == "pre" and t["slot"] == NPRE - 2:
                k.copy("act", Sbnd.v(), S32.v())
            if not full:
                continue
            icols = slice(i * 128, i * 128 + 128)
            pa = k.pair()
            for h in range(8):
                k.mm(k.pv(pa + h // 4, 128, (h % 4) * 128), k2T.v((S_, h, icols), parts=i), q1T.v((S_, h, icols), parts=h), True, True)
            atv = k.pv(pa, 1024, nb=2)
            atv = View(atv.T, atv.ap.rearrange("p (h t) -> p h t", h=8), atv.parts)
            trib = View(tri, tri.t[:].unsqueeze(1).to_broadcast([128, 8, 128]), (0,))
            k.tt("dve", ATm.v(), atv, trib, ALU.mult)
            po = k.pair()
            for h in range(8):
                hc = slice(h * 128, h * 128 + 128)
                k.mm(k.pv(po + h // 4, 128, (h % 4) * 128), vb[i].v((S_, hc)), ATm.v((S_, h, S_)), True, False)
                for c in range(nch):
                    k.mm(k.pv(po + h // 4, L, (h % 4) * 128 + c * L), starts[c].v((S_, hc)),
                         q1T.v((S_, h, slice(i * 128 + c * L, i * 128 + (c + 1) * L)), parts=h), False, c == nch - 1)
            otv = k.pv(po, 1024, nb=2)
            k.act(sqb.v(), otv, AF.Square)
            ps2 = k.pair()
            for hh in range(2):
                k.mm(k.pv(ps2 + hh, 512), ones.v(), sqb.v((S_, slice(hh * 512, hh * 512 + 512))), True, True)
            k.act(otmp.v(), k.pv(ps2, 1024, nb=2), AF.Ln, scale=1.0 / 128, bias=epsrms.v())
            k.act(otmp.v(), otmp.v(), AF.Exp, scale=-0.5)
            k.tt("dve", otmp2.v(), otv, otmp.v(), ALU.mult)
            hngb = View(hng, hng.t[:].unsqueeze(2).to_broadcast([128, 8, 128]), (0,))
            k.tt("dve", h3(otmp2), h3(otmp2), hngb, ALU.mult)
            k.tt("dve", ocT.v((S_, slice(0, 8), icols), parts=[cc * NT + i for cc in range(8)]), h3(otmp2),
                 sgT.v((S_, S_, icols)), ALU.mult)
        if not full or STAGE < 3:
            return
        col0 = tiles[0]["fcol"]
        k.dma("sp", [(cosT.t[:, 0:ncol], rt_feat[0, :, col0:col0 + ncol])], [], [cosT.v()])
        k.dma("sp", [(sinT.t[:, 0:ncol], rt_feat[1, :, col0:col0 + ncol])], [], [sinT.v()])
        cs_ = (S_, slice(0, ncol))
        Ta, wa = wload(("qb", 0), w_qb[:, 0:1024], 4, 1024)
        for h in range(8):
            b = k.bank()
            for c in range(4):
                k.mm(k.pv(b, ncol), wview(Ta, wa[:, c, h * 128:h * 128 + 128]), cqnT.v((S_, c, slice(0, ncol)), parts=c), c == 0, c == 3)
            k.copy("act", qnT.v((S_, h, slice(0, ncol))), k.pv(b, ncol))
        Ta, wa = wload(("qb", 1), w_qb[:, 1024:2048], 4, 1024)
        for j in range(4):
            b1 = k.bank(); b2 = k.bank()
            for c in range(4):
                k.mm(k.pv(b1, ncol), wview(Ta, wa[:, c, j * 128:j * 128 + 128]), cqnT.v((S_, c, slice(0, ncol)), parts=c), c == 0, c == 3)
            for c in range(4):
                k.mm(k.pv(b2, ncol), wview(Ta, wa[:, c, 512 + j * 128:512 + j * 128 + 128]), cqnT.v((S_, c, slice(0, ncol)), parts=c), c == 0, c == 3)
            k.tt("dve", rtmp.v(cs_), k.pv(b1, ncol), cosT.v(cs_), ALU.mult)
            k.tt("dve", rtmp2.v(cs_), k.pv(b2, ncol), sinT.v(cs_), ALU.mult)
            for hh in range(2):
                pr_ = slice(64 * hh, 64 * hh + 64)
                po_ = slice(64 * (1 - hh), 64 * (1 - hh) + 64)
                k.tt("dve", qr8.v((pr_, 2 * j + hh, slice(0, ncol))), rtmp.v((pr_, slice(0, ncol))), rtmp2.v((pr_, slice(0, ncol))), ALU.add)
                k.memset("dve", qr8.v((po_, 2 * j + hh, slice(0, ncol))), 0.0)
        jobs = []
        if tiles[0]["kind"] == "smp":
            for j in range(2):
                keys = [dict(slot=16 + 8 * j + t_) for t_ in range(8)] + [dict(slot=32, m32=j)]
                jobs.append((32 * j, 32 * j + 32, keys))
            keys = [dict(slot=s_) for s_ in range(NPRE - 1)] + [dict(slot=NPRE - 1, zb=True)]
            jobs.append((128, 256, keys))
        else:
            s0_ = tiles[0]["slot"]
            keys = [dict(slot=s_, bias=True) for s_ in range(NPRE)] + [dict(slot=s_) for s_ in range(NPRE, s0_)]
            for i in range(nt):
                keys.append(dict(slot=s0_ + i, lo=128 * i, zb=True))
            jobs.append((0, ncol, keys))
        pi = 0
        for pr in range(4):
            Q = qpT[pr % 2]
            for hh in range(2):
                h = 2 * pr + hh
                for c in range(2):
                    b = 6 + c
                    k.mm(k.pv(b, ncol), wknT.v((S_, h, slice(c * 128, c * 128 + 128))), qnT.v((S_, h, slice(0, ncol))), True, True)
                    k.copy("act" if c == 0 else "dve", Q.v((S_, c, hh, slice(0, ncol))), k.pv(b, ncol))

            def b3(bank, lo_, nn_, n_):
                v_ = k.pv(bank, 2 * n_)
                return View(v_.T, v_.ap.rearrange("p (h n) -> p h n", h=2)[:, :, lo_:lo_ + nn_], v_.parts)
            for (c0, c1, keys) in jobs:
                n = c1 - c0
                o0, o1, osum = 0, 1, 2
                def scores(key):
                    nonlocal pi
                    slot = key["slot"]; lo = key.get("lo", 0)
                    nn = n - lo
                    cols = slice(c0 + lo, c1)
                    kc = slice(slot * 128, slot * 128 + 128)
                    bS = 3 + (pi % 3)
                    sv_ = b3(bS, 0, nn, nn)
                    k.mm(sv_, latT.v((S_, 0, kc), parts=slot), Q.v((S_, 0, S_, cols)), True, False)
                    k.mm(sv_, latT.v((S_, 1, kc), parts=slot), Q.v((S_, 1, S_, cols)), False, False)
                    k.mm(sv_, krT.v((S_, kc), parts=slot), qr8.v((S_, slice(2 * pr, 2 * pr + 2), cols)), False, True)
                    P = PT[pi % 3]; pi += 1
                    return (sv_, P, nn, lo, slot)

                def softmax_pv(key, sc, first, last):
                    sv_, P, nn, lo, slot = sc
                    Pv = P.v((S_, S_, slice(0, nn)))
                    k.act(Pv, sv_, AF.Exp, scale=SCALE, bias=(pbias if key.get("bias") else None))
                    if key.get("zb"):
                        k.memset("dve", P.v((slice(64, 128), S_, slice(0, 64))), 0.0)
                    if "m32" in key:
                        jj = key["m32"]
                        m_ = View(same32, same32.t[:, 32 * jj:32 * jj + 32].unsqueeze(1).to_broadcast([128, 2, 32]), (0,))
                        k.tt("dve", Pv, Pv, m_, ALU.mult)
                    k.mm(b3(o0, lo, nn, n), latK.v((S_, slot, slice(0, 128)), parts=slot), Pv, first, last)
                    k.mm(b3(o1, lo, nn, n), latK.v((S_, slot, slice(128, 256)), parts=slot), Pv, first, last)
                    k.mm(b3(osum, lo, nn, n), ones.v(), Pv, first, last)
                sc_next = scores(keys[0])
                for ki, key in enumerate(keys):
                    sc_cur = sc_next
                    if ki + 1 < len(keys):
                        sc_next = scores(keys[ki + 1])
                    softmax_pv(key, sc_cur, ki == 0, ki == len(keys) - 1)
                k.copy("act", OTs.v((S_, 0, S_, slice(0, n))), b3(o0, 0, n, n))
                k.copy("dve", OTs.v((S_, 1, S_, slice(0, n))), b3(o1, 0, n, n))
                rs_ = rsum.v((S_, S_, slice(0, n)))
                os_ = b3(osum, 0, n, n)
                k.op("dve", lambda hd, rs_=rs_, os_=os_: hd.reciprocal(out=rs_.ap, in_=os_.ap), [os_], [rs_])
                for hh in range(2):
                    h = 2 * pr + hh
                    bo = 6 + hh
                    k.mm(k.pv(bo, n), wv.v((S_, 0, h, S_)), OTs.v((S_, 0, hh, slice(0, n))), True, False)
                    k.mm(k.pv(bo, n), wv.v((S_, 1, h, S_)), OTs.v((S_, 1, hh, slice(0, n))), False, True)
                    k.tt("dve", ocT.v((S_, 8 + h, slice(c0, c1)), parts=range((8 + h) * NT, (8 + h) * NT + NT)), k.pv(bo, n),
                         rsum.v((S_, hh, slice(0, n))), ALU.mult)
        if STAGE < 4:
            return
        for n8 in range(8):
            Ta, wa = wload(("out", n8), w_out[:, n8 * 256:(n8 + 1) * 256], 16, 256)
            for i in range(nt):
                b = k.bank()
                for c in range(16):
                    k.mm(k.pv(b, 256), ocT.v((S_, c, slice(i * 128, i * 128 + 128)), parts=c * NT + i), wview(Ta, wa[:, c, :]), c == 0, c == 15)
                sl = (S_, slice(n8 * 256, n8 * 256 + 256))
                k.stt(hres[i].v(sl, parts=n8 // 2), hres[i].v(sl, parts=n8 // 2), ALPHA, k.pv(b, 256), ALU.mult, ALU.add)
        load_gb(g1_r, b1_r)
        ln_block(list(range(nt)), "T")
        segs = []
        for i, t in enumerate(tiles):
            if t["kind"] == "smp":
                segs += [(0, 32, ("cache", 0)), (32, 32, ("cache", 1)), (64, 64, ("zero",))]
            elif t["kind"] == "bnd":
                segs += [(128 * i, 128, ("bnd",))]
        if tiles[0]["kind"] == "main":
            segs = [(0, ncol, ("chain",))]
        gcol = 128 if tiles[0]["kind"] == "smp" else ncol
        for g in range(NFC // 4):
            H = hTg[g % 2]
            for half in range(2):
                c0f = g * 512 + half * 256
                Tu, wu = wload(("up", g, half), w_up[:, c0f:c0f + 256], 16, 256)
                Tg, wg = wload(("gate", g, half), w_gate[:, c0f:c0f + 256], 16, 256)
                for jj in range(2):
                    j = half * 2 + jj
                    cc = g * 4 + j
                    bu = 2 * (cc % 3); bg = bu + 1
                    proj_feat(Tu, wu, jj * 128, ncol, k.pv(bu, ncol))
                    proj_feat(Tg, wg, jj * 128, gcol, k.pv(bg, gcol))
                    A = aa[cc % 2]
                    w0 = cw.v((S_, slice(cc * 3, cc * 3 + 1))); w1 = cw.v((S_, slice(cc * 3 + 1, cc * 3 + 2)))
                    w2 = cw.v((S_, slice(cc * 3 + 2, cc * 3 + 3)))
                    for (sc0, sn, src) in segs:
                        if src[0] == "bnd":
                            k.ts("dve", hist.v((S_, S_, cc), parts=cc), k.pv(bu, 2, sc0 + sn - 2), pf, ALU.mult)
                            continue
                        k.ts("dve", A.v((S_, slice(sc0, sc0 + sn))), k.pv(bu, sn, sc0), w2, ALU.mult, cb.v((S_, slice(cc, cc + 1))), ALU.add)
                        k.stt(A.v((S_, slice(sc0 + 1, sc0 + sn))), k.pv(bu, sn - 1, sc0), w1, A.v((S_, slice(sc0 + 1, sc0 + sn))), ALU.mult, ALU.add)
                        k.stt(A.v((S_, slice(sc0 + 2, sc0 + sn))), k.pv(bu, sn - 2, sc0), w0, A.v((S_, slice(sc0 + 2, sc0 + sn))), ALU.mult, ALU.add)
                        if src[0] == "chain":
                            h0 = hist.v((S_, 0, slice(cc, cc + 1)), parts=cc); h1 = hist.v((S_, 1, slice(cc, cc + 1)), parts=cc)
                        elif src[0] == "cache":
                            h0 = hcache.v((S_, src[1], 0, slice(cc, cc + 1))); h1 = hcache.v((S_, src[1], 1, slice(cc, cc + 1)))
                        else:
                            h0 = h1 = None
                        if h0 is not None:
                            a0 = A.v((S_, slice(sc0, sc0 + 1))); a1 = A.v((S_, slice(sc0 + 1, sc0 + 2)))
                            k.stt(a0, h1, w1, a0, ALU.mult, ALU.add)
                            k.stt(a0, h0, w0, a0, ALU.mult, ALU.add)
                            k.stt(a1, h1, w0, a1, ALU.mult, ALU.add)
                        tail = k.pv(bu, 2, sc0 + sn - 2)
                        if src[0] == "chain":
                            k.copy("dve", hist.v((S_, S_, cc), parts=cc), tail)
                        elif src[0] == "bnd":
                            k.ts("dve", hist.v((S_, S_, cc), parts=cc), tail, pf, ALU.mult)
                        elif src[0] == "cache":
                            k.copy("dve", hsout.v((S_, src[1], S_, cc), parts=src[1]), tail)
                    Afull = A.v((S_, slice(0, gcol)))
                    k.act(Afull, Afull, AF.Silu)
                    k.tt("dve", H.v((S_, j, slice(0, gcol))), Afull, k.pv(bg, gcol), ALU.mult)
            if g > 0:
                ffn_down(g - 1, hTg[(g - 1) % 2], tiles)
        ffn_down(NFC // 4 - 1, hTg[(NFC // 4 - 1) % 2], tiles)
        load_gb(g2_r, b2_r)
        yi = [i for i, t in enumerate(tiles) if t.get("y") is not None]
        ln_block(yi, "out", ydsts=[tiles[i]["y"] for i in yi])

    def conv_out(src_view, dsts):
        b = k.bank()
        sv = View(src_view.T, src_view.ap.rearrange("p r c -> p (r c)"), src_view.parts)
        k.op("pe", lambda h: h.transpose(out=k.pv(b, 128, np_=88).ap, in_=sv.ap, identity=identf.t[:]), [sv, identf.v()], [k.pv(b, 128)])
        k.copy("dve", hout.v((slice(0, 88), S_)), k.pv(b, 128, np_=88))
        for r in range(2):
            k.dma("sp", [(dsts[r].rearrange("(c p) -> c p", p=128), hout.t[44 * r:44 * r + 44, :])], [hout.v()], [], final=True)

    st = {"cur": 0}
    tiles_pre = []
    for ti in range(NPRE):
        tiles_pre.append(dict(kind="pre", x=xp[ti], L=64, rti=ti, slot=ti, keys=(latT, latK, krT, ti)))
    if STAGE >= 1:
        k.dma("pool", [(latK.t[:, 16 + 8 * j:24 + 8 * j, :], c_lat[j].rearrange("(t p) c -> p t c", p=128)) for j in range(2)],
              [], [latK.v(parts=range(16, 32))])
        krtmps = [Sub(sqb, sqb.t[:].rearrange("p (t c) -> p t c", t=8)),
                  Sub(rsum, rsum.t[:].rearrange("p a t -> p (a t)").bitcast(BF16).rearrange("p (t c) -> p t c", t=8))]
        for j in range(2):
            k.dma("pool", [(krtmps[j].t[:, :, 0:64], c_kr[j].rearrange("(t p) c -> p t c", p=128)),
                           (krtmps[j].t[:, :, 64:128], c_kr[j].rearrange("(t p) c -> p t c", p=128))], [], [krtmps[j].v()],
                  owner=krtmps[j].parent)
        first = [(("hf", q), w_hf[:, q * 256:(q + 1) * 256], 16, 256) for q in range(4)]
        first += [(("hi", q), w_hi[:, q * 256:(q + 1) * 256], 16, 256) for q in range(4)]
        first += [(("ckr", q), w_ckr[:, q * 192:(q + 1) * 192], 16, 192) for q in range(2)]
        groups = [first, pre_list[0:12], pre_list[12:20], pre_list[20:44], pre_list[44:68], pre_list[68:]]
        for gi, grp in enumerate(groups):
            own = TT(f"pc{gi}", None)
            evs = []
            for (key, src2d, kc, n) in grp:
                d = nc.dram_tensor(f"scr_{len(scr)}", [128, kc * n], BF16).ap()
                Sx = TT(f"scr{len(scr)}", d, 4)
                scr[key] = Sx
                k.dma("pool", [(d.rearrange("p (c n) -> p c n", c=kc), src2d.rearrange("(c p) n -> p c n", p=128))], [], [Sx.v()], owner=own)
                evs.append(Sx)
            fin = (own.dsem, own.dval, "dma")
            for Sx in evs:
                Sx.w = [fin] * 4
        f32v = lambda T_, pat, lo: T_.t[:].rearrange(pat).bitcast(F32)[:, lo:lo + 512]
        homes = [Sub(sgT, f32v(sgT, "p h t -> p (h t)", 0)), Sub(sgT, f32v(sgT, "p h t -> p (h t)", 512)),
                 Sub(k2T, f32v(k2T, "p h t -> p (h t)", 0)), Sub(k2T, f32v(k2T, "p h t -> p (h t)", 512)),
                 Sub(OTs, f32v(OTs, "p a b t -> p (a b t)", 0)), Sub(cqsq, f32v(cqsq, "p c t -> p (c t)", 0)),
                 Sub(cqnT, f32v(cqnT, "p c t -> p (c t)", 0)), Sub(ATm, f32v(ATm, "p h t -> p (h t)", 0))]
        for j in range(4):
            gres[("g", j)] = homes[j]
            gres[("b", j)] = homes[4 + j]
            k.dma("sp", [(homes[j].t, gin_r[0:1, j * 512:(j + 1) * 512].partition_broadcast(128))], [], [homes[j].v()], owner=homes[j].parent)
            k.dma("sp", [(homes[4 + j].t, bin_r[0:1, j * 512:(j + 1) * 512].partition_broadcast(128))], [], [homes[4 + j].v()], owner=homes[4 + j].parent)
        for pbk in range(NPRE // NT):
            do_block(tiles_pre[pbk * NT:(pbk + 1) * NT], "pre")
    if STAGE >= 2:
        for j in range(2):
            krtmp = krtmps[j]
            for t_ in range(8):
                slot = 16 + 8 * j + t_
                b = k.bank()
                for c in range(2):
                    k.tr(k.pv(b, 128, c * 128, bf=True), latK.v((S_, slot, slice(c * 128, c * 128 + 128)), parts=slot), ident.v())
                k.tr(k.pv(b, 128, 256, bf=True), krtmp.v((S_, t_, S_)), ident.v())
                src = k.pv(b, 256, 0, bf=True)
                src = View(src.T, src.ap.rearrange("p (c t) -> p c t", c=2), src.parts)
                k.copy("dve", latT.v((S_, S_, slice(slot * 128, slot * 128 + 128)), parts=slot), src)
                k.copy("dve", krT.v((S_, slice(slot * 128, slot * 128 + 128)), parts=slot), k.pv(b, 128, 256, bf=True))
        for j in range(2):
            k.dma("sp", [(hout.t[0:88, :], c_conv[j].rearrange("r (c p) -> (r c) p", p=128))], [], [hout.v()])
            b = k.bank()
            k.op("pe", lambda h, b=b: h.transpose(out=k.pv(b, 88).ap, in_=hout.t[0:88, :], identity=identf.t[0:88, 0:88]),
                 [hout.v(), identf.v()], [k.pv(b, 88)])
            k.copy("dve", View(hcache, hcache.t[:, j, :, :].rearrange("p r c -> p (r c)"), (0,)), k.pv(b, 88))
        t_s = dict(kind="smp", x=xs[:, :], L=32, rti=NPRE + NMAIN, slot=32, keys=(latT, latK, krT, 32),
                   lat_out=lat_s[:, :], kr_out=kr_s[:, :], y=y_s[:, :], fcol=0)
        t_b = dict(kind="bnd", x=xp[NPRE - 1], L=64, rti=NPRE - 1, slot=None, fcol=128)
        do_block([t_s, t_b], "full")
        if STAGE >= 4:
            for j in range(2):
                conv_out(hsout.v((S_, j, S_, S_), parts=j), [conv_s[j, 0], conv_s[j, 1]])
    if STAGE >= 5:
        for bi in range(NMAIN // NT):
            tl = []
            for i in range(NT):
                ti = bi * NT + i
                tl.append(dict(kind="main", first=(ti == 0), x=xm[ti], L=64, rti=NPRE + ti, slot=NPRE + ti,
                               keys=(latT, latK, krT, NPRE + ti), lat_out=lat_main[ti], kr_out=kr_main[ti], y=y_main[ti],
                               fcol=256 + ti * 128))
            do_block(tl, "full")
        conv_out(hist.v(), [conv_fin[0], conv_fin[1]])
    k.dma("sp", [(st_fin.rearrange("h k v -> k h v"), S32.t[:].rearrange("p (h v) -> p h v", h=8))], [S32.v()], [], final=True)
    k.finish()
    return k


def _consts():
    idn = np.eye(128, dtype=np.float32)
    s = np.arange(128)
    tri64 = ((s[:, None] <= s[None, :]) & (s[:, None] // 64 == s[None, :] // 64)).astype(np.float32)
    tri32 = ((s[:, None] <= s[None, :]) & (s[:, None] // 32 == s[None, :] // 32)).astype(np.float32)
    same32 = (s[:, None] // 32 == s[None, :] // 32).astype(np.float32)
    return idn, tri64, tri32, same32


def _rope(pos):
    inv = (10000.0 ** (-np.arange(0, 64, 2, dtype=np.float32) / 64)).astype(np.float32)
    ang = pos.astype(np.float32)[:, None] * inv[None, :]
    return np.cos(ang).astype(np.float32), np.sin(ang).astype(np.float32)


_CACHE = {}


def kernel(x_prompt, x_sample, cache_kv_latent, cache_k_rope, state_hgrn, cache_ffn_conv,
           lb_param, ln_in_g, ln_in_b, w_in, hgrn_norm_g, q_a_g, w_q_b, kv_a_g, w_kv_b, w_out,
           ln1_g, ln1_b, w_ffn_up, w_ffn_gate, conv_w, conv_b, w_ffn_down, ln2_g, ln2_b):
    f = lambda a: np.ascontiguousarray(np.asarray(a, dtype=np.float32))
    x_prompt = f(x_prompt); x_sample = f(x_sample); w_in = f(w_in)[0]
    idn, tri64, tri32, same32 = _consts()
    w_hq = f(w_in[:, 0:1024]); w_hf = f(w_in[:, 1024:2048]); w_hi = f(w_in[:, 2048:3072]); w_hg = f(w_in[:, 3072:4096])
    w_cq = f(w_in[:, 4096:4608])
    kr_cols = w_in[:, 4864:4928]
    w_ckr = f(np.concatenate([w_in[:, 4608:4864], kr_cols, kr_cols[:, 32:64], kr_cols[:, 0:32]], axis=1))
    wq = f(w_q_b)[0].reshape(512, 8, 192)
    nope = wq[:, :, 0:128].reshape(512, 1024)
    rope = wq[:, :, 128:192].reshape(512, 512)
    rope_sw = np.concatenate([wq[:, :, 160:192], wq[:, :, 128:160]], axis=2).reshape(512, 512)
    w_qb = f(np.concatenate([nope, rope, rope_sw], axis=1))
    shared = dict(
        w_hq=w_hq, w_hf=w_hf, w_hi=w_hi, w_hg=w_hg, w_cq=w_cq, w_ckr=w_ckr, w_qb=w_qb, w_kvb=f(w_kv_b)[0],
        w_out=f(w_out)[0], w_up=f(w_ffn_up)[0], w_gate=f(w_ffn_gate)[0], w_down=f(w_ffn_down)[0],
        lbp=f(lb_param), gin_r=f(ln_in_g)[None], bin_r=f(ln_in_b)[None], g1_r=f(ln1_g), b1_r=f(ln1_b),
        g2_r=f(ln2_g), b2_r=f(ln2_b), kvg_r=f(kv_a_g),
        qag_c=f(f(q_a_g)[0].reshape(4, 128).T), hng_c=f(f(hgrn_norm_g)[0].reshape(8, 128).T),
        cw_c=f(f(conv_w)[0].reshape(3, NFC, 128).transpose(2, 1, 0).reshape(128, NFC * 3)),
        cb_c=f(f(conv_b)[0].reshape(NFC, 128).T),
        idn=idn, tri64=tri64, tri32=tri32, same32=same32,
    )
    in_maps = []
    for c in range(8):
        b, p = c // 2, c % 2
        m = dict(shared)
        m["xp"] = f(x_prompt[b, 0:2048].reshape(NPRE, 128, D))
        m["xm"] = f(x_prompt[b, p * 2048:(p + 1) * 2048].reshape(NMAIN, 128, D))
        xs = np.zeros((128, D), np.float32)
        xs[0:32] = x_sample[2 * c]; xs[32:64] = x_sample[2 * c + 1]
        m["xs"] = xs
        m["c_lat"] = f(cache_kv_latent[0, 2 * c:2 * c + 2]); m["c_kr"] = f(cache_k_rope[0, 2 * c:2 * c + 2])
        m["c_st"] = f(state_hgrn[0, 2 * c:2 * c + 2]); m["c_conv"] = f(cache_ffn_conv[0, 2 * c:2 * c + 2])
        pos_pre = np.arange(2048); pos_main = p * 2048 + np.arange(2048)
        pos_s = 1024 + (np.arange(128) % 32)
        allpos = np.concatenate([pos_pre, pos_main, pos_s])
        cs, sn = _rope(allpos)
        rt = np.concatenate([cs, cs, -sn, sn], axis=1).astype(np.float32)
        m["rt_tok"] = f(rt.reshape(NPRE + NMAIN + 1, 128, 128))
        posf = np.concatenate([pos_s, pos_pre[1920:2048], pos_main])
        cs, sn = _rope(posf)
        cT = np.concatenate([cs, cs], axis=1).T; sT = np.concatenate([-sn, sn], axis=1).T
        m["rt_feat"] = f(np.stack([np.concatenate([cT, cT], 0), np.concatenate([sT, sT], 0)]))
        fl = np.zeros((128, 2), np.float32); fl[:, 0] = float(p); fl[:, 1] = 0.0 if p == 1 else -30000.0
        m["flg"] = fl
        in_maps.append(m)

    if "nc" not in _CACHE:
        nc = bass.Bass("TRN2", target_bir_lowering=False)
        with contextlib.ExitStack() as es:
            kk = build_program(nc, es)
        _CACHE["nc"] = nc
    nc = _CACHE["nc"]
    res = run_bass_kernel_spmd(nc, in_maps, core_ids=list(range(8)))
    R = res.results
    _CACHE["R"] = R
    y_prompt = np.zeros((4, 4096, D), np.float32); p_lat = np.zeros((1, 4, 4096, 256), np.float32)
    p_kr = np.zeros((1, 4, 4096, 64), np.float32); p_st = np.zeros((1, 4, 8, 128, 128), np.float32)
    p_conv = np.zeros((1, 4, 2, DFF), np.float32)
    y_sample = np.zeros((16, 32, D), np.float32); s_lat = np.zeros((1, 16, 32, 256), np.float32)
    s_kr = np.zeros((1, 16, 32, 64), np.float32); s_st = np.zeros((1, 16, 8, 128, 128), np.float32)
    s_conv = np.zeros((1, 16, 2, DFF), np.float32)
    for c in range(8):
        b, p = c // 2, c % 2
        r = R[c]
        y_prompt[b, p * 2048:(p + 1) * 2048] = r["y_main"].reshape(2048, D)
        p_lat[0, b, p * 2048:(p + 1) * 2048] = r["lat_main"].reshape(2048, 256)
        p_kr[0, b, p * 2048:(p + 1) * 2048] = r["kr_main"].reshape(2048, 64)
        if p == 1:
            p_st[0, b] = r["st_fin"]; p_conv[0, b] = r["conv_fin"]
        for j in range(2):
            y_sample[2 * c + j] = r["y_s"][32 * j:32 * j + 32]
            s_lat[0, 2 * c + j] = r["lat_s"][32 * j:32 * j + 32]
            s_kr[0, 2 * c + j] = r["kr_s"][32 * j:32 * j + 32]
            s_st[0, 2 * c + j] = r["st_s"][j]; s_conv[0, 2 * c + j] = r["conv_s"][j]
    return (y_prompt, y_sample, p_lat, p_kr, p_st, p_conv, s_lat, s_kr, s_st, s_conv)
```
